# Optimizing a Trainium2 kernel written in Bass

```python
import math
import jax, jax.numpy as jnp
from jax import lax
import numpy as np

D_MODEL = 1024
BATCH = 2
SEQ = 16384
DEPTH = 4

HEAD_DIM = 64
N_MIX_HEADS = 8
N_MEM_HEADS = 4
MIX_WIDTH = N_MIX_HEADS * HEAD_DIM
MEM_WIDTH = N_MEM_HEADS * HEAD_DIM
MERGED_WIDTH = MIX_WIDTH + MEM_WIDTH
N_MEM = 256
D_FF = 4 * D_MODEL
BLOCK_Q = 128
GROUP_Q = 1024
N_A = DEPTH // 2
N_B = DEPTH - N_A
W_IN_A = 3 * MIX_WIDTH + MEM_WIDTH
W_IN_B = MIX_WIDTH + MEM_WIDTH
W_KV_SHARED = 2 * MIX_WIDTH + N_MIX_HEADS
EPS = 1e-6
NEG_INF = -1e30
FORGET_BIAS_INIT = 2.0

kernel_name = "yoco_stickbreak_fox_hybrid"


def rms_norm(x, g):
    xf = x.astype(jnp.float32)
    y = xf * lax.rsqrt(jnp.mean(xf * xf, axis=-1, keepdims=True) + EPS) * g.astype(jnp.float32)
    return y.astype(x.dtype)


def split_heads(t, n_heads):
    b, s, _ = t.shape
    return t.reshape(b, s, n_heads, HEAD_DIM).transpose(0, 2, 1, 3)


def merge_heads(t):
    b, h, s, d = t.shape
    return t.transpose(0, 2, 1, 3).reshape(b, s, h * d)


def to_blocks(t):
    b, h, s = t.shape[:3]
    t = t.reshape((b, h, s // BLOCK_Q, BLOCK_Q) + t.shape[3:])
    return jnp.moveaxis(t, 2, 0)


def from_blocks(t):
    nb, b, h, blk, d = t.shape
    return jnp.moveaxis(t, 0, 2).reshape(b, h, nb * blk, d)


def causal_sweep(block_fn, q_side, kv_side):
    s_len = q_side[0].shape[2]
    outs = []
    for g0 in range(0, s_len, GROUP_Q):
        g1 = min(g0 + GROUP_Q, s_len)
        kv_g = tuple(t[:, :, :g1] for t in kv_side)
        q_blocks = tuple(to_blocks(t[:, :, g0:g1]) for t in q_side)
        starts = g0 + jnp.arange((g1 - g0) // BLOCK_Q) * BLOCK_Q

        def body(args, kv_g=kv_g):
            return block_fn(*args, *kv_g)

        outs.append(from_blocks(lax.map(body, q_blocks + (starts,))))
    return jnp.concatenate(outs, axis=2)


def _stick_breaking_block(q_blk, t0, k, v):
    scale = 1.0 / math.sqrt(HEAD_DIM)
    z = jnp.einsum('bhqd,bhkd->bhqk', q_blk, k).astype(jnp.float32) * scale
    key_pos = jnp.arange(k.shape[2])
    t_pos = t0 + jnp.arange(BLOCK_Q)
    mask = key_pos[None, :] < t_pos[:, None]
    log_one_minus = jnp.where(mask, jax.nn.log_sigmoid(-z), 0.0)
    between = lax.cumsum(log_one_minus, axis=3, reverse=True) - log_one_minus
    w = jnp.where(mask, jnp.exp(jax.nn.log_sigmoid(z) + between), 0.0)
    return jnp.einsum('bhqk,bhkd->bhqd', w.astype(v.dtype), v)


def stick_breaking_attention(q, k, v):
    return causal_sweep(_stick_breaking_block, (q,), (k, v))


def _forgetting_block(q_blk, c_blk, t0, k, v, c_k):
    scale = 1.0 / math.sqrt(HEAD_DIM)
    z = jnp.einsum('bhqd,bhkd->bhqk', q_blk, k).astype(jnp.float32) * scale
    z = z + c_blk[..., :, None] - c_k[:, :, None, :]
    key_pos = jnp.arange(k.shape[2])
    t_pos = t0 + jnp.arange(BLOCK_Q)
    mask = key_pos[None, :] <= t_pos[:, None]
    p = jax.nn.softmax(jnp.where(mask, z, NEG_INF), axis=-1)
    return jnp.einsum('bhqk,bhkd->bhqd', p.astype(v.dtype), v)


def forgetting_attention(q, k, v, log_f_cum):
    return causal_sweep(_forgetting_block, (q, log_f_cum), (k, v, log_f_cum))


def memory_attention(q_mem, mem_k, mem_v):
    scale = 1.0 / math.sqrt(HEAD_DIM)
    s = jnp.einsum('bshd,bmhd->bhsm', q_mem, mem_k).astype(jnp.float32) * scale
    p = jax.nn.softmax(s, axis=-1)
    o = jnp.einsum('bhsm,bmhd->bshd', p.astype(mem_v.dtype), mem_v)
    b, sl = q_mem.shape[:2]
    return o.reshape(b, sl, MEM_WIDTH)


def squared_relu_mlp(x, w1, w2):
    h = jnp.square(jax.nn.relu(x @ w1))
    return h @ w2


def setup_inputs(seed: int = 0) -> dict:
    key = jax.random.key(seed)
    ks = jax.random.split(key, 16)
    f32 = jnp.float32
    nrm = lambda k, shape, s: (jax.random.normal(k, shape, f32) * s).astype(f32)
    x = jax.random.normal(ks[0], (BATCH, SEQ, D_MODEL), f32)
    mem = jax.random.normal(ks[1], (BATCH, N_MEM, D_MODEL), f32)
    norm1_g = 1.0 + nrm(ks[2], (DEPTH, D_MODEL), 0.02)
    w_in_a = nrm(ks[3], (N_A, D_MODEL, W_IN_A), D_MODEL ** -0.5)
    w_in_b = nrm(ks[4], (N_B, D_MODEL, W_IN_B), D_MODEL ** -0.5)
    w_mem_kv = nrm(ks[5], (DEPTH, D_MODEL, 2 * MEM_WIDTH), D_MODEL ** -0.5)
    mem_norm_g = 1.0 + nrm(ks[6], (DEPTH, D_MODEL), 0.02)
    w_o = nrm(ks[7], (DEPTH, MERGED_WIDTH, D_MODEL), MERGED_WIDTH ** -0.5)
    norm2_g = 1.0 + nrm(ks[8], (DEPTH, D_MODEL), 0.02)
    w_mlp1 = nrm(ks[9], (DEPTH, D_MODEL, D_FF), D_MODEL ** -0.5)
    w_mlp2 = nrm(ks[10], (DEPTH, D_FF, D_MODEL), D_FF ** -0.5)
    kv_norm_g = 1.0 + nrm(ks[11], (D_MODEL,), 0.02)
    w_kv_shared = nrm(ks[12], (D_MODEL, W_KV_SHARED), D_MODEL ** -0.5)
    b_f = FORGET_BIAS_INIT + nrm(ks[13], (N_MIX_HEADS,), 0.1)
    final_norm_g = 1.0 + nrm(ks[14], (D_MODEL,), 0.02)
    return {"x": x, "mem": mem, "norm1_g": norm1_g, "w_in_a": w_in_a, "w_in_b": w_in_b,
            "w_mem_kv": w_mem_kv, "mem_norm_g": mem_norm_g, "w_o": w_o, "norm2_g": norm2_g,
            "w_mlp1": w_mlp1, "w_mlp2": w_mlp2, "kv_norm_g": kv_norm_g,
            "w_kv_shared": w_kv_shared, "b_f": b_f, "final_norm_g": final_norm_g}


def reference(x, mem, norm1_g, w_in_a, w_in_b, w_mem_kv, mem_norm_g, w_o, norm2_g,
              w_mlp1, w_mlp2, kv_norm_g, w_kv_shared, b_f, final_norm_g):
    b, s_len, _ = x.shape
    m_len = mem.shape[1]
    h = x
    k_sh = v_sh = log_f_cum = None
    for l in range(DEPTH):
        if l == N_A:
            hs = rms_norm(h, kv_norm_g)
            kvf = hs @ w_kv_shared
            k_sh = split_heads(kvf[..., :MIX_WIDTH], N_MIX_HEADS)
            v_sh = split_heads(kvf[..., MIX_WIDTH:2 * MIX_WIDTH], N_MIX_HEADS)
            f_logit = kvf[..., 2 * MIX_WIDTH:].astype(jnp.float32) + b_f.astype(jnp.float32)
            log_f_cum = jnp.moveaxis(lax.cumsum(jax.nn.log_sigmoid(f_logit), axis=1), 1, 2)

        hn = rms_norm(h, norm1_g[l])
        mkv = rms_norm(mem, mem_norm_g[l]) @ w_mem_kv[l]
        mem_k = mkv[..., :MEM_WIDTH].reshape(b, m_len, N_MEM_HEADS, HEAD_DIM)
        mem_v = mkv[..., MEM_WIDTH:].reshape(b, m_len, N_MEM_HEADS, HEAD_DIM)

        if l < N_A:
            proj = hn @ w_in_a[l]
            q = split_heads(proj[..., :MIX_WIDTH], N_MIX_HEADS)
            k = split_heads(proj[..., MIX_WIDTH:2 * MIX_WIDTH], N_MIX_HEADS)
            v = split_heads(proj[..., 2 * MIX_WIDTH:3 * MIX_WIDTH], N_MIX_HEADS)
            q_mem = proj[..., 3 * MIX_WIDTH:]
            mix = stick_breaking_attention(q, k, v)
        else:
            proj = hn @ w_in_b[l - N_A]
            q = split_heads(proj[..., :MIX_WIDTH], N_MIX_HEADS)
            q_mem = proj[..., MIX_WIDTH:]
            mix = forgetting_attention(q, k_sh, v_sh, log_f_cum)

        mem_out = memory_attention(q_mem.reshape(b, s_len, N_MEM_HEADS, HEAD_DIM), mem_k, mem_v)
        merged = jnp.concatenate([merge_heads(mix), mem_out], axis=-1)
        h = h + merged @ w_o[l]
        h = h + squared_relu_mlp(rms_norm(h, norm2_g[l]), w_mlp1[l], w_mlp2[l])
    return rms_norm(h, final_norm_g)
```

```python
import contextlib
import numpy as np
import ml_dtypes
import concourse.bass as bass
import concourse.mybir as mybir
from concourse.bass_utils import run_bass_kernel_spmd

F32 = mybir.dt.float32
BF16 = mybir.dt.bfloat16
I32 = mybir.dt.int32
AF = mybir.ActivationFunctionType
ALU = mybir.AluOpType

D = 1024
KD = 8
NMEM = 256
EPS = 1e-6
GROUPS = [[0, 1, 2, 3], [4, 5, 6, 7]]


class _Q:
    def __init__(self, name, sem):
        self.name = name
        self.sem = sem
        self.count = 0
        self.entries = []
        self.waited = {}


class Late:
    def __init__(self, f):
        self.f = f


class _Rec:
    def __getattr__(self, name):
        return lambda *a, **kw: (name, a, kw)


_REC = _Rec()


def _replay(eng, call):
    name, a, kw = call
    a = [x.f() if isinstance(x, Late) else x for x in a]
    kw = {k: (v.f() if isinstance(v, Late) else v) for k, v in kw.items()}
    try:
        return getattr(eng, name)(*a, **kw)
    except Exception:
        print("REPLAY FAIL", name, [getattr(x, "shape", x) for x in a], {k: getattr(v, "shape", v) for k, v in kw.items()})
        raise


class Prog:
    def __init__(self, nc, n_dma_sems=24):
        self.nc = nc
        self.q = {}
        self.keys = {}
        self.dma_sems = []
        self.n_dma_sems = n_dma_sems
        self.dma_rr = 0
        self.no_same_engine_wait = {"pe"}
        self.pre = {}
        self.cc_sems = []

    def setup(self, stack):
        nc = self.nc
        self.stack = stack
        for name in ("pe", "act", "dve", "pool", "sp"):
            sem = stack.enter_context(nc.semaphore("s_" + name))
            self.q[name] = _Q(name, sem)
        for i in range(self.n_dma_sems):
            sem = stack.enter_context(nc.semaphore("s_dma%d" % i))
            self.dma_sems.append([sem, 0])

    def _deps(self, reads, writes):
        deps = []
        for k in reads:
            st = self.keys.get(k)
            if st and st[0] is not None:
                deps.append(st[0])
        for k in writes:
            st = self.keys.get(k)
            if st:
                if st[0] is not None:
                    deps.append(st[0])
                deps.extend(st[1].values())
        return deps

    def _commit(self, reads, writes, tok):
        for k in reads:
            st = self.keys.setdefault(k, [None, {}])
            st[1][id(tok[0])] = tok
        for k in writes:
            self.keys[k] = [tok, {}]

    def _waits(self, q, deps):
        need = {}
        for sem, val in deps:
            if sem is q.sem and q.name in self.no_same_engine_wait:
                continue
            if q.waited.get(id(sem), 0) >= val:
                continue
            if need.get(id(sem), (None, 0))[1] < val:
                need[id(sem)] = (sem, val)
        out = []
        for sem, val in need.values():
            q.waited[id(sem)] = val
            out.append((sem, val))
        return out

    def op(self, qname, fn, reads=(), writes=()):
        q = self.q[qname]
        waits = self._waits(q, self._deps(reads, writes))
        q.count += 1
        tok = (q.sem, q.count)
        q.entries.append((waits, fn(_REC), (q.sem, 1)))
        self._commit(reads, writes, tok)
        return tok

    def dma(self, qname, fn, reads=(), writes=(), inc=16, fresh=False):
        q = self.q[qname]
        if fresh:
            slot = [self.stack.enter_context(self.nc.semaphore("s_cc%d" % len(self.cc_sems))), 0]
            self.cc_sems.append(slot)
        else:
            slot = self.dma_sems[self.dma_rr % self.n_dma_sems]
            self.dma_rr += 1
        deps = self._deps(reads, writes)
        if slot[1] > 0:
            deps.append((slot[0], slot[1]))
        waits = self._waits(q, deps)
        slot[1] += inc
        tok = (slot[0], slot[1])
        q.entries.append((waits, fn(_REC), (slot[0], inc)))
        self._commit(reads, writes, tok)
        return tok

    def wait_all(self, qname):
        q = self.q[qname]
        deps = []
        for st in self.keys.values():
            if st[0] is not None:
                deps.append(st[0])
            deps.extend(st[1].values())
        for slot in self.dma_sems + self.cc_sems:
            if slot[1] > 0:
                deps.append((slot[0], slot[1]))
        waits = self._waits(q, deps)
        q.entries.append((waits, None, None))

    def emit(self):
        nc = self.nc
        engs = {"pe": "tensor", "act": "scalar", "dve": "vector", "pool": "gpsimd", "sp": "sync"}
        with nc.Block() as block:
            for qname, attr in engs.items():
                q = self.q[qname]
                entries = q.entries
                q.entries = []

                def body(eng, entries=entries, qname=qname):
                    with (self.pre.pop(qname)(eng) if qname in self.pre else contextlib.nullcontext()):
                        for waits, fn, inc in entries:
                            for sem, val in waits:
                                eng.wait_ge(sem, val)
                            if fn is not None:
                                _replay(eng, fn).then_inc(inc[0], inc[1])

                getattr(block, attr)(body)


def build_program(S, DFF, NA, NB):
    L = NA + NB
    T = S // 4
    CH1 = min(512, T)
    NC1 = T // CH1
    CH3 = min(512, T)
    NC3 = T // CH3
    NQ = S // 512
    NKB = S // 128
    KF = DFF // 128
    NG = 3 * L + 2
    G1, GM, G2, GKV, GFIN = 0, L, 2 * L, 3 * L, 3 * L + 1

    nc = bass.Bass("TRN2", target_bir_lowering=False)
    din = lambda name, shape, dt=F32: nc.dram_tensor(name, shape, dt, kind="ExternalInput").ap()
    xT = din("xT", [D, T])
    memT = din("memT", [D, NMEM])
    gains = din("gains", [128, NG * KD])
    w_qm = din("w_qm", [L, D, 256])
    w_mkv = din("w_mkv", [L, D, 512])
    w_o = din("w_o", [L, 768, D])
    w1 = din("w1", [L, D, DFF])
    w2 = din("w2", [L, DFF, D])
    wa_own = din("wa_own", [max(NA, 1), D, 384])
    wb_own = din("wb_own", [max(NB, 1), D, 128])
    wkvs_own = din("wkvs_own", [D, 258])
    bf_own = din("bf_own", [2, 1])
    roff = din("roff", [1, 1], I32)
    c_trineg = din("c_trineg", [128, 128], BF16)
    c_negones = din("c_negones", [128, 128], BF16)
    c_ones = din("c_ones", [128, 128], BF16)
    c_msb = din("c_msb", [128, 4 * 512], BF16)
    c_mfox = din("c_mfox", [128, 4 * 512], BF16)
    c_swap = din("c_swap", [128, 128], F32)
    yT = nc.dram_tensor("yT", [D, T], F32, kind="ExternalOutput").ap()

    hT = nc.dram_tensor("hT", [D, T], F32).ap()
    xg_in = nc.dram_tensor("xg_in", [D, T], BF16).ap()
    xg = nc.dram_tensor("xg", [4 * D, T], BF16).ap()
    memo = nc.dram_tensor("memo", [256, T], BF16).ap()
    mg_in = nc.dram_tensor("mg_in", [4 * 128, T], BF16).ap()
    mg = nc.dram_tensor("mg", [4 * 512, T], BF16).ap()
    mgl = nc.dram_tensor("mgl", [4 * 128, T], BF16).ap()
    ksave = nc.dram_tensor("ksave", [2 * 128, S], BF16).ap()
    vsave = nc.dram_tensor("vsave", [2 * 128, NKB * 128], BF16).ap()

    kview = lambda ap2d: ap2d.rearrange("(k p) n -> p k n", p=128)

    with contextlib.ExitStack() as st:
        P = Prog(nc)
        P.setup(st)
        sb = lambda name, shape, dt, stack=st: stack.enter_context(nc.sbuf_tensor(name, shape, dt))
        PSA = st.enter_context(nc.psum_tensor("psall", [128, 8 * 512], F32))
        PS = [PSA[:, i * 512:(i + 1) * 512] for i in range(8)]
        psk = lambda i: ("ps", i)

        dyn = {}

        @contextlib.contextmanager
        def sp_pre(eng):
            dyn["n"] = dyn.get("n", 0) + 1
            with eng.register("roff_reg%d" % dyn["n"]) as reg:
                eng.reg_load(reg, roff[0:1, 0:1])
                dyn["off"] = eng.snap(reg, min_val=0, max_val=1536)
                yield

        ones = sb("ones", [128, 128], BF16)
        swp = sb("swp", [128, 128], F32)
        gn = sb("gn", [128, NG * KD], F32)
        for dst, src, nm in ((ones, c_ones, "ones"), (swp, c_swap, "swp"), (gn, gains, "gn")):
            P.dma("sp", lambda e, dst=dst, src=src: e.dma_start(out=dst[:], in_=src), writes=[nm])
        gcol = lambda g, k: gn[:, g * KD + k:g * KD + k + 1]

        def rms_rstd(X, n, key, tmp_sq, rstd, ps_i, sqkey="sq", rkey="rstd"):
            P.op("act", lambda e: e.activation(out=tmp_sq[:, 0:KD, 0:n], in_=X[:, :, 0:n], func=AF.Square), reads=[key, "ones"], writes=[sqkey])
            for k in range(KD):
                P.op("pe", lambda e, k=k: e.matmul(PS[ps_i][:, 0:n], ones[:], tmp_sq[:, k, 0:n], start=(k == 0), stop=(k == KD - 1)),
                     reads=[sqkey, "ones"], writes=[psk(ps_i)])
            P.op("act", lambda e: e.activation(out=rstd[:, 0:n], in_=PS[ps_i][:, 0:n], func=AF.Sqrt, scale=1.0 / D, bias=EPS), reads=[psk(ps_i)], writes=[rkey])
            P.op("dve", lambda e: e.reciprocal(out=rstd[:, 0:n], in_=rstd[:, 0:n]), reads=[rkey], writes=[rkey])

        def softmax_av(z_mm, nblk, mask_fn, vaug_fn, vkey, base, out_tile, out_key, wk, n=512):
            o_ps = 6
            for bi in range(nblk):
                zi = bi % 2
                z_mm(bi, zi)
                pt = wk["p"][bi % 2]
                m = mask_fn(bi)
                if m is not None:
                    zt = wk["zt"]
                    P.op("dve", lambda e, zi=zi, m=m: e.tensor_tensor(out=zt[:, 0:n], in0=PS[zi][:, 0:n], in1=m, op=ALU.add), reads=[psk(zi), "mfox"], writes=["zt"])
                    P.op("act", lambda e, pt=pt: e.activation(out=pt[:, 0:n], in_=zt[:, 0:n], func=AF.Exp), reads=["zt"], writes=[("p", bi % 2)])
                else:
                    P.op("act", lambda e, zi=zi, pt=pt: e.activation(out=pt[:, 0:n], in_=PS[zi][:, 0:n], func=AF.Exp), reads=[psk(zi)], writes=[("p", bi % 2)])
                P.op("pe", lambda e, bi=bi, pt=pt: e.matmul(PS[o_ps][:, 0:n], vaug_fn(bi), pt[:, 0:n], start=(bi == 0), stop=(bi == nblk - 1)),
                     reads=[("p", bi % 2), vkey], writes=[psk(o_ps)])
            osb = wk["osb"]
            P.op("act", lambda e: e.activation(out=osb[:, 0:n], in_=PS[o_ps][:, 0:n], func=AF.Copy), reads=[psk(o_ps)], writes=["osb"])
            P.op("pe", lambda e: e.matmul(PS[7][:, 0:n], swp[:], osb[:, 0:n], start=True, stop=True), reads=["osb", "swp"], writes=[psk(7)])
            rd = wk["rden"]
            P.op("dve", lambda e: e.reciprocal(out=rd[base:base + 64, 0:n], in_=PS[7][base:base + 64, 0:n]), reads=[psk(7)], writes=["rden"])
            P.op("dve", lambda e: e.tensor_tensor(out=out_tile[base:base + 64, 0:n], in0=osb[base:base + 64, 0:n], in1=rd[base:base + 64, 0:n], op=ALU.mult),
                 reads=["osb", "rden"], writes=[out_key])

        stg_n = [0]

        def load_w(stg, dst3, src2d, K, N, key):
            SW = stg[0].shape[1]
            kper = max(1, SW // N)
            ncol = min(N, SW)
            for k0 in range(0, K, kper):
                kk = min(kper, K - k0)
                for n0 in range(0, N, ncol):
                    i = stg_n[0] % 3
                    stg_n[0] += 1
                    sview = stg[i][:, 0:kk * ncol].rearrange("p (k n) -> p k n", n=ncol)
                    P.dma("sp", lambda e: e.dma_start(out=sview, in_=src2d[k0 * 128:(k0 + kk) * 128, n0:n0 + ncol].rearrange("(k p) n -> p k n", p=128)), writes=[("stg", i)])
                    eng = ("dve", "pool", "act")[i]
                    if eng == "act":
                        P.op("act", lambda e: e.activation(out=dst3[:, k0:k0 + kk, n0:n0 + ncol], in_=sview, func=AF.Copy), reads=[("stg", i)], writes=[key])
                    else:
                        P.op(eng, lambda e: e.tensor_copy(out=dst3[:, k0:k0 + kk, n0:n0 + ncol], in_=sview), reads=[("stg", i)], writes=[key])

        for l in range(L):
            isA = l < NA
            h_src = xT if l == 0 else hT
            with contextlib.ExitStack() as ph:
                t = lambda name, shape, dt: sb("p1_%d_%s" % (l, name), shape, dt, ph)
                wqm = t("wqm", [128, KD, 256], BF16)
                wmkv = t("wmkv", [128, KD, 512], BF16)
                stg = [t("stg%d" % i, [128, 1024], F32) for i in range(3)]
                load_w(stg, wqm, w_qm[l], KD, 256, "wqm")
                load_w(stg, wmkv, w_mkv[l], KD, 512, "wmkv")
                hc = t("hc", [128, KD, 512], F32)
                sq = t("sq", [128, KD, 512], BF16)
                rstd = t("rstd", [128, 512], F32)
                xh = t("xh", [128, KD, 512], BF16)
                hn = t("hn", [128, KD, 512], BF16)
                mkT = t("mkT", [128, 2, NMEM], BF16)
                mvaug = t("mvaug", [128, 4, 2, 128], BF16)
                qm = t("qm", [128, 2, 512], BF16)
                mo = t("mo", [128, 2, 512], BF16)
                wk = {"p": [t("pA", [128, 512], BF16), t("pB", [128, 512], BF16)], "osb": t("osb", [128, 512], F32), "rden": t("rden", [128, 512], F32)}
                P.dma("sp", lambda e: e.dma_start(out=hc[:, :, 0:NMEM], in_=kview(memT)), writes=["hc"])
                rms_rstd(hc, NMEM, "hc", sq, rstd, 2)
                for k in range(KD):
                    P.op("dve", lambda e, k=k: e.scalar_tensor_tensor(out=hn[:, k, 0:NMEM], in0=hc[:, k, 0:NMEM], scalar=gcol(GM + l, k), in1=rstd[:, 0:NMEM], op0=ALU.mult, op1=ALU.mult),
                         reads=["hc", "rstd", "gn"], writes=["hn"])
                for m in range(2):
                    for k in range(KD):
                        P.op("pe", lambda e, m=m, k=k: e.matmul(PS[m][:, 0:NMEM], wmkv[:, k, m * 128:(m + 1) * 128], hn[:, k, 0:NMEM], start=(k == 0), stop=(k == KD - 1)),
                             reads=["hn", "wmkv"], writes=[psk(m)])
                    P.op("act", lambda e, m=m: e.activation(out=mkT[:, m, :], in_=PS[m][:, 0:NMEM], func=AF.Copy), reads=[psk(m)], writes=["mkT"])
                P.op("pool", lambda e: e.memset(mvaug[:], 1.0), writes=["mvaug"])
                for mb in range(2):
                    for k in range(KD):
                        P.op("pe", lambda e, mb=mb, k=k: e.matmul(PS[2 + mb][:, 0:256], hn[:, k, mb * 128:(mb + 1) * 128], wmkv[:, k, 256:512], start=(k == 0), stop=(k == KD - 1)),
                             reads=["hn", "wmkv"], writes=[psk(2 + mb)])
                    for j in range(4):
                        c0 = (j % 2) * 64
                        P.op("act", lambda e, mb=mb, j=j, c0=c0: e.activation(out=mvaug[:, j, mb, c0:c0 + 64], in_=PS[2 + mb][:, j * 64:(j + 1) * 64], func=AF.Copy),
                             reads=[psk(2 + mb)], writes=["mvaug"])
                for c in range(NC1):
                    cs = slice(c * CH1, (c + 1) * CH1)
                    n = CH1
                    P.dma("sp", lambda e, cs=cs: e.dma_start(out=hc[:, :, 0:n], in_=kview(h_src)[:, :, cs]), reads=["hT"], writes=["hc"])
                    rms_rstd(hc, n, "hc", sq, rstd, 2)
                    for k in range(KD):
                        P.op("dve", lambda e, k=k: e.tensor_tensor(out=xh[:, k, 0:n], in0=hc[:, k, 0:n], in1=rstd[:, 0:n], op=ALU.mult), reads=["hc", "rstd"], writes=["xh"])
                        P.op("act", lambda e, k=k: e.activation(out=hn[:, k, 0:n], in_=xh[:, k, 0:n], func=AF.Copy, scale=gcol(G1 + l, k)),
                             reads=["xh", "gn"], writes=["hn"])
                    P.dma("sp", lambda e, cs=cs: e.dma_start(out=kview(xg_in)[:, :, cs], in_=xh[:, :, 0:n]), reads=["xh"], writes=["xg_in"])
                    for m in range(2):
                        for k in range(KD):
                            P.op("pe", lambda e, m=m, k=k: e.matmul(PS[m][:, 0:n], wqm[:, k, m * 128:(m + 1) * 128], hn[:, k, 0:n], start=(k == 0), stop=(k == KD - 1)),
                                 reads=["hn", "wqm"], writes=[psk(m)])
                        P.op("act", lambda e, m=m: e.activation(out=qm[:, m, 0:n], in_=PS[m][:, 0:n], func=AF.Copy, scale=0.125), reads=[psk(m)], writes=["qm"])
                    for j in range(4):
                        base = (j % 2) * 64
                        jj = j // 2

                        def z_mm(bi, zi, base=base, jj=jj):
                            P.op("pe", lambda e: e.matmul(PS[zi][:, 0:n], mkT[base:base + 64, jj, bi * 128:(bi + 1) * 128], qm[base:base + 64, jj, 0:n], start=True, stop=True),
                                 reads=["mkT", "qm"], writes=[psk(zi)])

                        softmax_av(z_mm, 2, lambda bi: None, lambda bi, j=j: mvaug[:, j, bi, :], "mvaug", base, mo[:, jj, :], "mo", wk, n=n)
                    P.dma("sp", lambda e, cs=cs: e.dma_start(out=memo.rearrange("(k p) n -> p k n", p=128)[:, :, cs], in_=mo[:, :, 0:n]), reads=["mo"], writes=["memo"])
                P.wait_all("sp")
                P.emit()

            for k in range(KD):
                P.dma("pool", lambda e, k=k: e.collective_compute("AllGather", ALU.bypass, replica_groups=GROUPS, ins=[xg_in[k * 128:(k + 1) * 128, :]], outs=[xg[k * 512:(k + 1) * 512, :]]),
                      reads=["xg_in"], writes=["xg"], inc=1, fresh=True)

            with contextlib.ExitStack() as ph:
                t = lambda name, shape, dt: sb("p2_%d_%s" % (l, name), shape, dt, ph)
                xc = [t("xcA", [128, KD, 512], BF16), t("xcB", [128, KD, 512], BF16)]
                wst = t("wst", [128, KD, 384], F32)
                mixc = [t("mix%d" % i, [128, 512], BF16) for i in range(4)]

                def load_own_w(dst, src_ap, ncol, gidx, key, c0=0):
                    P.dma("sp", lambda e: e.dma_start(out=wst[:, :, 0:ncol], in_=kview(src_ap)), writes=["wst"])
                    for k in range(KD):
                        P.op("dve", lambda e, k=k: e.tensor_scalar(out=dst[:, k, c0:c0 + ncol], in0=wst[:, k, 0:ncol], scalar1=gcol(gidx, k), scalar2=None, op0=ALU.mult),
                             reads=["wst", "gn"], writes=[key])

                def load_xc(tc, slot=None):
                    slot = tc % 2 if slot is None else slot % 2
                    rk, cc = divmod(tc * 512, T)
                    buf = xc[slot]
                    P.dma("sp", lambda e: e.dma_start(out=buf[:], in_=xg.rearrange("(k r p) t -> r p k t", k=KD, r=4)[rk][:, :, cc:cc + 512]), reads=["xg"], writes=[("xc", slot)])
                    return buf, ("xc", slot)

                def mix_out(mix_tile, mkey, base, qc):
                    rdst, cc = divmod(qc * 512, T)
                    P.dma("sp", lambda e: e.dma_start(out=mg_in[rdst * 128 + base:rdst * 128 + base + 64, cc:cc + 512], in_=mix_tile[base:base + 64, :]),
                          reads=[mkey], writes=["mg_in"])

                def load_const(name, shape, dt, src):
                    tl = t(name, shape, dt)
                    P.dma("sp", lambda e: e.dma_start(out=tl[:], in_=src), writes=[name])
                    return tl

                if isA:
                    trineg = load_const("trineg", [128, 128], BF16, c_trineg)
                    negones = load_const("negones", [128, 128], BF16, c_negones)
                    msb = load_const("msb", [128, 4 * 512], BF16, c_msb)
                    wown = t("wown", [128, KD, 384], BF16)
                    load_own_w(wown, wa_own[l], 384, G1 + l, "wown")
                    qT2 = t("qT2", [128, S], BF16)
                    kT2 = t("kT2", [128, S], BF16)
                    v2 = t("v2", [128, NKB, 128], BF16)
                    E2 = [t("e%d" % i, [128, 1024], F32) for i in range(3)]
                    SP2 = [t("sp%d" % i, [128, 1024], BF16) for i in range(3)]
                    X2 = [t("x%d" % i, [128, 1024], F32) for i in range(2)]
                    W2 = [t("w%d" % i, [128, 1024], BF16) for i in range(2)]
                    LB2 = [t("lb%d" % i, [128, 1024], BF16) for i in range(2)]
                    nxt = load_xc(0)
                    for tc in range(NQ):
                        buf, bk = nxt
                        if tc + 1 < NQ:
                            nxt = load_xc(tc + 1)
                        ts_ = slice(tc * 512, (tc + 1) * 512)
                        for which, dst, dkey, sc in ((0, qT2, "qT2", 0.125), (1, kT2, "kT2", 1.0)):
                            pi = which
                            for k in range(KD):
                                P.op("pe", lambda e, k=k, which=which, pi=pi: e.matmul(PS[pi][:, :], wown[:, k, which * 128:(which + 1) * 128], buf[:, k, :], start=(k == 0), stop=(k == KD - 1)),
                                     reads=[bk, "wown"], writes=[psk(pi)])
                            P.op("act", lambda e, dst=dst, pi=pi, sc=sc: e.activation(out=dst[:, ts_], in_=PS[pi][:, :], func=AF.Copy, scale=sc), reads=[psk(pi)], writes=[dkey])
                        for tb in range(4):
                            pi = 2 + tb % 2
                            for k in range(KD):
                                P.op("pe", lambda e, k=k, tb=tb, pi=pi: e.matmul(PS[pi][:, 0:128], buf[:, k, tb * 128:(tb + 1) * 128], wown[:, k, 256:384], start=(k == 0), stop=(k == KD - 1)),
                                     reads=[bk, "wown"], writes=[psk(pi)])
                            P.op("dve", lambda e, tb=tb, pi=pi: e.tensor_copy(out=v2[:, tc * 4 + tb, :], in_=PS[pi][:, 0:128]), reads=[psk(pi)], writes=["v2"])
                    ptiles = [(qc, i, 4 * qc + 4) for qc in range(NQ) for i in range(4 * qc + 4)]
                    NPT = len(ptiles)
                    hsl = lambda hh: slice(hh * 512, (hh + 1) * 512)

                    def pA_pe(s_):
                        qc, i, nb = ptiles[s_]
                        kb, zb = nb - 1 - i, (0 if s_ % 2 == 0 else 6)
                        qs = slice(qc * 512, (qc + 1) * 512)
                        for hh in range(2):
                            base = hh * 64
                            P.op("pe", lambda e: e.matmul(PS[zb + hh][:, :], kT2[base:base + 64, kb * 128:(kb + 1) * 128], qT2[base:base + 64, qs], start=True, stop=True),
                                 reads=["kT2", "qT2"], writes=[psk(zb + hh)])

                    def pA(s_):
                        qc, i, nb = ptiles[s_]
                        kb, zb, eb = nb - 1 - i, (0 if s_ % 2 == 0 else 6), s_ % 3
                        P.op("act", lambda e: e.activation(out=E2[eb][:], in_=PSA[:, zb * 512:(zb + 2) * 512], func=AF.Exp), reads=[psk(zb), psk(zb + 1)], writes=[("E2", eb)])
                        P.op("act", lambda e: e.activation(out=SP2[eb][:], in_=E2[eb][:], func=AF.Ln, bias=1.0), reads=[("E2", eb)], writes=[("SP2", eb)])
                        if kb >= 4 * qc:
                            rel = kb - 4 * qc
                            mk = msb[:, rel * 512:(rel + 1) * 512]
                            for hh in range(2):
                                P.op("pool", lambda e: e.tensor_tensor(out=SP2[eb][:, hsl(hh)], in0=SP2[eb][:, hsl(hh)], in1=mk, op=ALU.mult), reads=[("SP2", eb), "msb"], writes=[("SP2", eb)])
                                P.op("pool", lambda e: e.tensor_tensor(out=E2[eb][:, hsl(hh)], in0=E2[eb][:, hsl(hh)], in1=mk, op=ALU.mult), reads=[("E2", eb), "msb"], writes=[("E2", eb)])

                    def pB(s_):
                        qc, i, nb = ptiles[s_]
                        eb = s_ % 3
                        for hh in range(2):
                            P.op("pe", lambda e: e.matmul(PS[2 + hh][:, :], trineg[:], SP2[eb][:, hsl(hh)], start=True, stop=(i == 0)), reads=[("SP2", eb), "trineg"], writes=[psk(2 + hh)])
                            if i > 0:
                                P.op("pe", lambda e: e.matmul(PS[2 + hh][:, :], negones[:], LB2[(i - 1) % 2][:, hsl(hh)], start=False, stop=True),
                                     reads=[("LB2", (i - 1) % 2), "negones"], writes=[psk(2 + hh)])
                        if i < nb - 1:
                            if i == 0:
                                P.op("pool", lambda e: e.tensor_copy(out=LB2[0][:], in_=SP2[eb][:]), reads=[("SP2", eb)], writes=[("LB2", 0)])
                            else:
                                P.op("pool", lambda e: e.tensor_tensor(out=LB2[i % 2][:], in0=LB2[(i - 1) % 2][:], in1=SP2[eb][:], op=ALU.add),
                                     reads=[("SP2", eb), ("LB2", (i - 1) % 2)], writes=[("LB2", i % 2)])
                        P.op("act", lambda e: e.activation(out=X2[s_ % 2][:], in_=PSA[:, 2 * 512:4 * 512], func=AF.Exp), reads=[psk(2), psk(3)], writes=[("X2", s_ % 2)])

                    def pC(s_):
                        qc, i, nb = ptiles[s_]
                        kb, eb = nb - 1 - i, s_ % 3
                        o_ps = 4 + qc % 2
                        P.op("dve", lambda e: e.tensor_tensor(out=W2[s_ % 2][:], in0=E2[eb][:], in1=X2[s_ % 2][:], op=ALU.mult), reads=[("E2", eb), ("X2", s_ % 2)], writes=[("W2", s_ % 2)])
                        for hh in range(2):
                            base = hh * 64
                            P.op("pe", lambda e: e.matmul(PS[o_ps][base:base + 64, :], v2[:, kb, base:base + 64], W2[s_ % 2][:, hsl(hh)], start=(i == 0), stop=(i == nb - 1)),
                                 reads=[("W2", s_ % 2), "v2"], writes=[(psk(o_ps), hh)])
                        if i == nb - 1:
                            mt, mkey = mixc[qc % 4], ("mix", qc % 4)
                            P.op("dve", lambda e: e.tensor_copy(out=mt[:, :], in_=PS[o_ps][:, :]), reads=[(psk(o_ps), 0), (psk(o_ps), 1)], writes=[mkey])
                            for hh in range(2):
                                mix_out(mt, mkey, hh * 64, qc)

                    pA_pe(0)
                    for s_ in range(NPT + 2):
                        if s_ + 1 < NPT:
                            pA_pe(s_ + 1)
                        if s_ < NPT:
                            pA(s_)
                        if 0 <= s_ - 1 < NPT:
                            pB(s_ - 1)
                        if 0 <= s_ - 2 < NPT:
                            pC(s_ - 2)
                else:
                    lb_ = l - NA
                    first_b = (lb_ == 0)
                    mfox = load_const("mfox", [128, 4 * 512], BF16, c_mfox)
                    wq = t("wq", [128, KD, 128], BF16)
                    load_own_w(wq, wb_own[lb_], 128, G1 + l, "wq")
                    if first_b:
                        wkv = t("wkv", [128, KD, 258], BF16)
                        load_own_w(wkv, wkvs_own, 258, GKV, "wkv")
                        bfo = t("bfo", [2, 1], F32)
                        P.dma("sp", lambda e: e.dma_start(out=bfo[:], in_=bf_own), writes=["bfo"])
                        P.op("dve", lambda e: e.tensor_scalar(out=bfo[:], in0=bfo[:], scalar1=-1.0, scalar2=None, op0=ALU.mult), reads=["bfo"], writes=["bfo"])
                        one2 = t("one2", [2, 512], F32)
                        P.op("pool", lambda e: e.memset(one2[:], 1.0), writes=["one2"])
                        fe = t("fe", [2, 512], F32)
                        cc_ = [t("ccA", [2, 512], F32), t("ccB", [2, 512], F32)]
                        r1 = t("r1", [2, 512], F32)
                        spl = t("spl", [2, 6, 512], BF16)
                    qaug = t("qaug", [128, S], BF16)
                    kaug = t("kaug", [128, S], BF16)
                    vaug = t("vaug", [128, NKB, 128], BF16)
                    wk = {"p": [t("pA", [128, 512], BF16), t("pB", [128, 512], BF16)], "osb": t("osb", [128, 512], F32), "rden": t("rden", [128, 512], F32),
                          "zt": t("zt", [128, 512], F32), "zt2": [t("ztA", [128, 512], F32), t("ztB", [128, 512], F32)],
                          "p3": [t("p3_%d" % i, [128, 512], BF16) for i in range(4)]}
                    for hh in range(2):
                        base = hh * 64
                        P.op("pool", lambda e: e.memset(qaug[64:128, :], 1.0), reads=[], writes=["qaug"])
                        if first_b:
                            P.op("pool", lambda e: e.memset(kaug[64:128, :], 1.0), writes=["kaug"])
                            P.op("pool", lambda e: e.memset(vaug[:], 1.0), writes=["vaug"])
                        else:
                            P.dma("sp", lambda e, hh=hh: e.dma_start(out=kaug[:], in_=ksave[hh * 128:(hh + 1) * 128, :]), reads=["ksave"], writes=["kaug"])
                            P.dma("sp", lambda e, hh=hh: e.dma_start(out=vaug[:], in_=vsave[hh * 128:(hh + 1) * 128, :].rearrange("p (b c) -> p b c", c=128)), reads=["vsave"], writes=["vaug"])
                        nxt = load_xc(0, hh * NQ)
                        for tc in range(NQ):
                            buf, bk = nxt
                            if tc + 1 < NQ:
                                nxt = load_xc(tc + 1, tc + 1 + hh * NQ)
                            ts_ = slice(tc * 512, (tc + 1) * 512)
                            for k in range(KD):
                                P.op("pe", lambda e, k=k: e.matmul(PS[0][0:64, :], wq[:, k, base:base + 64], buf[:, k, :], start=(k == 0), stop=(k == KD - 1)),
                                     reads=[bk, "wq"], writes=[psk(0)])
                            P.op("act", lambda e: e.activation(out=qaug[0:64, ts_], in_=PS[0][0:64, :], func=AF.Copy, scale=0.125), reads=[psk(0)], writes=["qaug"])
                            if first_b:
                                for k in range(KD):
                                    P.op("pe", lambda e, k=k: e.matmul(PS[1][0:64, :], wkv[:, k, base:base + 64], buf[:, k, :], start=(k == 0), stop=(k == KD - 1)),
                                         reads=[bk, "wkv"], writes=[psk(1)])
                                P.op("act", lambda e: e.activation(out=kaug[0:64, ts_], in_=PS[1][0:64, :], func=AF.Copy), reads=[psk(1)], writes=["kaug"])
                                for tb in range(4):
                                    pi = 2 + tb % 2
                                    for k in range(KD):
                                        P.op("pe", lambda e, k=k, tb=tb, pi=pi: e.matmul(PS[pi][:, 0:64], buf[:, k, tb * 128:(tb + 1) * 128], wkv[:, k, 128 + base:128 + base + 64], start=(k == 0), stop=(k == KD - 1)),
                                             reads=[bk, "wkv"], writes=[psk(pi)])
                                    P.op("dve", lambda e, tb=tb, pi=pi: e.tensor_copy(out=vaug[:, tc * 4 + tb, base:base + 64], in_=PS[pi][:, 0:64]), reads=[psk(pi)], writes=["vaug"])
                                for k in range(KD):
                                    P.op("pe", lambda e, k=k: e.matmul(PS[4][0:2, :], wkv[:, k, 256:258], buf[:, k, :], start=(k == 0), stop=(k == KD - 1)),
                                         reads=[bk, "wkv"], writes=[psk(4)])
                                P.op("act", lambda e: e.activation(out=fe[:], in_=PS[4][0:2, :], func=AF.Exp, scale=-1.0, bias=bfo[:, 0:1]), reads=[psk(4), "bfo"], writes=["fe"])
                                P.op("act", lambda e: e.activation(out=fe[:], in_=fe[:], func=AF.Ln, bias=1.0), reads=["fe"], writes=["fe"])
                                P.op("dve", lambda e: e.tensor_scalar(out=fe[:], in0=fe[:], scalar1=-1.0, scalar2=None, op0=ALU.mult), reads=["fe"], writes=["fe"])
                                cur, prev = cc_[tc % 2], cc_[(tc + 1) % 2]
                                init = 0.0 if tc == 0 else prev[:, 511:512]
                                P.op("dve", lambda e, cur=cur, init=init: e.tensor_tensor_scan(out=cur[:], data0=one2[:], data1=fe[:], initial=init, op0=ALU.mult, op1=ALU.add),
                                     reads=["fe", "one2", ("cc", (tc + 1) % 2)], writes=[("cc", tc % 2)])
                                P.op("dve", lambda e, cur=cur: e.tensor_copy(out=spl[:, 0, :], in_=cur[:]), reads=[("cc", tc % 2)], writes=["spl"])
                                P.op("dve", lambda e, cur=cur: e.tensor_tensor(out=r1[:], in0=cur[:], in1=spl[:, 0, :], op=ALU.subtract), reads=[("cc", tc % 2), "spl"], writes=["r1"])
                                P.op("dve", lambda e: e.tensor_copy(out=spl[:, 1, :], in_=r1[:]), reads=["r1"], writes=["spl"])
                                P.op("dve", lambda e: e.tensor_tensor(out=r1[:], in0=r1[:], in1=spl[:, 1, :], op=ALU.subtract), reads=["r1", "spl"], writes=["r1"])
                                P.op("dve", lambda e: e.tensor_copy(out=spl[:, 2, :], in_=r1[:]), reads=["r1"], writes=["spl"])
                                P.op("dve", lambda e: e.tensor_scalar(out=spl[:, 3:6, :], in0=spl[:, 0:3, :], scalar1=-1.0, scalar2=None, op0=ALU.mult), reads=["spl"], writes=["spl"])
                                for j3 in range(3):
                                    P.dma("sp", lambda e, hh=hh, j3=j3: e.dma_start(out=qaug[64 + j3:65 + j3, ts_], in_=spl[hh:hh + 1, j3, :]), reads=["spl"], writes=["qaug"])
                                    P.dma("sp", lambda e, hh=hh, j3=j3: e.dma_start(out=kaug[67 + j3:68 + j3, ts_], in_=spl[hh:hh + 1, 3 + j3, :]), reads=["spl"], writes=["kaug"])
                                    P.dma("sp", lambda e, hh=hh, j3=j3: e.dma_start(out=kaug[70 + j3:71 + j3, ts_], in_=spl[hh:hh + 1, j3, :]), reads=["spl"], writes=["kaug"])
                        if first_b and NB > 1:
                            P.dma("sp", lambda e, hh=hh: e.dma_start(out=ksave[hh * 128:(hh + 1) * 128, :], in_=kaug[:]), reads=["kaug"], writes=["ksave"])
                            P.dma("sp", lambda e, hh=hh: e.dma_start(out=vsave[hh * 128:(hh + 1) * 128, :].rearrange("p (b c) -> p b c", c=128), in_=vaug[:]), reads=["vaug"], writes=["vsave"])
                        if not first_b:
                            P.dma("sp", lambda e: e.dma_start(out=qaug[64:67, :], in_=kaug[70:73, :]), reads=["kaug"], writes=["qaug"])
                        ftiles = [(qc, bi, 4 * qc + 4) for qc in range(NQ) for bi in range(4 * qc + 4)]
                        NFT = len(ftiles)
                        PB = wk["p3"]

                        def fA_pe(g):
                            qc, bi, nb = ftiles[g]
                            zi = g % 3
                            qs = slice(qc * 512, (qc + 1) * 512)
                            P.op("pe", lambda e: e.matmul(PS[zi][:, :], kaug[0:70, bi * 128:(bi + 1) * 128], qaug[0:70, qs], start=True, stop=True),
                                 reads=["kaug", "qaug"], writes=[psk(zi)])

                        def fA(g, base=base):
                            qc, bi, nb = ftiles[g]
                            zi, pb = g % 3, g % 4
                            if bi >= 4 * qc:
                                rel = bi - 4 * qc
                                m = mfox[:, rel * 512:(rel + 1) * 512]
                                zt = wk["zt2"][g % 2]
                                P.op("dve", lambda e: e.tensor_tensor(out=zt[:], in0=PS[zi][:, :], in1=m, op=ALU.add), reads=[psk(zi), "mfox"], writes=[("zt", g % 2)])
                                P.op("act", lambda e: e.activation(out=PB[pb][:], in_=zt[:], func=AF.Exp), reads=[("zt", g % 2)], writes=[("p3", pb)])
                            else:
                                P.op("act", lambda e: e.activation(out=PB[pb][:], in_=PS[zi][:, :], func=AF.Exp), reads=[psk(zi)], writes=[("p3", pb)])

                        def fB(g, base=base):
                            qc, bi, nb = ftiles[g]
                            o_ps = 4 + qc % 2
                            P.op("pe", lambda e: e.matmul(PS[o_ps][:, :], vaug[:, bi, :], PB[g % 4][:], start=(bi == 0), stop=(bi == nb - 1)),
                                 reads=[("p3", g % 4), "vaug"], writes=[psk(o_ps)])

                        def fEpi(g, base=base):
                            qc, bi, nb = ftiles[g]
                            o_ps = 4 + qc % 2
                            osb, rd = wk["osb"], wk["rden"]
                            mt, mkey = mixc[qc % 2], ("mix", qc % 2)
                            P.op("act", lambda e: e.activation(out=osb[:], in_=PS[o_ps][:, :], func=AF.Copy), reads=[psk(o_ps)], writes=["osb"])
                            P.op("pe", lambda e: e.matmul(PS[7][:, :], swp[:], osb[:], start=True, stop=True), reads=["osb", "swp"], writes=[psk(7)])
                            P.op("dve", lambda e: e.reciprocal(out=rd[base:base + 64, :], in_=PS[7][base:base + 64, :]), reads=[psk(7)], writes=["rden"])
                            P.op("dve", lambda e: e.tensor_tensor(out=mt[base:base + 64, :], in0=osb[base:base + 64, :], in1=rd[base:base + 64, :], op=ALU.mult),
                                 reads=["osb", "rden"], writes=[mkey])
                            mix_out(mt, mkey, base, qc)

                        fA_pe(0)
                        for s_ in range(NFT + 4):
                            if s_ + 1 < NFT:
                                fA_pe(s_ + 1)
                            if s_ < NFT:
                                fA(s_)
                            if 0 <= s_ - 2 < NFT:
                                fB(s_ - 2)
                            g2 = s_ - 4
                            if 0 <= g2 < NFT and ftiles[g2][1] == ftiles[g2][2] - 1:
                                fEpi(g2)
                P.wait_all("sp")
                P.emit()

            for r_ in range(4):
                P.dma("pool", lambda e, r_=r_: e.collective_compute("AllGather", ALU.bypass, replica_groups=GROUPS, ins=[mg_in[r_ * 128:(r_ + 1) * 128, :]], outs=[mg[r_ * 512:(r_ + 1) * 512, :]]),
                      reads=["mg_in"], writes=["mg"], inc=1, fresh=True)

            with contextlib.ExitStack() as ph:
                t = lambda name, shape, dt: sb("p3_%d_%s" % (l, name), shape, dt, ph)
                wo = t("wo", [128, 6, D], BF16)
                W1 = t("W1", [128, KD, DFF], BF16)
                W2 = t("W2", [128, KF, D], BF16)
                stg = [t("stg%d" % i, [128, 512], F32) for i in range(3)]
                load_w(stg, wo, w_o[l], 6, D, "wo")
                load_w(stg, W1, w1[l], KD, DFF, "W1")
                load_w(stg, W2, w2[l], KF, D, "W2")
                n = CH3
                mx = t("mx", [128, KD, n], BF16)
                hc = t("hc", [128, KD, n], F32)
                h1 = t("h1", [128, max(KF, KD), n], BF16)
                rl = [t("rlA", [128, n], F32), t("rlB", [128, n], F32)]
                rstd, RK = rl[0], ("rl", 0)
                P.dma("sp", lambda e: e.dma_start(out=mgl, in_=Late(lambda: mg[bass.ds(dyn["off"], 512), :])), reads=["mg"], writes=["mgl"])
                for c in range(NC3):
                    cs = slice(c * n, (c + 1) * n)
                    P.dma("sp", lambda e: e.dma_start(out=mx[:, 0:4, :], in_=kview(mgl)[:, :, cs]), reads=["mgl"], writes=["mx"])
                    P.dma("sp", lambda e: e.dma_start(out=mx[:, 4:6, :], in_=memo.rearrange("(k p) n -> p k n", p=128)[:, :, cs]), reads=["memo"], writes=["mx"])
                    P.dma("sp", lambda e: e.dma_start(out=hc[:], in_=kview(h_src)[:, :, cs]), reads=["hT"], writes=["hc"])
                    for m in range(KD):
                        pi = m % 2
                        for k in range(6):
                            P.op("pe", lambda e: e.matmul(PS[pi][:, 0:n], wo[:, k, m * 128:(m + 1) * 128], mx[:, k, :], start=(k == 0), stop=(k == 5)),
                                 reads=["mx", "wo"], writes=[psk(pi)])
                        P.op("dve", lambda e: e.tensor_tensor(out=hc[:, m, :], in0=hc[:, m, :], in1=PS[pi][:, 0:n], op=ALU.add), reads=[psk(pi), "hc"], writes=["hc"])
                    rms_rstd(hc, n, "hc", h1, rstd, 2, sqkey="h1", rkey=RK)
                    for k in range(KD):
                        P.op("dve", lambda e: e.scalar_tensor_tensor(out=mx[:, k, :], in0=hc[:, k, :], scalar=gcol(G2 + l, k), in1=rstd[:, 0:n], op0=ALU.mult, op1=ALU.mult),
                             reads=["hc", RK, "gn"], writes=["mx"])
                    for m in range(KF):
                        pi = (0, 1, 3, 4)[m % 4]
                        for k in range(KD):
                            P.op("pe", lambda e: e.matmul(PS[pi][:, 0:n], W1[:, k, m * 128:(m + 1) * 128], mx[:, k, :], start=(k == 0), stop=(k == KD - 1)),
                                 reads=["mx", "W1"], writes=[psk(pi)])
                        r = rl[m % 2]
                        P.op("act", lambda e: e.activation(out=r[:], in_=PS[pi][:, 0:n], func=AF.Relu), reads=[psk(pi)], writes=[("rl", m % 2)])
                        P.op("pool" if m % 2 else "dve", lambda e: e.tensor_tensor(out=h1[:, m, :], in0=r[:], in1=r[:], op=ALU.mult), reads=[("rl", m % 2)], writes=["h1"])
                    last = (l == L - 1)
                    for m in range(KD):
                        pi = 5 + m % 2
                        for k in range(KF):
                            P.op("pe", lambda e: e.matmul(PS[pi][:, 0:n], W2[:, k, m * 128:(m + 1) * 128], h1[:, k, :], start=(k == 0), stop=(k == KF - 1)),
                                 reads=["h1", "W2"], writes=[psk(pi)])
                        P.op("dve", lambda e: e.tensor_tensor(out=hc[:, m, :], in0=hc[:, m, :], in1=PS[pi][:, 0:n], op=ALU.add), reads=[psk(pi), "hc"], writes=["hc"])
                    if not last:
                        P.dma("sp", lambda e: e.dma_start(out=kview(hT)[:, :, cs], in_=hc[:]), reads=["hc"], writes=["hT"])
                    else:
                        rms_rstd(hc, n, "hc", h1, rstd, 2, sqkey="h1", rkey=RK)
                        for k in range(KD):
                            P.op("dve", lambda e: e.scalar_tensor_tensor(out=hc[:, k, :], in0=hc[:, k, :], scalar=gcol(GFIN, k), in1=rstd[:, 0:n], op0=ALU.mult, op1=ALU.mult),
                                 reads=["hc", RK, "gn"], writes=["hc"])
                        P.dma("sp", lambda e: e.dma_start(out=kview(yT)[:, :, cs], in_=hc[:]), reads=["hc"], writes=["yT"])
                P.wait_all("sp")
                P.pre["sp"] = sp_pre
                P.emit()
    return nc


def _consts():
    bf = ml_dtypes.bfloat16
    j = np.arange(128)[:, None]
    s = np.arange(128)[None, :]
    trineg = np.where(j >= s, -1.0, 0.0).astype(bf)
    negones = np.full((128, 128), -1.0).astype(bf)
    ones = np.ones((128, 128)).astype(bf)
    p = np.arange(128)[:, None]
    tq = np.arange(512)[None, :]
    msb = np.concatenate([(r * 128 + p < tq) for r in range(4)], axis=1).astype(np.float32).astype(bf)
    mfox = (np.concatenate([(r * 128 + p <= tq) for r in range(4)], axis=1).astype(np.float32) - 1.0) * 30000.0
    mfox = mfox.astype(bf)
    swap = np.zeros((128, 128), np.float32)
    swap[(np.arange(128) + 64) % 128, np.arange(128)] = 1.0
    return dict(c_trineg=trineg, c_negones=negones, c_ones=ones, c_msb=msb, c_mfox=mfox, c_swap=swap)


def make_in_maps(inp, S, DFF, NA, NB):
    L = NA + NB
    T = S // 4
    f32 = np.float32
    G = np.concatenate([inp["norm1_g"], inp["mem_norm_g"], inp["norm2_g"], inp["kv_norm_g"][None], inp["final_norm_g"][None]], 0).astype(f32)
    NG = G.shape[0]
    gains = np.ascontiguousarray(G.reshape(NG, 8, 128).transpose(2, 0, 1).reshape(128, NG * 8))
    consts = _consts()
    w_in_a, w_in_b, wkv = inp["w_in_a"], inp["w_in_b"], inp["w_kv_shared"]
    qm_cols = lambda w, off: w[:, :, off:off + 256]
    w_qm = np.ascontiguousarray(np.concatenate([qm_cols(w_in_a, 1536), qm_cols(w_in_b, 512)], 0)) if NB > 0 else np.ascontiguousarray(qm_cols(w_in_a, 1536))
    maps = []
    for c in range(8):
        b, r = divmod(c, 4)
        hs = slice(r * 128, (r + 1) * 128)
        wa = np.concatenate([w_in_a[:, :, r * 128:(r + 1) * 128], w_in_a[:, :, 512 + r * 128:512 + (r + 1) * 128], w_in_a[:, :, 1024 + r * 128:1024 + (r + 1) * 128]], axis=2)
        wb = w_in_b[:, :, hs] if NB > 0 else np.zeros((1, D, 128), f32)
        wk = np.concatenate([wkv[:, hs], wkv[:, 512 + r * 128:512 + (r + 1) * 128], wkv[:, 1024 + 2 * r:1024 + 2 * r + 2]], axis=1)
        m = dict(
            xT=np.ascontiguousarray(inp["x"][b, r * T:(r + 1) * T, :].T),
            memT=np.ascontiguousarray(inp["mem"][b].T),
            gains=gains, w_qm=w_qm, w_mkv=inp["w_mem_kv"], w_o=inp["w_o"], w1=inp["w_mlp1"], w2=inp["w_mlp2"],
            wa_own=np.ascontiguousarray(wa), wb_own=np.ascontiguousarray(wb), wkvs_own=np.ascontiguousarray(wk),
            bf_own=np.ascontiguousarray(inp["b_f"][2 * r:2 * r + 2].reshape(2, 1)),
            roff=np.array([[r * 512]], np.int32),
        )
        m.update(consts)
        maps.append(m)
    return maps


def assemble(results, S):
    T = S // 4
    out = np.empty((2, S, D), np.float32)
    for c in range(8):
        b, r = divmod(c, 4)
        out[b, r * T:(r + 1) * T, :] = results[c]["yT"].T
    return out


def kernel(**inputs):
    inp = {k: np.asarray(v) for k, v in inputs.items()}
    S, DFF, NA, NB = 16384, 4096, 2, 2
    nc = build_program(S, DFF, NA, NB)
    maps = make_in_maps(inp, S, DFF, NA, NB)
    res = run_bass_kernel_spmd(nc, maps, core_ids=list(range(8)))
    return assemble(res.results, S)
```

```python
import contextlib
import numpy as np
import ml_dtypes
import concourse.bass as bass
import concourse.mybir as mybir
from concourse.bass_utils import run_bass_kernel_spmd

F32 = mybir.dt.float32
BF16 = mybir.dt.bfloat16
I32 = mybir.dt.int32
AF = mybir.ActivationFunctionType
ALU = mybir.AluOpType

D = 1024
KD = 8
NMEM = 256
EPS = 1e-6
GROUPS = [[0, 1, 2, 3], [4, 5, 6, 7]]


class _Q:
    def __init__(self, name, sem):
        self.name = name
        self.sem = sem
        self.count = 0
        self.entries = []
        self.waited = {}


class Late:
    def __init__(self, f):
        self.f = f


class _Rec:
    def __getattr__(self, name):
        return lambda *a, **kw: (name, a, kw)


_REC = _Rec()


def _replay(eng, call):
    name, a, kw = call
    a = [x.f() if isinstance(x, Late) else x for x in a]
    kw = {k: (v.f() if isinstance(v, Late) else v) for k, v in kw.items()}
    try:
        return getattr(eng, name)(*a, **kw)
    except Exception:
        print("REPLAY FAIL", name, [getattr(x, "shape", x) for x in a], {k: getattr(v, "shape", v) for k, v in kw.items()})
        raise


class Prog:
    def __init__(self, nc, n_dma_sems=24):
        self.nc = nc
        self.q = {}
        self.keys = {}
        self.dma_sems = []
        self.n_dma_sems = n_dma_sems
        self.dma_rr = 0
        self.no_same_engine_wait = {"pe"}
        self.pre = {}
        self.cc_sems = []

    def setup(self, stack):
        nc = self.nc
        self.stack = stack
        for name in ("pe", "act", "dve", "pool", "sp"):
            sem = stack.enter_context(nc.semaphore("s_" + name))
            self.q[name] = _Q(name, sem)
        for i in range(self.n_dma_sems):
            sem = stack.enter_context(nc.semaphore("s_dma%d" % i))
            self.dma_sems.append([sem, 0])

    def _deps(self, reads, writes):
        deps = []
        for k in reads:
            st = self.keys.get(k)
            if st and st[0] is not None:
                deps.append(st[0])
        for k in writes:
            st = self.keys.get(k)
            if st:
                if st[0] is not None:
                    deps.append(st[0])
                deps.extend(st[1].values())
        return deps

    def _commit(self, reads, writes, tok):
        for k in reads:
            st = self.keys.setdefault(k, [None, {}])
            st[1][id(tok[0])] = tok
        for k in writes:
            self.keys[k] = [tok, {}]

    def _waits(self, q, deps):
        need = {}
        for sem, val in deps:
            if sem is q.sem and q.name in self.no_same_engine_wait:
                continue
            if q.waited.get(id(sem), 0) >= val:
                continue
            if need.get(id(sem), (None, 0))[1] < val:
                need[id(sem)] = (sem, val)
        out = []
        for sem, val in need.values():
            q.waited[id(sem)] = val
            out.append((sem, val))
        return out

    def op(self, qname, fn, reads=(), writes=()):
        q = self.q[qname]
        waits = self._waits(q, self._deps(reads, writes))
        q.count += 1
        tok = (q.sem, q.count)
        q.entries.append((waits, fn(_REC), (q.sem, 1)))
        self._commit(reads, writes, tok)
        return tok

    def dma(self, qname, fn, reads=(), writes=(), inc=16, fresh=False):
        q = self.q[qname]
        if fresh:
            slot = [self.stack.enter_context(self.nc.semaphore("s_cc%d" % len(self.cc_sems))), 0]
            self.cc_sems.append(slot)
        else:
            slot = self.dma_sems[self.dma_rr % self.n_dma_sems]
            self.dma_rr += 1
        deps = self._deps(reads, writes)
        if slot[1] > 0:
            deps.append((slot[0], slot[1]))
        waits = self._waits(q, deps)
        slot[1] += inc
        tok = (slot[0], slot[1])
        q.entries.append((waits, fn(_REC), (slot[0], inc)))
        self._commit(reads, writes, tok)
        return tok

    def wait_all(self, qname):
        q = self.q[qname]
        deps = []
        for st in self.keys.values():
            if st[0] is not None:
                deps.append(st[0])
            deps.extend(st[1].values())
        for slot in self.dma_sems + self.cc_sems:
            if slot[1] > 0:
                deps.append((slot[0], slot[1]))
        waits = self._waits(q, deps)
        q.entries.append((waits, None, None))

    def emit(self):
        nc = self.nc
        engs = {"pe": "tensor", "act": "scalar", "dve": "vector", "pool": "gpsimd", "sp": "sync"}
        with nc.Block() as block:
            for qname, attr in engs.items():
                q = self.q[qname]
                entries = q.entries
                q.entries = []

                def body(eng, entries=entries, qname=qname):
                    with (self.pre.pop(qname)(eng) if qname in self.pre else contextlib.nullcontext()):
                        for waits, fn, inc in entries:
                            for sem, val in waits:
                                eng.wait_ge(sem, val)
                            if fn is not None:
                                _replay(eng, fn).then_inc(inc[0], inc[1])

                getattr(block, attr)(body)


def build_program(S, DFF, NA, NB):
    L = NA + NB
    T = S // 4
    CH1 = min(512, T)
    NC1 = T // CH1
    CH3 = min(512, T)
    NC3 = T // CH3
    NQ = S // 512
    NKB = S // 128
    KF = DFF // 128
    NG = 3 * L + 2
    G1, GM, G2, GKV, GFIN = 0, L, 2 * L, 3 * L, 3 * L + 1

    nc = bass.Bass("TRN2", target_bir_lowering=False)
    din = lambda name, shape, dt=F32: nc.dram_tensor(name, shape, dt, kind="ExternalInput").ap()
    xT = din("xT", [D, T])
    memT = din("memT", [D, NMEM])
    gains = din("gains", [128, NG * KD])
    w_qm = din("w_qm", [L, D, 256])
    w_mkv = din("w_mkv", [L, D, 512])
    w_o = din("w_o", [L, 768, D])
    w1 = din("w1", [L, D, DFF])
    w2 = din("w2", [L, DFF, D])
    wa_own = din("wa_own", [max(NA, 1), D, 384])
    wb_own = din("wb_own", [max(NB, 1), D, 128])
    wkvs_own = din("wkvs_own", [D, 258])
    bf_own = din("bf_own", [2, 1])
    roff = din("roff", [1, 1], I32)
    c_trineg = din("c_trineg", [128, 128], BF16)
    c_negones = din("c_negones", [128, 128], BF16)
    c_ones = din("c_ones", [128, 128], BF16)
    c_msb = din("c_msb", [128, 4 * 512], BF16)
    c_mfox = din("c_mfox", [128, 4 * 512], BF16)
    c_swap = din("c_swap", [128, 128], F32)
    yT = nc.dram_tensor("yT", [D, T], F32, kind="ExternalOutput").ap()

    hT = nc.dram_tensor("hT", [D, T], F32).ap()
    xg_in = nc.dram_tensor("xg_in", [D, T], BF16).ap()
    xg = nc.dram_tensor("xg", [4 * D, T], BF16).ap()
    memo = nc.dram_tensor("memo", [256, T], BF16).ap()
    mg_in = nc.dram_tensor("mg_in", [4 * 128, T], BF16).ap()
    mg = nc.dram_tensor("mg", [4 * 512, T], BF16).ap()
    mgl = nc.dram_tensor("mgl", [4 * 128, T], BF16).ap()
    ksave = nc.dram_tensor("ksave", [2 * 128, S], BF16).ap()
    vsave = nc.dram_tensor("vsave", [2 * 128, NKB * 128], BF16).ap()

    kview = lambda ap2d: ap2d.rearrange("(k p) n -> p k n", p=128)

    with contextlib.ExitStack() as st:
        P = Prog(nc)
        P.setup(st)
        sb = lambda name, shape, dt, stack=st: stack.enter_context(nc.sbuf_tensor(name, shape, dt))
        PSA = st.enter_context(nc.psum_tensor("psall", [128, 8 * 512], F32))
        PS = [PSA[:, i * 512:(i + 1) * 512] for i in range(8)]
        psk = lambda i: ("ps", i)

        dyn = {}

        @contextlib.contextmanager
        def sp_pre(eng):
            dyn["n"] = dyn.get("n", 0) + 1
            with eng.register("roff_reg%d" % dyn["n"]) as reg:
                eng.reg_load(reg, roff[0:1, 0:1])
                dyn["off"] = eng.snap(reg, min_val=0, max_val=1536)
                yield

        ones = sb("ones", [128, 128], BF16)
        swp = sb("swp", [128, 128], F32)
        gn = sb("gn", [128, NG * KD], F32)
        for dst, src, nm in ((ones, c_ones, "ones"), (swp, c_swap, "swp"), (gn, gains, "gn")):
            P.dma("sp", lambda e, dst=dst, src=src: e.dma_start(out=dst[:], in_=src), writes=[nm])
        gcol = lambda g, k: gn[:, g * KD + k:g * KD + k + 1]

        def rms_rstd(X, n, key, tmp_sq, rstd, ps_i, sqkey="sq", rkey="rstd"):
            P.op("act", lambda e: e.activation(out=tmp_sq[:, 0:KD, 0:n], in_=X[:, :, 0:n], func=AF.Square), reads=[key, "ones"], writes=[sqkey])
            for k in range(KD):
                P.op("pe", lambda e, k=k: e.matmul(PS[ps_i][:, 0:n], ones[:], tmp_sq[:, k, 0:n], start=(k == 0), stop=(k == KD - 1)),
                     reads=[sqkey, "ones"], writes=[psk(ps_i)])
            P.op("act", lambda e: e.activation(out=rstd[:, 0:n], in_=PS[ps_i][:, 0:n], func=AF.Sqrt, scale=1.0 / D, bias=EPS), reads=[psk(ps_i)], writes=[rkey])
            P.op("dve", lambda e: e.reciprocal(out=rstd[:, 0:n], in_=rstd[:, 0:n]), reads=[rkey], writes=[rkey])

        def softmax_av(z_mm, nblk, mask_fn, vaug_fn, vkey, base, out_tile, out_key, wk, n=512):
            o_ps = 6
            for bi in range(nblk):
                zi = bi % 2
                z_mm(bi, zi)
                pt = wk["p"][bi % 2]
                m = mask_fn(bi)
                if m is not None:
                    zt = wk["zt"]
                    P.op("dve", lambda e, zi=zi, m=m: e.tensor_tensor(out=zt[:, 0:n], in0=PS[zi][:, 0:n], in1=m, op=ALU.add), reads=[psk(zi), "mfox"], writes=["zt"])
                    P.op("act", lambda e, pt=pt: e.activation(out=pt[:, 0:n], in_=zt[:, 0:n], func=AF.Exp), reads=["zt"], writes=[("p", bi % 2)])
                else:
                    P.op("act", lambda e, zi=zi, pt=pt: e.activation(out=pt[:, 0:n], in_=PS[zi][:, 0:n], func=AF.Exp), reads=[psk(zi)], writes=[("p", bi % 2)])
                P.op("pe", lambda e, bi=bi, pt=pt: e.matmul(PS[o_ps][:, 0:n], vaug_fn(bi), pt[:, 0:n], start=(bi == 0), stop=(bi == nblk - 1)),
                     reads=[("p", bi % 2), vkey], writes=[psk(o_ps)])
            osb = wk["osb"]
            P.op("act", lambda e: e.activation(out=osb[:, 0:n], in_=PS[o_ps][:, 0:n], func=AF.Copy), reads=[psk(o_ps)], writes=["osb"])
            P.op("pe", lambda e: e.matmul(PS[7][:, 0:n], swp[:], osb[:, 0:n], start=True, stop=True), reads=["osb", "swp"], writes=[psk(7)])
            rd = wk["rden"]
            P.op("dve", lambda e: e.reciprocal(out=rd[base:base + 64, 0:n], in_=PS[7][base:base + 64, 0:n]), reads=[psk(7)], writes=["rden"])
            P.op("dve", lambda e: e.tensor_tensor(out=out_tile[base:base + 64, 0:n], in0=osb[base:base + 64, 0:n], in1=rd[base:base + 64, 0:n], op=ALU.mult),
                 reads=["osb", "rden"], writes=[out_key])

        stg_n = [0]

        def load_w(stg, dst3, src2d, K, N, key):
            SW = stg[0].shape[1]
            kper = max(1, SW // N)
            ncol = min(N, SW)
            for k0 in range(0, K, kper):
                kk = min(kper, K - k0)
                for n0 in range(0, N, ncol):
                    i = stg_n[0] % 3
                    stg_n[0] += 1
                    sview = stg[i][:, 0:kk * ncol].rearrange("p (k n) -> p k n", n=ncol)
                    P.dma("sp", lambda e: e.dma_start(out=sview, in_=src2d[k0 * 128:(k0 + kk) * 128, n0:n0 + ncol].rearrange("(k p) n -> p k n", p=128)), writes=[("stg", i)])
                    eng = ("dve", "pool", "act")[i]
                    if eng == "act":
                        P.op("act", lambda e: e.activation(out=dst3[:, k0:k0 + kk, n0:n0 + ncol], in_=sview, func=AF.Copy), reads=[("stg", i)], writes=[key])
                    else:
                        P.op(eng, lambda e: e.tensor_copy(out=dst3[:, k0:k0 + kk, n0:n0 + ncol], in_=sview), reads=[("stg", i)], writes=[key])

        for l in range(L):
            isA = l < NA
            h_src = xT if l == 0 else hT
            with contextlib.ExitStack() as ph:
                t = lambda name, shape, dt: sb("p1_%d_%s" % (l, name), shape, dt, ph)
                wqm = t("wqm", [128, KD, 256], BF16)
                wmkv = t("wmkv", [128, KD, 512], BF16)
                stg = [t("stg%d" % i, [128, 1024], F32) for i in range(3)]
                load_w(stg, wqm, w_qm[l], KD, 256, "wqm")
                load_w(stg, wmkv, w_mkv[l], KD, 512, "wmkv")
                hc = t("hc", [128, KD, 512], F32)
                sq = t("sq", [128, KD, 512], BF16)
                rstd = t("rstd", [128, 512], F32)
                hn = t("hn", [128, KD, 512], BF16)
                mkT = t("mkT", [128, 2, NMEM], BF16)
                mvaug = t("mvaug", [128, 4, 2, 128], BF16)
                qm = t("qm", [128, 2, 512], BF16)
                mo = t("mo", [128, 2, 512], BF16)
                wk = {"p": [t("pA", [128, 512], BF16), t("pB", [128, 512], BF16)], "osb": t("osb", [128, 512], F32), "rden": t("rden", [128, 512], F32)}
                P.dma("sp", lambda e: e.dma_start(out=hc[:, :, 0:NMEM], in_=kview(memT)), writes=["hc"])
                rms_rstd(hc, NMEM, "hc", sq, rstd, 2)
                for k in range(KD):
                    P.op("dve", lambda e, k=k: e.scalar_tensor_tensor(out=hn[:, k, 0:NMEM], in0=hc[:, k, 0:NMEM], scalar=gcol(GM + l, k), in1=rstd[:, 0:NMEM], op0=ALU.mult, op1=ALU.mult),
                         reads=["hc", "rstd", "gn"], writes=["hn"])
                for m in range(2):
                    for k in range(KD):
                        P.op("pe", lambda e, m=m, k=k: e.matmul(PS[m][:, 0:NMEM], wmkv[:, k, m * 128:(m + 1) * 128], hn[:, k, 0:NMEM], start=(k == 0), stop=(k == KD - 1)),
                             reads=["hn", "wmkv"], writes=[psk(m)])
                    P.op("act", lambda e, m=m: e.activation(out=mkT[:, m, :], in_=PS[m][:, 0:NMEM], func=AF.Copy), reads=[psk(m)], writes=["mkT"])
                P.op("pool", lambda e: e.memset(mvaug[:], 1.0), writes=["mvaug"])
                for mb in range(2):
                    for k in range(KD):
                        P.op("pe", lambda e, mb=mb, k=k: e.matmul(PS[2 + mb][:, 0:256], hn[:, k, mb * 128:(mb + 1) * 128], wmkv[:, k, 256:512], start=(k == 0), stop=(k == KD - 1)),
                             reads=["hn", "wmkv"], writes=[psk(2 + mb)])
                    for j in range(4):
                        c0 = (j % 2) * 64
                        P.op("act", lambda e, mb=mb, j=j, c0=c0: e.activation(out=mvaug[:, j, mb, c0:c0 + 64], in_=PS[2 + mb][:, j * 64:(j + 1) * 64], func=AF.Copy),
                             reads=[psk(2 + mb)], writes=["mvaug"])
                xha = t("xha", [128, KD, T], BF16)
                for c in range(NC1):
                    cs = slice(c * CH1, (c + 1) * CH1)
                    n = CH1
                    P.dma("sp", lambda e, cs=cs: e.dma_start(out=hc[:, :, 0:n], in_=kview(h_src)[:, :, cs]), reads=["hT"], writes=["hc"])
                    rms_rstd(hc, n, "hc", sq, rstd, 2)
                    for k in range(KD):
                        P.op("dve", lambda e, k=k: e.tensor_tensor(out=xha[:, k, cs], in0=hc[:, k, 0:n], in1=rstd[:, 0:n], op=ALU.mult), reads=["hc", "rstd"], writes=[("xh", c)])
                    P.dma("sp", lambda e, cs=cs: e.dma_start(out=kview(xg_in)[:, :, cs], in_=xha[:, :, cs]), reads=[("xh", c)], writes=["xg_in"])
                for k in range(KD):
                    P.dma("pool", lambda e, k=k: e.collective_compute("AllGather", ALU.bypass, replica_groups=GROUPS, ins=[xg_in[k * 128:(k + 1) * 128, :]], outs=[xg[k * 512:(k + 1) * 512, :]]),
                          reads=["xg_in"], writes=["xg"], inc=1, fresh=True)
                for c in range(NC1):
                    cs = slice(c * CH1, (c + 1) * CH1)
                    n = CH1
                    for k in range(KD):
                        P.op("act", lambda e, k=k: e.activation(out=hn[:, k, 0:n], in_=xha[:, k, cs], func=AF.Copy, scale=gcol(G1 + l, k)),
                             reads=[("xh", c), "gn"], writes=["hn"])
                    for m in range(2):
                        for k in range(KD):
                            P.op("pe", lambda e, m=m, k=k: e.matmul(PS[m][:, 0:n], wqm[:, k, m * 128:(m + 1) * 128], hn[:, k, 0:n], start=(k == 0), stop=(k == KD - 1)),
                                 reads=["hn", "wqm"], writes=[psk(m)])
                        P.op("act", lambda e, m=m: e.activation(out=qm[:, m, 0:n], in_=PS[m][:, 0:n], func=AF.Copy, scale=0.125), reads=[psk(m)], writes=["qm"])
                    for j in range(4):
                        base = (j % 2) * 64
                        jj = j // 2

                        def z_mm(bi, zi, base=base, jj=jj):
                            P.op("pe", lambda e: e.matmul(PS[zi][:, 0:n], mkT[base:base + 64, jj, bi * 128:(bi + 1) * 128], qm[base:base + 64, jj, 0:n], start=True, stop=True),
                                 reads=["mkT", "qm"], writes=[psk(zi)])

                        softmax_av(z_mm, 2, lambda bi: None, lambda bi, j=j: mvaug[:, j, bi, :], "mvaug", base, mo[:, jj, :], "mo", wk, n=n)
                    P.dma("sp", lambda e, cs=cs: e.dma_start(out=memo.rearrange("(k p) n -> p k n", p=128)[:, :, cs], in_=mo[:, :, 0:n]), reads=["mo"], writes=["memo"])
                P.wait_all("sp")
                P.emit()

            with contextlib.ExitStack() as ph:
                t = lambda name, shape, dt: sb("p2_%d_%s" % (l, name), shape, dt, ph)
                xc = [t("xcA", [128, KD, 512], BF16), t("xcB", [128, KD, 512], BF16)]
                wst = t("wst", [128, KD, 384], F32)
                mixc = [t("mix%d" % i, [128, 512], BF16) for i in range(4)]

                def load_own_w(dst, src_ap, ncol, gidx, key, c0=0):
                    P.dma("sp", lambda e: e.dma_start(out=wst[:, :, 0:ncol], in_=kview(src_ap)), writes=["wst"])
                    for k in range(KD):
                        P.op("dve", lambda e, k=k: e.tensor_scalar(out=dst[:, k, c0:c0 + ncol], in0=wst[:, k, 0:ncol], scalar1=gcol(gidx, k), scalar2=None, op0=ALU.mult),
                             reads=["wst", "gn"], writes=[key])

                def load_xc(tc, slot=None):
                    slot = tc % 2 if slot is None else slot % 2
                    rk, cc = divmod(tc * 512, T)
                    buf = xc[slot]
                    P.dma("sp", lambda e: e.dma_start(out=buf[:], in_=xg.rearrange("(k r p) t -> r p k t", k=KD, r=4)[rk][:, :, cc:cc + 512]), reads=["xg"], writes=[("xc", slot)])
                    return buf, ("xc", slot)

                def mix_out(mix_tile, mkey, base, qc):
                    rdst, cc = divmod(qc * 512, T)
                    P.dma("sp", lambda e: e.dma_start(out=mg_in[rdst * 128 + base:rdst * 128 + base + 64, cc:cc + 512], in_=mix_tile[base:base + 64, :]),
                          reads=[mkey], writes=["mg_in"])

                def load_const(name, shape, dt, src):
                    tl = t(name, shape, dt)
                    P.dma("sp", lambda e: e.dma_start(out=tl[:], in_=src), writes=[name])
                    return tl

                if isA:
                    trineg = load_const("trineg", [128, 128], BF16, c_trineg)
                    negones = load_const("negones", [128, 128], BF16, c_negones)
                    msb = load_const("msb", [128, 4 * 512], BF16, c_msb)
                    wown = t("wown", [128, KD, 384], BF16)
                    load_own_w(wown, wa_own[l], 384, G1 + l, "wown")
                    qT2 = t("qT2", [128, S], BF16)
                    kT2 = t("kT2", [128, S], BF16)
                    v2 = t("v2", [128, NKB, 128], BF16)
                    E2 = [t("e%d" % i, [128, 1024], F32) for i in range(3)]
                    SP2 = [t("sp%d" % i, [128, 1024], BF16) for i in range(3)]
                    X2 = [t("x%d" % i, [128, 1024], F32) for i in range(2)]
                    W2 = [t("w%d" % i, [128, 1024], BF16) for i in range(2)]
                    LB2 = [t("lb%d" % i, [128, 1024], BF16) for i in range(2)]
                    nxt = load_xc(0)
                    for tc in range(NQ):
                        buf, bk = nxt
                        if tc + 1 < NQ:
                            nxt = load_xc(tc + 1)
                        ts_ = slice(tc * 512, (tc + 1) * 512)
                        for which, dst, dkey, sc in ((0, qT2, "qT2", 0.125), (1, kT2, "kT2", 1.0)):
                            pi = which
                            for k in range(KD):
                                P.op("pe", lambda e, k=k, which=which, pi=pi: e.matmul(PS[pi][:, :], wown[:, k, which * 128:(which + 1) * 128], buf[:, k, :], start=(k == 0), stop=(k == KD - 1)),
                                     reads=[bk, "wown"], writes=[psk(pi)])
                            P.op("act", lambda e, dst=dst, pi=pi, sc=sc: e.activation(out=dst[:, ts_], in_=PS[pi][:, :], func=AF.Copy, scale=sc), reads=[psk(pi)], writes=[dkey])
                        for tb in range(4):
                            pi = 2 + tb % 2
                            for k in range(KD):
                                P.op("pe", lambda e, k=k, tb=tb, pi=pi: e.matmul(PS[pi][:, 0:128], buf[:, k, tb * 128:(tb + 1) * 128], wown[:, k, 256:384], start=(k == 0), stop=(k == KD - 1)),
                                     reads=[bk, "wown"], writes=[psk(pi)])
                            P.op("dve", lambda e, tb=tb, pi=pi: e.tensor_copy(out=v2[:, tc * 4 + tb, :], in_=PS[pi][:, 0:128]), reads=[psk(pi)], writes=["v2"])
                    ptiles = [(qc, i, 4 * qc + 4) for qc in range(NQ) for i in range(4 * qc + 4)]
                    NPT = len(ptiles)
                    hsl = lambda hh: slice(hh * 512, (hh + 1) * 512)

                    def pA_pe(s_):
                        qc, i, nb = ptiles[s_]
                        kb, zb = nb - 1 - i, (0 if s_ % 2 == 0 else 6)
                        qs = slice(qc * 512, (qc + 1) * 512)
                        for hh in range(2):
                            base = hh * 64
                            P.op("pe", lambda e: e.matmul(PS[zb + hh][:, :], kT2[base:base + 64, kb * 128:(kb + 1) * 128], qT2[base:base + 64, qs], start=True, stop=True),
                                 reads=["kT2", "qT2"], writes=[psk(zb + hh)])

                    def pA(s_):
                        qc, i, nb = ptiles[s_]
                        kb, zb, eb = nb - 1 - i, (0 if s_ % 2 == 0 else 6), s_ % 3
                        P.op("act", lambda e: e.activation(out=E2[eb][:], in_=PSA[:, zb * 512:(zb + 2) * 512], func=AF.Exp), reads=[psk(zb), psk(zb + 1)], writes=[("E2", eb)])
                        P.op("act", lambda e: e.activation(out=SP2[eb][:], in_=E2[eb][:], func=AF.Ln, bias=1.0), reads=[("E2", eb)], writes=[("SP2", eb)])
                        if kb >= 4 * qc:
                            rel = kb - 4 * qc
                            mk = msb[:, rel * 512:(rel + 1) * 512]
                            for hh in range(2):
                                P.op("pool", lambda e: e.tensor_tensor(out=SP2[eb][:, hsl(hh)], in0=SP2[eb][:, hsl(hh)], in1=mk, op=ALU.mult), reads=[("SP2", eb), "msb"], writes=[("SP2", eb)])
                                P.op("pool", lambda e: e.tensor_tensor(out=E2[eb][:, hsl(hh)], in0=E2[eb][:, hsl(hh)], in1=mk, op=ALU.mult), reads=[("E2", eb), "msb"], writes=[("E2", eb)])

                    def pB(s_):
                        qc, i, nb = ptiles[s_]
                        eb = s_ % 3
                        for hh in range(2):
                            P.op("pe", lambda e: e.matmul(PS[2 + hh][:, :], trineg[:], SP2[eb][:, hsl(hh)], start=True, stop=(i == 0)), reads=[("SP2", eb), "trineg"], writes=[psk(2 + hh)])
                            if i > 0:
                                P.op("pe", lambda e: e.matmul(PS[2 + hh][:, :], negones[:], LB2[(i - 1) % 2][:, hsl(hh)], start=False, stop=True),
                                     reads=[("LB2", (i - 1) % 2), "negones"], writes=[psk(2 + hh)])
                        if i < nb - 1:
                            if i == 0:
                                P.op("pool", lambda e: e.tensor_copy(out=LB2[0][:], in_=SP2[eb][:]), reads=[("SP2", eb)], writes=[("LB2", 0)])
                            else:
                                P.op("pool", lambda e: e.tensor_tensor(out=LB2[i % 2][:], in0=LB2[(i - 1) % 2][:], in1=SP2[eb][:], op=ALU.add),
                                     reads=[("SP2", eb), ("LB2", (i - 1) % 2)], writes=[("LB2", i % 2)])
                        P.op("act", lambda e: e.activation(out=X2[s_ % 2][:], in_=PSA[:, 2 * 512:4 * 512], func=AF.Exp), reads=[psk(2), psk(3)], writes=[("X2", s_ % 2)])

                    def pC(s_):
                        qc, i, nb = ptiles[s_]
                        kb, eb = nb - 1 - i, s_ % 3
                        o_ps = 4 + qc % 2
                        P.op("dve", lambda e: e.tensor_tensor(out=W2[s_ % 2][:], in0=E2[eb][:], in1=X2[s_ % 2][:], op=ALU.mult), reads=[("E2", eb), ("X2", s_ % 2)], writes=[("W2", s_ % 2)])
                        for hh in range(2):
                            base = hh * 64
                            P.op("pe", lambda e: e.matmul(PS[o_ps][base:base + 64, :], v2[:, kb, base:base + 64], W2[s_ % 2][:, hsl(hh)], start=(i == 0), stop=(i == nb - 1)),
                                 reads=[("W2", s_ % 2), "v2"], writes=[(psk(o_ps), hh)])
                        if i == nb - 1:
                            mt, mkey = mixc[qc % 4], ("mix", qc % 4)
                            P.op("dve", lambda e: e.tensor_copy(out=mt[:, :], in_=PS[o_ps][:, :]), reads=[(psk(o_ps), 0), (psk(o_ps), 1)], writes=[mkey])
                            for hh in range(2):
                                mix_out(mt, mkey, hh * 64, qc)

                    pA_pe(0)
                    for s_ in range(NPT + 2):
                        if s_ + 1 < NPT:
                            pA_pe(s_ + 1)
                        if s_ < NPT:
                            pA(s_)
                        if 0 <= s_ - 1 < NPT:
                            pB(s_ - 1)
                        if 0 <= s_ - 2 < NPT:
                            pC(s_ - 2)
                else:
                    lb_ = l - NA
                    first_b = (lb_ == 0)
                    mfox = load_const("mfox", [128, 4 * 512], BF16, c_mfox)
                    wq = t("wq", [128, KD, 128], BF16)
                    load_own_w(wq, wb_own[lb_], 128, G1 + l, "wq")
                    if first_b:
                        wkv = t("wkv", [128, KD, 258], BF16)
                        load_own_w(wkv, wkvs_own, 258, GKV, "wkv")
                        bfo = t("bfo", [2, 1], F32)
                        P.dma("sp", lambda e: e.dma_start(out=bfo[:], in_=bf_own), writes=["bfo"])
                        P.op("dve", lambda e: e.tensor_scalar(out=bfo[:], in0=bfo[:], scalar1=-1.0, scalar2=None, op0=ALU.mult), reads=["bfo"], writes=["bfo"])
                        one2 = t("one2", [2, 512], F32)
                        P.op("pool", lambda e: e.memset(one2[:], 1.0), writes=["one2"])
                        fe = t("fe", [2, 512], F32)
                        cc_ = [t("ccA", [2, 512], F32), t("ccB", [2, 512], F32)]
                        r1 = t("r1", [2, 512], F32)
                        spl = t("spl", [2, 6, 512], BF16)
                    qaug = t("qaug", [128, S], BF16)
                    kaug = t("kaug", [128, S], BF16)
                    vaug = t("vaug", [128, NKB, 128], BF16)
                    wk = {"p": [t("pA", [128, 512], BF16), t("pB", [128, 512], BF16)], "osb": t("osb", [128, 512], F32), "rden": t("rden", [128, 512], F32),
                          "zt": t("zt", [128, 512], F32), "zt2": [t("ztA", [128, 512], F32), t("ztB", [128, 512], F32)],
                          "p3": [t("p3_%d" % i, [128, 512], BF16) for i in range(4)]}
                    for hh in range(2):
                        base = hh * 64
                        P.op("pool", lambda e: e.memset(qaug[64:128, :], 1.0), reads=[], writes=["qaug"])
                        if first_b:
                            P.op("pool", lambda e: e.memset(kaug[64:128, :], 1.0), writes=["kaug"])
                            P.op("pool", lambda e: e.memset(vaug[:], 1.0), writes=["vaug"])
                        else:
                            P.dma("sp", lambda e, hh=hh: e.dma_start(out=kaug[:], in_=ksave[hh * 128:(hh + 1) * 128, :]), reads=["ksave"], writes=["kaug"])
                            P.dma("sp", lambda e, hh=hh: e.dma_start(out=vaug[:], in_=vsave[hh * 128:(hh + 1) * 128, :].rearrange("p (b c) -> p b c", c=128)), reads=["vsave"], writes=["vaug"])
                        nxt = load_xc(0, hh * NQ)
                        for tc in range(NQ):
                            buf, bk = nxt
                            if tc + 1 < NQ:
                                nxt = load_xc(tc + 1, tc + 1 + hh * NQ)
                            ts_ = slice(tc * 512, (tc + 1) * 512)
                            for k in range(KD):
                                P.op("pe", lambda e, k=k: e.matmul(PS[0][0:64, :], wq[:, k, base:base + 64], buf[:, k, :], start=(k == 0), stop=(k == KD - 1)),
                                     reads=[bk, "wq"], writes=[psk(0)])
                            P.op("act", lambda e: e.activation(out=qaug[0:64, ts_], in_=PS[0][0:64, :], func=AF.Copy, scale=0.125), reads=[psk(0)], writes=["qaug"])
                            if first_b:
                                for k in range(KD):
                                    P.op("pe", lambda e, k=k: e.matmul(PS[1][0:64, :], wkv[:, k, base:base + 64], buf[:, k, :], start=(k == 0), stop=(k == KD - 1)),
                                         reads=[bk, "wkv"], writes=[psk(1)])
                                P.op("act", lambda e: e.activation(out=kaug[0:64, ts_], in_=PS[1][0:64, :], func=AF.Copy), reads=[psk(1)], writes=["kaug"])
                                for tb in range(4):
                                    pi = 2 + tb % 2
                                    for k in range(KD):
                                        P.op("pe", lambda e, k=k, tb=tb, pi=pi: e.matmul(PS[pi][:, 0:64], buf[:, k, tb * 128:(tb + 1) * 128], wkv[:, k, 128 + base:128 + base + 64], start=(k == 0), stop=(k == KD - 1)),
                                             reads=[bk, "wkv"], writes=[psk(pi)])
                                    P.op("dve", lambda e, tb=tb, pi=pi: e.tensor_copy(out=vaug[:, tc * 4 + tb, base:base + 64], in_=PS[pi][:, 0:64]), reads=[psk(pi)], writes=["vaug"])
                                for k in range(KD):
                                    P.op("pe", lambda e, k=k: e.matmul(PS[4][0:2, :], wkv[:, k, 256:258], buf[:, k, :], start=(k == 0), stop=(k == KD - 1)),
                                         reads=[bk, "wkv"], writes=[psk(4)])
                                P.op("act", lambda e: e.activation(out=fe[:], in_=PS[4][0:2, :], func=AF.Exp, scale=-1.0, bias=bfo[:, 0:1]), reads=[psk(4), "bfo"], writes=["fe"])
                                P.op("act", lambda e: e.activation(out=fe[:], in_=fe[:], func=AF.Ln, bias=1.0), reads=["fe"], writes=["fe"])
                                P.op("dve", lambda e: e.tensor_scalar(out=fe[:], in0=fe[:], scalar1=-1.0, scalar2=None, op0=ALU.mult), reads=["fe"], writes=["fe"])
                                cur, prev = cc_[tc % 2], cc_[(tc + 1) % 2]
                                init = 0.0 if tc == 0 else prev[:, 511:512]
                                P.op("dve", lambda e, cur=cur, init=init: e.tensor_tensor_scan(out=cur[:], data0=one2[:], data1=fe[:], initial=init, op0=ALU.mult, op1=ALU.add),
                                     reads=["fe", "one2", ("cc", (tc + 1) % 2)], writes=[("cc", tc % 2)])
                                P.op("dve", lambda e, cur=cur: e.tensor_copy(out=spl[:, 0, :], in_=cur[:]), reads=[("cc", tc % 2)], writes=["spl"])
                                P.op("dve", lambda e, cur=cur: e.tensor_tensor(out=r1[:], in0=cur[:], in1=spl[:, 0, :], op=ALU.subtract), reads=[("cc", tc % 2), "spl"], writes=["r1"])
                                P.op("dve", lambda e: e.tensor_copy(out=spl[:, 1, :], in_=r1[:]), reads=["r1"], writes=["spl"])
                                P.op("dve", lambda e: e.tensor_tensor(out=r1[:], in0=r1[:], in1=spl[:, 1, :], op=ALU.subtract), reads=["r1", "spl"], writes=["r1"])
                                P.op("dve", lambda e: e.tensor_copy(out=spl[:, 2, :], in_=r1[:]), reads=["r1"], writes=["spl"])
                                P.op("dve", lambda e: e.tensor_scalar(out=spl[:, 3:6, :], in0=spl[:, 0:3, :], scalar1=-1.0, scalar2=None, op0=ALU.mult), reads=["spl"], writes=["spl"])
                                for j3 in range(3):
                                    P.dma("sp", lambda e, hh=hh, j3=j3: e.dma_start(out=qaug[64 + j3:65 + j3, ts_], in_=spl[hh:hh + 1, j3, :]), reads=["spl"], writes=["qaug"])
                                    P.dma("sp", lambda e, hh=hh, j3=j3: e.dma_start(out=kaug[67 + j3:68 + j3, ts_], in_=spl[hh:hh + 1, 3 + j3, :]), reads=["spl"], writes=["kaug"])
                                    P.dma("sp", lambda e, hh=hh, j3=j3: e.dma_start(out=kaug[70 + j3:71 + j3, ts_], in_=spl[hh:hh + 1, j3, :]), reads=["spl"], writes=["kaug"])
                        if first_b and NB > 1:
                            P.dma("sp", lambda e, hh=hh: e.dma_start(out=ksave[hh * 128:(hh + 1) * 128, :], in_=kaug[:]), reads=["kaug"], writes=["ksave"])
                            P.dma("sp", lambda e, hh=hh: e.dma_start(out=vsave[hh * 128:(hh + 1) * 128, :].rearrange("p (b c) -> p b c", c=128), in_=vaug[:]), reads=["vaug"], writes=["vsave"])
                        if not first_b:
                            P.dma("sp", lambda e: e.dma_start(out=qaug[64:67, :], in_=kaug[70:73, :]), reads=["kaug"], writes=["qaug"])
                        ftiles = [(qc, bi, 4 * qc + 4) for qc in range(NQ) for bi in range(4 * qc + 4)]
                        NFT = len(ftiles)
                        PB = wk["p3"]

                        def fA_pe(g):
                            qc, bi, nb = ftiles[g]
                            zi = g % 3
                            qs = slice(qc * 512, (qc + 1) * 512)
                            P.op("pe", lambda e: e.matmul(PS[zi][:, :], kaug[0:70, bi * 128:(bi + 1) * 128], qaug[0:70, qs], start=True, stop=True),
                                 reads=["kaug", "qaug"], writes=[psk(zi)])

                        def fA(g, base=base):
                            qc, bi, nb = ftiles[g]
                            zi, pb = g % 3, g % 4
                            if bi >= 4 * qc:
                                rel = bi - 4 * qc
                                m = mfox[:, rel * 512:(rel + 1) * 512]
                                zt = wk["zt2"][g % 2]
                                P.op("dve", lambda e: e.tensor_tensor(out=zt[:], in0=PS[zi][:, :], in1=m, op=ALU.add), reads=[psk(zi), "mfox"], writes=[("zt", g % 2)])
                                P.op("act", lambda e: e.activation(out=PB[pb][:], in_=zt[:], func=AF.Exp), reads=[("zt", g % 2)], writes=[("p3", pb)])
                            else:
                                P.op("act", lambda e: e.activation(out=PB[pb][:], in_=PS[zi][:, :], func=AF.Exp), reads=[psk(zi)], writes=[("p3", pb)])

                        def fB(g, base=base):
                            qc, bi, nb = ftiles[g]
                            o_ps = 4 + qc % 2
                            P.op("pe", lambda e: e.matmul(PS[o_ps][:, :], vaug[:, bi, :], PB[g % 4][:], start=(bi == 0), stop=(bi == nb - 1)),
                                 reads=[("p3", g % 4), "vaug"], writes=[psk(o_ps)])

                        def fEpi(g, base=base):
                            qc, bi, nb = ftiles[g]
                            o_ps = 4 + qc % 2
                            osb, rd = wk["osb"], wk["rden"]
                            mt, mkey = mixc[qc % 2], ("mix", qc % 2)
                            P.op("act", lambda e: e.activation(out=osb[:], in_=PS[o_ps][:, :], func=AF.Copy), reads=[psk(o_ps)], writes=["osb"])
                            P.op("pe", lambda e: e.matmul(PS[7][:, :], swp[:], osb[:], start=True, stop=True), reads=["osb", "swp"], writes=[psk(7)])
                            P.op("dve", lambda e: e.reciprocal(out=rd[base:base + 64, :], in_=PS[7][base:base + 64, :]), reads=[psk(7)], writes=["rden"])
                            P.op("dve", lambda e: e.tensor_tensor(out=mt[base:base + 64, :], in0=osb[base:base + 64, :], in1=rd[base:base + 64, :], op=ALU.mult),
                                 reads=["osb", "rden"], writes=[mkey])
                            mix_out(mt, mkey, base, qc)

                        fA_pe(0)
                        for s_ in range(NFT + 4):
                            if s_ + 1 < NFT:
                                fA_pe(s_ + 1)
                            if s_ < NFT:
                                fA(s_)
                            if 0 <= s_ - 2 < NFT:
                                fB(s_ - 2)
                            g2 = s_ - 4
                            if 0 <= g2 < NFT and ftiles[g2][1] == ftiles[g2][2] - 1:
                                fEpi(g2)
                P.wait_all("sp")
                P.emit()

            for r_ in range(4):
                P.dma("pool", lambda e, r_=r_: e.collective_compute("AllGather", ALU.bypass, replica_groups=GROUPS, ins=[mg_in[r_ * 128:(r_ + 1) * 128, :]], outs=[mg[r_ * 512:(r_ + 1) * 512, :]]),
                      reads=["mg_in"], writes=["mg"], inc=1, fresh=True)

            with contextlib.ExitStack() as ph:
                t = lambda name, shape, dt: sb("p3_%d_%s" % (l, name), shape, dt, ph)
                wo = t("wo", [128, 6, D], BF16)
                W1 = t("W1", [128, KD, DFF], BF16)
                W2 = t("W2", [128, KF, D], BF16)
                stg = [t("stg%d" % i, [128, 512], F32) for i in range(3)]
                load_w(stg, wo, w_o[l], 6, D, "wo")
                load_w(stg, W1, w1[l], KD, DFF, "W1")
                load_w(stg, W2, w2[l], KF, D, "W2")
                n = CH3
                mx = t("mx", [128, KD, n], BF16)
                hc = t("hc", [128, KD, n], F32)
                h1 = t("h1", [128, max(KF, KD), n], BF16)
                rl = [t("rlA", [128, n], F32), t("rlB", [128, n], F32)]
                rstd, RK = rl[0], ("rl", 0)
                P.dma("sp", lambda e: e.dma_start(out=mgl, in_=Late(lambda: mg[bass.ds(dyn["off"], 512), :])), reads=["mg"], writes=["mgl"])
                for c in range(NC3):
                    cs = slice(c * n, (c + 1) * n)
                    P.dma("sp", lambda e: e.dma_start(out=mx[:, 0:4, :], in_=kview(mgl)[:, :, cs]), reads=["mgl"], writes=["mx"])
                    P.dma("sp", lambda e: e.dma_start(out=mx[:, 4:6, :], in_=memo.rearrange("(k p) n -> p k n", p=128)[:, :, cs]), reads=["memo"], writes=["mx"])
                    P.dma("sp", lambda e: e.dma_start(out=hc[:], in_=kview(h_src)[:, :, cs]), reads=["hT"], writes=["hc"])
                    for m in range(KD):
                        pi = m % 2
                        for k in range(6):
                            P.op("pe", lambda e: e.matmul(PS[pi][:, 0:n], wo[:, k, m * 128:(m + 1) * 128], mx[:, k, :], start=(k == 0), stop=(k == 5)),
                                 reads=["mx", "wo"], writes=[psk(pi)])
                        P.op("dve", lambda e: e.tensor_tensor(out=hc[:, m, :], in0=hc[:, m, :], in1=PS[pi][:, 0:n], op=ALU.add), reads=[psk(pi), "hc"], writes=["hc"])
                    rms_rstd(hc, n, "hc", h1, rstd, 2, sqkey="h1", rkey=RK)
                    for k in range(KD):
                        P.op("dve", lambda e: e.scalar_tensor_tensor(out=mx[:, k, :], in0=hc[:, k, :], scalar=gcol(G2 + l, k), in1=rstd[:, 0:n], op0=ALU.mult, op1=ALU.mult),
                             reads=["hc", RK, "gn"], writes=["mx"])
                    for m in range(KF):
                        pi = 3 + m % 2
                        for k in range(KD):
                            P.op("pe", lambda e: e.matmul(PS[pi][:, 0:n], W1[:, k, m * 128:(m + 1) * 128], mx[:, k, :], start=(k == 0), stop=(k == KD - 1)),
                                 reads=["mx", "W1"], writes=[psk(pi)])
                        r = rl[m % 2]
                        P.op("act", lambda e: e.activation(out=r[:], in_=PS[pi][:, 0:n], func=AF.Relu), reads=[psk(pi)], writes=[("rl", m % 2)])
                        P.op("pool" if m % 2 else "dve", lambda e: e.tensor_tensor(out=h1[:, m, :], in0=r[:], in1=r[:], op=ALU.mult), reads=[("rl", m % 2)], writes=["h1"])
                    last = (l == L - 1)
                    for m in range(KD):
                        pi = 5 + m % 2
                        for k in range(KF):
                            P.op("pe", lambda e: e.matmul(PS[pi][:, 0:n], W2[:, k, m * 128:(m + 1) * 128], h1[:, k, :], start=(k == 0), stop=(k == KF - 1)),
                                 reads=["h1", "W2"], writes=[psk(pi)])
                        P.op("dve", lambda e: e.tensor_tensor(out=hc[:, m, :], in0=hc[:, m, :], in1=PS[pi][:, 0:n], op=ALU.add), reads=[psk(pi), "hc"], writes=["hc"])
                    if not last:
                        P.dma("sp", lambda e: e.dma_start(out=kview(hT)[:, :, cs], in_=hc[:]), reads=["hc"], writes=["hT"])
                    else:
                        rms_rstd(hc, n, "hc", h1, rstd, 2, sqkey="h1", rkey=RK)
                        for k in range(KD):
                            P.op("dve", lambda e: e.scalar_tensor_tensor(out=hc[:, k, :], in0=hc[:, k, :], scalar=gcol(GFIN, k), in1=rstd[:, 0:n], op0=ALU.mult, op1=ALU.mult),
                                 reads=["hc", RK, "gn"], writes=["hc"])
                        P.dma("sp", lambda e: e.dma_start(out=kview(yT)[:, :, cs], in_=hc[:]), reads=["hc"], writes=["yT"])
                P.wait_all("sp")
                P.pre["sp"] = sp_pre
                P.emit()
    return nc


def _consts():
    bf = ml_dtypes.bfloat16
    j = np.arange(128)[:, None]
    s = np.arange(128)[None, :]
    trineg = np.where(j >= s, -1.0, 0.0).astype(bf)
    negones = np.full((128, 128), -1.0).astype(bf)
    ones = np.ones((128, 128)).astype(bf)
    p = np.arange(128)[:, None]
    tq = np.arange(512)[None, :]
    msb = np.concatenate([(r * 128 + p < tq) for r in range(4)], axis=1).astype(np.float32).astype(bf)
    mfox = (np.concatenate([(r * 128 + p <= tq) for r in range(4)], axis=1).astype(np.float32) - 1.0) * 30000.0
    mfox = mfox.astype(bf)
    swap = np.zeros((128, 128), np.float32)
    swap[(np.arange(128) + 64) % 128, np.arange(128)] = 1.0
    return dict(c_trineg=trineg, c_negones=negones, c_ones=ones, c_msb=msb, c_mfox=mfox, c_swap=swap)


def make_in_maps(inp, S, DFF, NA, NB):
    L = NA + NB
    T = S // 4
    f32 = np.float32
    G = np.concatenate([inp["norm1_g"], inp["mem_norm_g"], inp["norm2_g"], inp["kv_norm_g"][None], inp["final_norm_g"][None]], 0).astype(f32)
    NG = G.shape[0]
    gains = np.ascontiguousarray(G.reshape(NG, 8, 128).transpose(2, 0, 1).reshape(128, NG * 8))
    consts = _consts()
    w_in_a, w_in_b, wkv = inp["w_in_a"], inp["w_in_b"], inp["w_kv_shared"]
    qm_cols = lambda w, off: w[:, :, off:off + 256]
    w_qm = np.ascontiguousarray(np.concatenate([qm_cols(w_in_a, 1536), qm_cols(w_in_b, 512)], 0)) if NB > 0 else np.ascontiguousarray(qm_cols(w_in_a, 1536))
    maps = []
    for c in range(8):
        b, r = divmod(c, 4)
        hs = slice(r * 128, (r + 1) * 128)
        wa = np.concatenate([w_in_a[:, :, r * 128:(r + 1) * 128], w_in_a[:, :, 512 + r * 128:512 + (r + 1) * 128], w_in_a[:, :, 1024 + r * 128:1024 + (r + 1) * 128]], axis=2)
        wb = w_in_b[:, :, hs] if NB > 0 else np.zeros((1, D, 128), f32)
        wk = np.concatenate([wkv[:, hs], wkv[:, 512 + r * 128:512 + (r + 1) * 128], wkv[:, 1024 + 2 * r:1024 + 2 * r + 2]], axis=1)
        m = dict(
            xT=np.ascontiguousarray(inp["x"][b, r * T:(r + 1) * T, :].T),
            memT=np.ascontiguousarray(inp["mem"][b].T),
            gains=gains, w_qm=w_qm, w_mkv=inp["w_mem_kv"], w_o=inp["w_o"], w1=inp["w_mlp1"], w2=inp["w_mlp2"],
            wa_own=np.ascontiguousarray(wa), wb_own=np.ascontiguousarray(wb), wkvs_own=np.ascontiguousarray(wk),
            bf_own=np.ascontiguousarray(inp["b_f"][2 * r:2 * r + 2].reshape(2, 1)),
            roff=np.array([[r * 512]], np.int32),
        )
        m.update(consts)
        maps.append(m)
    return maps


def assemble(results, S):
    T = S // 4
    out = np.empty((2, S, D), np.float32)
    for c in range(8):
        b, r = divmod(c, 4)
        out[b, r * T:(r + 1) * T, :] = results[c]["yT"].T
    return out


def kernel(**inputs):
    inp = {k: np.asarray(v) for k, v in inputs.items()}
    S, DFF, NA, NB = 16384, 4096, 2, 2
    nc = build_program(S, DFF, NA, NB)
    maps = make_in_maps(inp, S, DFF, NA, NB)
    res = run_bass_kernel_spmd(nc, maps, core_ids=list(range(8)))
    return assemble(res.results, S)
```

```python
import contextlib
import numpy as np
import ml_dtypes
import concourse.bass as bass
import concourse.mybir as mybir
from concourse.bass_utils import run_bass_kernel_spmd

F32 = mybir.dt.float32
BF16 = mybir.dt.bfloat16
I32 = mybir.dt.int32
AF = mybir.ActivationFunctionType
ALU = mybir.AluOpType

D = 1024
KD = 8
NMEM = 256
EPS = 1e-6
GROUPS = [[0, 1, 2, 3], [4, 5, 6, 7]]


class _Q:
    def __init__(self, name, sem):
        self.name = name
        self.sem = sem
        self.count = 0
        self.entries = []
        self.waited = {}


class Late:
    def __init__(self, f):
        self.f = f


class _Rec:
    def __getattr__(self, name):
        return lambda *a, **kw: (name, a, kw)


_REC = _Rec()


def _replay(eng, call):
    name, a, kw = call
    a = [x.f() if isinstance(x, Late) else x for x in a]
    kw = {k: (v.f() if isinstance(v, Late) else v) for k, v in kw.items()}
    try:
        return getattr(eng, name)(*a, **kw)
    except Exception:
        print("REPLAY FAIL", name, [getattr(x, "shape", x) for x in a], {k: getattr(v, "shape", v) for k, v in kw.items()})
        raise


class Prog:
    def __init__(self, nc, n_dma_sems=24):
        self.nc = nc
        self.q = {}
        self.keys = {}
        self.dma_sems = []
        self.n_dma_sems = n_dma_sems
        self.dma_rr = 0
        self.no_same_engine_wait = {"pe"}
        self.pre = {}
        self.cc_sems = []

    def setup(self, stack):
        nc = self.nc
        self.stack = stack
        for name in ("pe", "act", "dve", "pool", "sp"):
            sem = stack.enter_context(nc.semaphore("s_" + name))
            self.q[name] = _Q(name, sem)
        for i in range(self.n_dma_sems):
            sem = stack.enter_context(nc.semaphore("s_dma%d" % i))
            self.dma_sems.append([sem, 0])

    def _deps(self, reads, writes):
        deps = []
        for k in reads:
            st = self.keys.get(k)
            if st and st[0] is not None:
                deps.append(st[0])
        for k in writes:
            st = self.keys.get(k)
            if st:
                if st[0] is not None:
                    deps.append(st[0])
                deps.extend(st[1].values())
        return deps

    def _commit(self, reads, writes, tok):
        for k in reads:
            st = self.keys.setdefault(k, [None, {}])
            st[1][id(tok[0])] = tok
        for k in writes:
            self.keys[k] = [tok, {}]

    def _waits(self, q, deps):
        need = {}
        for sem, val in deps:
            if sem is q.sem and q.name in self.no_same_engine_wait:
                continue
            if q.waited.get(id(sem), 0) >= val:
                continue
            if need.get(id(sem), (None, 0))[1] < val:
                need[id(sem)] = (sem, val)
        out = []
        for sem, val in need.values():
            q.waited[id(sem)] = val
            out.append((sem, val))
        return out

    def op(self, qname, fn, reads=(), writes=()):
        q = self.q[qname]
        waits = self._waits(q, self._deps(reads, writes))
        q.count += 1
        tok = (q.sem, q.count)
        q.entries.append((waits, fn(_REC), (q.sem, 1)))
        self._commit(reads, writes, tok)
        return tok

    def dma(self, qname, fn, reads=(), writes=(), inc=16, fresh=False):
        q = self.q[qname]
        if fresh:
            slot = [self.stack.enter_context(self.nc.semaphore("s_cc%d" % len(self.cc_sems))), 0]
            self.cc_sems.append(slot)
        else:
            slot = self.dma_sems[self.dma_rr % self.n_dma_sems]
            self.dma_rr += 1
        deps = self._deps(reads, writes)
        if slot[1] > 0:
            deps.append((slot[0], slot[1]))
        waits = self._waits(q, deps)
        slot[1] += inc
        tok = (slot[0], slot[1])
        q.entries.append((waits, fn(_REC), (slot[0], inc)))
        self._commit(reads, writes, tok)
        return tok

    def wait_all(self, qname):
        q = self.q[qname]
        deps = []
        for st in self.keys.values():
            if st[0] is not None:
                deps.append(st[0])
            deps.extend(st[1].values())
        for slot in self.dma_sems + self.cc_sems:
            if slot[1] > 0:
                deps.append((slot[0], slot[1]))
        waits = self._waits(q, deps)
        q.entries.append((waits, None, None))

    def emit(self):
        nc = self.nc
        engs = {"pe": "tensor", "act": "scalar", "dve": "vector", "pool": "gpsimd", "sp": "sync"}
        with nc.Block() as block:
            for qname, attr in engs.items():
                q = self.q[qname]
                entries = q.entries
                q.entries = []

                def body(eng, entries=entries, qname=qname):
                    with (self.pre.pop(qname)(eng) if qname in self.pre else contextlib.nullcontext()):
                        for waits, fn, inc in entries:
                            for sem, val in waits:
                                eng.wait_ge(sem, val)
                            if fn is not None:
                                _replay(eng, fn).then_inc(inc[0], inc[1])

                getattr(block, attr)(body)


def build_program(S, DFF, NA, NB):
    L = NA + NB
    T = S // 4
    CH1 = min(512, T)
    NC1 = T // CH1
    CH3 = min(512, T)
    NC3 = T // CH3
    NQ = S // 512
    NKB = S // 128
    KF = DFF // 128
    NG = 3 * L + 2
    G1, GM, G2, GKV, GFIN = 0, L, 2 * L, 3 * L, 3 * L + 1

    nc = bass.Bass("TRN2", target_bir_lowering=False)
    din = lambda name, shape, dt=F32: nc.dram_tensor(name, shape, dt, kind="ExternalInput").ap()
    xT = din("xT", [D, T])
    memT = din("memT", [D, NMEM])
    gains = din("gains", [128, NG * KD])
    w_qm = din("w_qm", [L, D, 256])
    w_mkv = din("w_mkv", [L, D, 512])
    w_o = din("w_o", [L, 768, D])
    w1 = din("w1", [L, D, DFF])
    w2 = din("w2", [L, DFF, D])
    wa_own = din("wa_own", [max(NA, 1), D, 384])
    wb_own = din("wb_own", [max(NB, 1), D, 128])
    wkvs_own = din("wkvs_own", [D, 258])
    bf_own = din("bf_own", [2, 1])
    roff = din("roff", [1, 1], I32)
    c_trineg = din("c_trineg", [128, 128], BF16)
    c_negones = din("c_negones", [128, 128], BF16)
    c_ones = din("c_ones", [128, 128], BF16)
    c_msb = din("c_msb", [128, 4 * 512], BF16)
    c_mfox = din("c_mfox", [128, 4 * 512], BF16)
    c_swap = din("c_swap", [128, 128], F32)
    yT = nc.dram_tensor("yT", [D, T], F32, kind="ExternalOutput").ap()

    hT = nc.dram_tensor("hT", [D, T], F32).ap()
    xg_in = nc.dram_tensor("xg_in", [D, T], BF16).ap()
    xg = nc.dram_tensor("xg", [4 * D, T], BF16).ap()
    memo = nc.dram_tensor("memo", [256, T], BF16).ap()
    mg_in = nc.dram_tensor("mg_in", [4 * 128, T], BF16).ap()
    mg = nc.dram_tensor("mg", [4 * 512, T], BF16).ap()
    mgl = nc.dram_tensor("mgl", [4 * 128, T], BF16).ap()
    ksave = nc.dram_tensor("ksave", [2 * 128, S], BF16).ap()
    vsave = nc.dram_tensor("vsave", [2 * 128, NKB * 128], BF16).ap()

    kview = lambda ap2d: ap2d.rearrange("(k p) n -> p k n", p=128)

    with contextlib.ExitStack() as st:
        P = Prog(nc)
        P.setup(st)
        sb = lambda name, shape, dt, stack=st: stack.enter_context(nc.sbuf_tensor(name, shape, dt))
        PSA = st.enter_context(nc.psum_tensor("psall", [128, 8 * 512], F32))
        PS = [PSA[:, i * 512:(i + 1) * 512] for i in range(8)]
        psk = lambda i: ("ps", i)

        dyn = {}

        @contextlib.contextmanager
        def sp_pre(eng):
            dyn["n"] = dyn.get("n", 0) + 1
            with eng.register("roff_reg%d" % dyn["n"]) as reg:
                eng.reg_load(reg, roff[0:1, 0:1])
                dyn["off"] = eng.snap(reg, min_val=0, max_val=1536)
                yield

        ones = sb("ones", [128, 128], BF16)
        swp = sb("swp", [128, 128], F32)
        gn = sb("gn", [128, NG * KD], F32)
        for dst, src, nm in ((ones, c_ones, "ones"), (swp, c_swap, "swp"), (gn, gains, "gn")):
            P.dma("sp", lambda e, dst=dst, src=src: e.dma_start(out=dst[:], in_=src), writes=[nm])
        gcol = lambda g, k: gn[:, g * KD + k:g * KD + k + 1]

        def rms_rstd(X, n, key, tmp_sq, rstd, ps_i, sqkey="sq", rkey="rstd"):
            P.op("act", lambda e: e.activation(out=tmp_sq[:, 0:KD, 0:n], in_=X[:, :, 0:n], func=AF.Square), reads=[key, "ones"], writes=[sqkey])
            for k in range(KD):
                P.op("pe", lambda e, k=k: e.matmul(PS[ps_i][:, 0:n], ones[:], tmp_sq[:, k, 0:n], start=(k == 0), stop=(k == KD - 1)),
                     reads=[sqkey, "ones"], writes=[psk(ps_i)])
            P.op("act", lambda e: e.activation(out=rstd[:, 0:n], in_=PS[ps_i][:, 0:n], func=AF.Sqrt, scale=1.0 / D, bias=EPS), reads=[psk(ps_i)], writes=[rkey])
            P.op("dve", lambda e: e.reciprocal(out=rstd[:, 0:n], in_=rstd[:, 0:n]), reads=[rkey], writes=[rkey])

        def softmax_av(z_mm, nblk, mask_fn, vaug_fn, vkey, base, out_tile, out_key, wk, n=512):
            o_ps = 6
            for bi in range(nblk):
                zi = bi % 2
                z_mm(bi, zi)
                pt = wk["p"][bi % 2]
                m = mask_fn(bi)
                if m is not None:
                    zt = wk["zt"]
                    P.op("dve", lambda e, zi=zi, m=m: e.tensor_tensor(out=zt[:, 0:n], in0=PS[zi][:, 0:n], in1=m, op=ALU.add), reads=[psk(zi), "mfox"], writes=["zt"])
                    P.op("act", lambda e, pt=pt: e.activation(out=pt[:, 0:n], in_=zt[:, 0:n], func=AF.Exp), reads=["zt"], writes=[("p", bi % 2)])
                else:
                    P.op("act", lambda e, zi=zi, pt=pt: e.activation(out=pt[:, 0:n], in_=PS[zi][:, 0:n], func=AF.Exp), reads=[psk(zi)], writes=[("p", bi % 2)])
                P.op("pe", lambda e, bi=bi, pt=pt: e.matmul(PS[o_ps][:, 0:n], vaug_fn(bi), pt[:, 0:n], start=(bi == 0), stop=(bi == nblk - 1)),
                     reads=[("p", bi % 2), vkey], writes=[psk(o_ps)])
            osb = wk["osb"]
            P.op("act", lambda e: e.activation(out=osb[:, 0:n], in_=PS[o_ps][:, 0:n], func=AF.Copy), reads=[psk(o_ps)], writes=["osb"])
            P.op("pe", lambda e: e.matmul(PS[7][:, 0:n], swp[:], osb[:, 0:n], start=True, stop=True), reads=["osb", "swp"], writes=[psk(7)])
            rd = wk["rden"]
            P.op("dve", lambda e: e.reciprocal(out=rd[base:base + 64, 0:n], in_=PS[7][base:base + 64, 0:n]), reads=[psk(7)], writes=["rden"])
            P.op("dve", lambda e: e.tensor_tensor(out=out_tile[base:base + 64, 0:n], in0=osb[base:base + 64, 0:n], in1=rd[base:base + 64, 0:n], op=ALU.mult),
                 reads=["osb", "rden"], writes=[out_key])

        stg_n = [0]

        def load_w(stg, dst3, src2d, K, N, key):
            SW = stg[0].shape[1]
            kper = max(1, SW // N)
            ncol = min(N, SW)
            for k0 in range(0, K, kper):
                kk = min(kper, K - k0)
                for n0 in range(0, N, ncol):
                    i = stg_n[0] % len(stg)
                    stg_n[0] += 1
                    sview = stg[i][:, 0:kk * ncol].rearrange("p (k n) -> p k n", n=ncol)
                    P.dma("sp", lambda e: e.dma_start(out=sview, in_=src2d[k0 * 128:(k0 + kk) * 128, n0:n0 + ncol].rearrange("(k p) n -> p k n", p=128)), writes=[("stg", i)])
                    eng = ("dve", "pool", "act", "dve")[i % 4]
                    if eng == "act":
                        P.op("act", lambda e: e.activation(out=dst3[:, k0:k0 + kk, n0:n0 + ncol], in_=sview, func=AF.Copy), reads=[("stg", i)], writes=[key])
                    else:
                        P.op(eng, lambda e: e.tensor_copy(out=dst3[:, k0:k0 + kk, n0:n0 + ncol], in_=sview), reads=[("stg", i)], writes=[key])

        for l in range(L):
            isA = l < NA
            h_src = xT if l == 0 else hT
            with contextlib.ExitStack() as ph:
                t = lambda name, shape, dt: sb("p1_%d_%s" % (l, name), shape, dt, ph)
                wqm = t("wqm", [128, KD, 256], BF16)
                wmkv = t("wmkv", [128, KD, 512], BF16)
                stg = [t("stg%d" % i, [128, 1024], F32) for i in range(3)]
                load_w(stg, wqm, w_qm[l], KD, 256, "wqm")
                load_w(stg, wmkv, w_mkv[l], KD, 512, "wmkv")
                hc = t("hc", [128, KD, 512], F32)
                sq = t("sq", [128, KD, 512], BF16)
                rstd = t("rstd", [128, 512], F32)
                hn = t("hn", [128, KD, 512], BF16)
                mkT = t("mkT", [128, 2, NMEM], BF16)
                mvaug = t("mvaug", [128, 4, 2, 128], BF16)
                qm = t("qm", [128, 2, 512], BF16)
                mo = t("mo", [128, 2, 512], BF16)
                wk = {"p": [t("pA", [128, 512], BF16), t("pB", [128, 512], BF16)], "osb": t("osb", [128, 512], F32), "rden": t("rden", [128, 512], F32)}
                P.dma("sp", lambda e: e.dma_start(out=hc[:, :, 0:NMEM], in_=kview(memT)), writes=["hc"])
                rms_rstd(hc, NMEM, "hc", sq, rstd, 2)
                for k in range(KD):
                    P.op("dve", lambda e, k=k: e.scalar_tensor_tensor(out=hn[:, k, 0:NMEM], in0=hc[:, k, 0:NMEM], scalar=gcol(GM + l, k), in1=rstd[:, 0:NMEM], op0=ALU.mult, op1=ALU.mult),
                         reads=["hc", "rstd", "gn"], writes=["hn"])
                for m in range(2):
                    for k in range(KD):
                        P.op("pe", lambda e, m=m, k=k: e.matmul(PS[m][:, 0:NMEM], wmkv[:, k, m * 128:(m + 1) * 128], hn[:, k, 0:NMEM], start=(k == 0), stop=(k == KD - 1)),
                             reads=["hn", "wmkv"], writes=[psk(m)])
                    P.op("act", lambda e, m=m: e.activation(out=mkT[:, m, :], in_=PS[m][:, 0:NMEM], func=AF.Copy), reads=[psk(m)], writes=["mkT"])
                P.op("pool", lambda e: e.memset(mvaug[:], 1.0), writes=["mvaug"])
                for mb in range(2):
                    for k in range(KD):
                        P.op("pe", lambda e, mb=mb, k=k: e.matmul(PS[2 + mb][:, 0:256], hn[:, k, mb * 128:(mb + 1) * 128], wmkv[:, k, 256:512], start=(k == 0), stop=(k == KD - 1)),
                             reads=["hn", "wmkv"], writes=[psk(2 + mb)])
                    for j in range(4):
                        c0 = (j % 2) * 64
                        P.op("act", lambda e, mb=mb, j=j, c0=c0: e.activation(out=mvaug[:, j, mb, c0:c0 + 64], in_=PS[2 + mb][:, j * 64:(j + 1) * 64], func=AF.Copy),
                             reads=[psk(2 + mb)], writes=["mvaug"])
                xha = t("xha", [128, KD, T], BF16)
                for c in range(NC1):
                    cs = slice(c * CH1, (c + 1) * CH1)
                    n = CH1
                    P.dma("sp", lambda e, cs=cs: e.dma_start(out=hc[:, :, 0:n], in_=kview(h_src)[:, :, cs]), reads=["hT"], writes=["hc"])
                    rms_rstd(hc, n, "hc", sq, rstd, 2)
                    for k in range(KD):
                        P.op("dve", lambda e, k=k: e.tensor_tensor(out=xha[:, k, cs], in0=hc[:, k, 0:n], in1=rstd[:, 0:n], op=ALU.mult), reads=["hc", "rstd"], writes=[("xh", c)])
                    P.dma("sp", lambda e, cs=cs: e.dma_start(out=kview(xg_in)[:, :, cs], in_=xha[:, :, cs]), reads=[("xh", c)], writes=["xg_in"])
                for k in range(KD):
                    P.dma("pool", lambda e, k=k: e.collective_compute("AllGather", ALU.bypass, replica_groups=GROUPS, ins=[xg_in[k * 128:(k + 1) * 128, :]], outs=[xg[k * 512:(k + 1) * 512, :]]),
                          reads=["xg_in"], writes=["xg"], inc=1, fresh=True)
                for c in range(NC1):
                    cs = slice(c * CH1, (c + 1) * CH1)
                    n = CH1
                    for k in range(KD):
                        P.op("act", lambda e, k=k: e.activation(out=hn[:, k, 0:n], in_=xha[:, k, cs], func=AF.Copy, scale=gcol(G1 + l, k)),
                             reads=[("xh", c), "gn"], writes=["hn"])
                    for m in range(2):
                        for k in range(KD):
                            P.op("pe", lambda e, m=m, k=k: e.matmul(PS[m][:, 0:n], wqm[:, k, m * 128:(m + 1) * 128], hn[:, k, 0:n], start=(k == 0), stop=(k == KD - 1)),
                                 reads=["hn", "wqm"], writes=[psk(m)])
                        P.op("act", lambda e, m=m: e.activation(out=qm[:, m, 0:n], in_=PS[m][:, 0:n], func=AF.Copy, scale=0.125), reads=[psk(m)], writes=["qm"])
                    for j in range(4):
                        base = (j % 2) * 64
                        jj = j // 2

                        def z_mm(bi, zi, base=base, jj=jj):
                            P.op("pe", lambda e: e.matmul(PS[zi][:, 0:n], mkT[base:base + 64, jj, bi * 128:(bi + 1) * 128], qm[base:base + 64, jj, 0:n], start=True, stop=True),
                                 reads=["mkT", "qm"], writes=[psk(zi)])

                        softmax_av(z_mm, 2, lambda bi: None, lambda bi, j=j: mvaug[:, j, bi, :], "mvaug", base, mo[:, jj, :], "mo", wk, n=n)
                    P.dma("sp", lambda e, cs=cs: e.dma_start(out=memo.rearrange("(k p) n -> p k n", p=128)[:, :, cs], in_=mo[:, :, 0:n]), reads=["mo"], writes=["memo"])
                P.wait_all("sp")
                P.emit()

            with contextlib.ExitStack() as ph:
                t = lambda name, shape, dt: sb("p2_%d_%s" % (l, name), shape, dt, ph)
                xc = [t("xcA", [128, KD, 512], BF16), t("xcB", [128, KD, 512], BF16)]
                wst = t("wst", [128, KD, 384], F32)
                mixc = [t("mix%d" % i, [128, 512], BF16) for i in range(4)]

                def load_own_w(dst, src_ap, ncol, gidx, key, c0=0):
                    P.dma("sp", lambda e: e.dma_start(out=wst[:, :, 0:ncol], in_=kview(src_ap)), writes=["wst"])
                    for k in range(KD):
                        P.op("dve", lambda e, k=k: e.tensor_scalar(out=dst[:, k, c0:c0 + ncol], in0=wst[:, k, 0:ncol], scalar1=gcol(gidx, k), scalar2=None, op0=ALU.mult),
                             reads=["wst", "gn"], writes=[key])

                def load_xc(tc, slot=None):
                    slot = tc % 2 if slot is None else slot % 2
                    rk, cc = divmod(tc * 512, T)
                    buf = xc[slot]
                    P.dma("sp", lambda e: e.dma_start(out=buf[:], in_=xg.rearrange("(k r p) t -> r p k t", k=KD, r=4)[rk][:, :, cc:cc + 512]), reads=["xg"], writes=[("xc", slot)])
                    return buf, ("xc", slot)

                def mix_out(mix_tile, mkey, base, qc):
                    rdst, cc = divmod(qc * 512, T)
                    P.dma("sp", lambda e: e.dma_start(out=mg_in[rdst * 128 + base:rdst * 128 + base + 64, cc:cc + 512], in_=mix_tile[base:base + 64, :]),
                          reads=[mkey], writes=["mg_in"])

                def load_const(name, shape, dt, src):
                    tl = t(name, shape, dt)
                    P.dma("sp", lambda e: e.dma_start(out=tl[:], in_=src), writes=[name])
                    return tl

                if isA:
                    trineg = load_const("trineg", [128, 128], BF16, c_trineg)
                    negones = load_const("negones", [128, 128], BF16, c_negones)
                    msb = load_const("msb", [128, 4 * 512], BF16, c_msb)
                    wown = t("wown", [128, KD, 384], BF16)
                    load_own_w(wown, wa_own[l], 384, G1 + l, "wown")
                    qT2 = t("qT2", [128, S], BF16)
                    kT2 = t("kT2", [128, S], BF16)
                    v2 = t("v2", [128, NKB, 128], BF16)
                    E2 = [t("e%d" % i, [128, 1024], F32) for i in range(3)]
                    SP2 = [t("sp%d" % i, [128, 1024], BF16) for i in range(3)]
                    X2 = [t("x%d" % i, [128, 1024], F32) for i in range(2)]
                    W2 = [t("w%d" % i, [128, 1024], BF16) for i in range(2)]
                    LB2 = [t("lb%d" % i, [128, 1024], BF16) for i in range(2)]
                    nxt = load_xc(0)
                    for tc in range(NQ):
                        buf, bk = nxt
                        if tc + 1 < NQ:
                            nxt = load_xc(tc + 1)
                        ts_ = slice(tc * 512, (tc + 1) * 512)
                        for which, dst, dkey, sc in ((0, qT2, "qT2", 0.125), (1, kT2, "kT2", 1.0)):
                            pi = which
                            for k in range(KD):
                                P.op("pe", lambda e, k=k, which=which, pi=pi: e.matmul(PS[pi][:, :], wown[:, k, which * 128:(which + 1) * 128], buf[:, k, :], start=(k == 0), stop=(k == KD - 1)),
                                     reads=[bk, "wown"], writes=[psk(pi)])
                            P.op("act", lambda e, dst=dst, pi=pi, sc=sc: e.activation(out=dst[:, ts_], in_=PS[pi][:, :], func=AF.Copy, scale=sc), reads=[psk(pi)], writes=[dkey])
                        for tb in range(4):
                            pi = 2 + tb % 2
                            for k in range(KD):
                                P.op("pe", lambda e, k=k, tb=tb, pi=pi: e.matmul(PS[pi][:, 0:128], buf[:, k, tb * 128:(tb + 1) * 128], wown[:, k, 256:384], start=(k == 0), stop=(k == KD - 1)),
                                     reads=[bk, "wown"], writes=[psk(pi)])
                            P.op("dve", lambda e, tb=tb, pi=pi: e.tensor_copy(out=v2[:, tc * 4 + tb, :], in_=PS[pi][:, 0:128]), reads=[psk(pi)], writes=["v2"])
                    ptiles = [(qc, i, 4 * qc + 4) for qc in range(NQ) for i in range(4 * qc + 4)]
                    NPT = len(ptiles)
                    hsl = lambda hh: slice(hh * 512, (hh + 1) * 512)

                    def pA_pe(s_):
                        qc, i, nb = ptiles[s_]
                        kb, zb = nb - 1 - i, (0 if s_ % 2 == 0 else 6)
                        qs = slice(qc * 512, (qc + 1) * 512)
                        for hh in range(2):
                            base = hh * 64
                            P.op("pe", lambda e: e.matmul(PS[zb + hh][:, :], kT2[base:base + 64, kb * 128:(kb + 1) * 128], qT2[base:base + 64, qs], start=True, stop=True),
                                 reads=["kT2", "qT2"], writes=[psk(zb + hh)])

                    def pA(s_):
                        qc, i, nb = ptiles[s_]
                        kb, zb, eb = nb - 1 - i, (0 if s_ % 2 == 0 else 6), s_ % 3
                        P.op("act", lambda e: e.activation(out=E2[eb][:], in_=PSA[:, zb * 512:(zb + 2) * 512], func=AF.Exp), reads=[psk(zb), psk(zb + 1)], writes=[("E2", eb)])
                        P.op("act", lambda e: e.activation(out=SP2[eb][:], in_=E2[eb][:], func=AF.Ln, bias=1.0), reads=[("E2", eb)], writes=[("SP2", eb)])
                        if kb >= 4 * qc:
                            rel = kb - 4 * qc
                            mk = msb[:, rel * 512:(rel + 1) * 512]
                            for hh in range(2):
                                P.op("pool", lambda e: e.tensor_tensor(out=SP2[eb][:, hsl(hh)], in0=SP2[eb][:, hsl(hh)], in1=mk, op=ALU.mult), reads=[("SP2", eb), "msb"], writes=[("SP2", eb)])
                                P.op("pool", lambda e: e.tensor_tensor(out=E2[eb][:, hsl(hh)], in0=E2[eb][:, hsl(hh)], in1=mk, op=ALU.mult), reads=[("E2", eb), "msb"], writes=[("E2", eb)])

                    def pB(s_):
                        qc, i, nb = ptiles[s_]
                        eb = s_ % 3
                        for hh in range(2):
                            P.op("pe", lambda e: e.matmul(PS[2 + hh][:, :], trineg[:], SP2[eb][:, hsl(hh)], start=True, stop=(i == 0)), reads=[("SP2", eb), "trineg"], writes=[psk(2 + hh)])
                            if i > 0:
                                P.op("pe", lambda e: e.matmul(PS[2 + hh][:, :], negones[:], LB2[(i - 1) % 2][:, hsl(hh)], start=False, stop=True),
                                     reads=[("LB2", (i - 1) % 2), "negones"], writes=[psk(2 + hh)])
                        if i < nb - 1:
                            if i == 0:
                                P.op("pool", lambda e: e.tensor_copy(out=LB2[0][:], in_=SP2[eb][:]), reads=[("SP2", eb)], writes=[("LB2", 0)])
                            else:
                                P.op("pool", lambda e: e.tensor_tensor(out=LB2[i % 2][:], in0=LB2[(i - 1) % 2][:], in1=SP2[eb][:], op=ALU.add),
                                     reads=[("SP2", eb), ("LB2", (i - 1) % 2)], writes=[("LB2", i % 2)])
                        P.op("act", lambda e: e.activation(out=X2[s_ % 2][:], in_=PSA[:, 2 * 512:4 * 512], func=AF.Exp), reads=[psk(2), psk(3)], writes=[("X2", s_ % 2)])

                    def pC(s_):
                        qc, i, nb = ptiles[s_]
                        kb, eb = nb - 1 - i, s_ % 3
                        o_ps = 4 + qc % 2
                        P.op("dve", lambda e: e.tensor_tensor(out=W2[s_ % 2][:], in0=E2[eb][:], in1=X2[s_ % 2][:], op=ALU.mult), reads=[("E2", eb), ("X2", s_ % 2)], writes=[("W2", s_ % 2)])
                        for hh in range(2):
                            base = hh * 64
                            P.op("pe", lambda e: e.matmul(PS[o_ps][base:base + 64, :], v2[:, kb, base:base + 64], W2[s_ % 2][:, hsl(hh)], start=(i == 0), stop=(i == nb - 1)),
                                 reads=[("W2", s_ % 2), "v2"], writes=[(psk(o_ps), hh)])
                        if i == nb - 1:
                            mt, mkey = mixc[qc % 4], ("mix", qc % 4)
                            P.op("dve", lambda e: e.tensor_copy(out=mt[:, :], in_=PS[o_ps][:, :]), reads=[(psk(o_ps), 0), (psk(o_ps), 1)], writes=[mkey])
                            for hh in range(2):
                                mix_out(mt, mkey, hh * 64, qc)

                    pA_pe(0)
                    for s_ in range(NPT + 2):
                        if s_ + 1 < NPT:
                            pA_pe(s_ + 1)
                        if s_ < NPT:
                            pA(s_)
                        if 0 <= s_ - 1 < NPT:
                            pB(s_ - 1)
                        if 0 <= s_ - 2 < NPT:
                            pC(s_ - 2)
                else:
                    lb_ = l - NA
                    first_b = (lb_ == 0)
                    mfox = load_const("mfox", [128, 4 * 512], BF16, c_mfox)
                    wq = t("wq", [128, KD, 128], BF16)
                    load_own_w(wq, wb_own[lb_], 128, G1 + l, "wq")
                    if first_b:
                        wkv = t("wkv", [128, KD, 258], BF16)
                        load_own_w(wkv, wkvs_own, 258, GKV, "wkv")
                        bfo = t("bfo", [2, 1], F32)
                        P.dma("sp", lambda e: e.dma_start(out=bfo[:], in_=bf_own), writes=["bfo"])
                        P.op("dve", lambda e: e.tensor_scalar(out=bfo[:], in0=bfo[:], scalar1=-1.0, scalar2=None, op0=ALU.mult), reads=["bfo"], writes=["bfo"])
                        one2 = t("one2", [2, 512], F32)
                        P.op("pool", lambda e: e.memset(one2[:], 1.0), writes=["one2"])
                        fe = t("fe", [2, 512], F32)
                        cc_ = [t("ccA", [2, 512], F32), t("ccB", [2, 512], F32)]
                        r1 = t("r1", [2, 512], F32)
                        spl = t("spl", [2, 6, 512], BF16)
                    qaug = t("qaug", [128, S], BF16)
                    kaug = t("kaug", [128, S], BF16)
                    vaug = t("vaug", [128, NKB, 128], BF16)
                    wk = {"p": [t("pA", [128, 512], BF16), t("pB", [128, 512], BF16)], "osb": t("osb", [128, 512], F32), "rden": t("rden", [128, 512], F32),
                          "zt": t("zt", [128, 512], F32), "zt2": [t("ztA", [128, 512], F32), t("ztB", [128, 512], F32)],
                          "p3": [t("p3_%d" % i, [128, 512], BF16) for i in range(4)]}
                    for hh in range(2):
                        base = hh * 64
                        P.op("pool", lambda e: e.memset(qaug[64:128, :], 1.0), reads=[], writes=["qaug"])
                        if first_b:
                            P.op("pool", lambda e: e.memset(kaug[64:128, :], 1.0), writes=["kaug"])
                            P.op("pool", lambda e: e.memset(vaug[:], 1.0), writes=["vaug"])
                        else:
                            P.dma("sp", lambda e, hh=hh: e.dma_start(out=kaug[:], in_=ksave[hh * 128:(hh + 1) * 128, :]), reads=["ksave"], writes=["kaug"])
                            P.dma("sp", lambda e, hh=hh: e.dma_start(out=vaug[:], in_=vsave[hh * 128:(hh + 1) * 128, :].rearrange("p (b c) -> p b c", c=128)), reads=["vsave"], writes=["vaug"])
                        nxt = load_xc(0, hh * NQ)
                        for tc in range(NQ):
                            buf, bk = nxt
                            if tc + 1 < NQ:
                                nxt = load_xc(tc + 1, tc + 1 + hh * NQ)
                            ts_ = slice(tc * 512, (tc + 1) * 512)
                            for k in range(KD):
                                P.op("pe", lambda e, k=k: e.matmul(PS[0][0:64, :], wq[:, k, base:base + 64], buf[:, k, :], start=(k == 0), stop=(k == KD - 1)),
                                     reads=[bk, "wq"], writes=[psk(0)])
                            P.op("act", lambda e: e.activation(out=qaug[0:64, ts_], in_=PS[0][0:64, :], func=AF.Copy, scale=0.125), reads=[psk(0)], writes=["qaug"])
                            if first_b:
                                for k in range(KD):
                                    P.op("pe", lambda e, k=k: e.matmul(PS[1][0:64, :], wkv[:, k, base:base + 64], buf[:, k, :], start=(k == 0), stop=(k == KD - 1)),
                                         reads=[bk, "wkv"], writes=[psk(1)])
                                P.op("act", lambda e: e.activation(out=kaug[0:64, ts_], in_=PS[1][0:64, :], func=AF.Copy), reads=[psk(1)], writes=["kaug"])
                                for tb in range(4):
                                    pi = 2 + tb % 2
                                    for k in range(KD):
                                        P.op("pe", lambda e, k=k, tb=tb, pi=pi: e.matmul(PS[pi][:, 0:64], buf[:, k, tb * 128:(tb + 1) * 128], wkv[:, k, 128 + base:128 + base + 64], start=(k == 0), stop=(k == KD - 1)),
                                             reads=[bk, "wkv"], writes=[psk(pi)])
                                    P.op("dve", lambda e, tb=tb, pi=pi: e.tensor_copy(out=vaug[:, tc * 4 + tb, base:base + 64], in_=PS[pi][:, 0:64]), reads=[psk(pi)], writes=["vaug"])
                                for k in range(KD):
                                    P.op("pe", lambda e, k=k: e.matmul(PS[4][0:2, :], wkv[:, k, 256:258], buf[:, k, :], start=(k == 0), stop=(k == KD - 1)),
                                         reads=[bk, "wkv"], writes=[psk(4)])
                                P.op("act", lambda e: e.activation(out=fe[:], in_=PS[4][0:2, :], func=AF.Exp, scale=-1.0, bias=bfo[:, 0:1]), reads=[psk(4), "bfo"], writes=["fe"])
                                P.op("act", lambda e: e.activation(out=fe[:], in_=fe[:], func=AF.Ln, bias=1.0), reads=["fe"], writes=["fe"])
                                P.op("dve", lambda e: e.tensor_scalar(out=fe[:], in0=fe[:], scalar1=-1.0, scalar2=None, op0=ALU.mult), reads=["fe"], writes=["fe"])
                                cur, prev = cc_[tc % 2], cc_[(tc + 1) % 2]
                                init = 0.0 if tc == 0 else prev[:, 511:512]
                                P.op("dve", lambda e, cur=cur, init=init: e.tensor_tensor_scan(out=cur[:], data0=one2[:], data1=fe[:], initial=init, op0=ALU.mult, op1=ALU.add),
                                     reads=["fe", "one2", ("cc", (tc + 1) % 2)], writes=[("cc", tc % 2)])
                                P.op("dve", lambda e, cur=cur: e.tensor_copy(out=spl[:, 0, :], in_=cur[:]), reads=[("cc", tc % 2)], writes=["spl"])
                                P.op("dve", lambda e, cur=cur: e.tensor_tensor(out=r1[:], in0=cur[:], in1=spl[:, 0, :], op=ALU.subtract), reads=[("cc", tc % 2), "spl"], writes=["r1"])
                                P.op("dve", lambda e: e.tensor_copy(out=spl[:, 1, :], in_=r1[:]), reads=["r1"], writes=["spl"])
                                P.op("dve", lambda e: e.tensor_tensor(out=r1[:], in0=r1[:], in1=spl[:, 1, :], op=ALU.subtract), reads=["r1", "spl"], writes=["r1"])
                                P.op("dve", lambda e: e.tensor_copy(out=spl[:, 2, :], in_=r1[:]), reads=["r1"], writes=["spl"])
                                P.op("dve", lambda e: e.tensor_scalar(out=spl[:, 3:6, :], in0=spl[:, 0:3, :], scalar1=-1.0, scalar2=None, op0=ALU.mult), reads=["spl"], writes=["spl"])
                                for j3 in range(3):
                                    P.dma("sp", lambda e, hh=hh, j3=j3: e.dma_start(out=qaug[64 + j3:65 + j3, ts_], in_=spl[hh:hh + 1, j3, :]), reads=["spl"], writes=["qaug"])
                                    P.dma("sp", lambda e, hh=hh, j3=j3: e.dma_start(out=kaug[67 + j3:68 + j3, ts_], in_=spl[hh:hh + 1, 3 + j3, :]), reads=["spl"], writes=["kaug"])
                                    P.dma("sp", lambda e, hh=hh, j3=j3: e.dma_start(out=kaug[70 + j3:71 + j3, ts_], in_=spl[hh:hh + 1, j3, :]), reads=["spl"], writes=["kaug"])
                        if first_b and NB > 1:
                            P.dma("sp", lambda e, hh=hh: e.dma_start(out=ksave[hh * 128:(hh + 1) * 128, :], in_=kaug[:]), reads=["kaug"], writes=["ksave"])
                            P.dma("sp", lambda e, hh=hh: e.dma_start(out=vsave[hh * 128:(hh + 1) * 128, :].rearrange("p (b c) -> p b c", c=128), in_=vaug[:]), reads=["vaug"], writes=["vsave"])
                        if not first_b:
                            P.dma("sp", lambda e: e.dma_start(out=qaug[64:67, :], in_=kaug[70:73, :]), reads=["kaug"], writes=["qaug"])
                        ftiles = [(qc, bi, 4 * qc + 4) for qc in range(NQ) for bi in range(4 * qc + 4)]
                        NFT = len(ftiles)
                        PB = wk["p3"]

                        def fA_pe(g):
                            qc, bi, nb = ftiles[g]
                            zi = g % 3
                            qs = slice(qc * 512, (qc + 1) * 512)
                            P.op("pe", lambda e: e.matmul(PS[zi][:, :], kaug[0:70, bi * 128:(bi + 1) * 128], qaug[0:70, qs], start=True, stop=True),
                                 reads=["kaug", "qaug"], writes=[psk(zi)])

                        def fA(g, base=base):
                            qc, bi, nb = ftiles[g]
                            zi, pb = g % 3, g % 4
                            if bi >= 4 * qc:
                                rel = bi - 4 * qc
                                m = mfox[:, rel * 512:(rel + 1) * 512]
                                zt = wk["zt2"][g % 2]
                                P.op("dve", lambda e: e.tensor_tensor(out=zt[:], in0=PS[zi][:, :], in1=m, op=ALU.add), reads=[psk(zi), "mfox"], writes=[("zt", g % 2)])
                                P.op("act", lambda e: e.activation(out=PB[pb][:], in_=zt[:], func=AF.Exp), reads=[("zt", g % 2)], writes=[("p3", pb)])
                            else:
                                P.op("act", lambda e: e.activation(out=PB[pb][:], in_=PS[zi][:, :], func=AF.Exp), reads=[psk(zi)], writes=[("p3", pb)])

                        def fB(g, base=base):
                            qc, bi, nb = ftiles[g]
                            o_ps = 4 + qc % 2
                            P.op("pe", lambda e: e.matmul(PS[o_ps][:, :], vaug[:, bi, :], PB[g % 4][:], start=(bi == 0), stop=(bi == nb - 1)),
                                 reads=[("p3", g % 4), "vaug"], writes=[psk(o_ps)])

                        def fEpi(g, base=base):
                            qc, bi, nb = ftiles[g]
                            o_ps = 4 + qc % 2
                            osb, rd = wk["osb"], wk["rden"]
                            mt, mkey = mixc[qc % 2], ("mix", qc % 2)
                            P.op("act", lambda e: e.activation(out=osb[:], in_=PS[o_ps][:, :], func=AF.Copy), reads=[psk(o_ps)], writes=["osb"])
                            P.op("pe", lambda e: e.matmul(PS[7][:, :], swp[:], osb[:], start=True, stop=True), reads=["osb", "swp"], writes=[psk(7)])
                            P.op("dve", lambda e: e.reciprocal(out=rd[base:base + 64, :], in_=PS[7][base:base + 64, :]), reads=[psk(7)], writes=["rden"])
                            P.op("dve", lambda e: e.tensor_tensor(out=mt[base:base + 64, :], in0=osb[base:base + 64, :], in1=rd[base:base + 64, :], op=ALU.mult),
                                 reads=["osb", "rden"], writes=[mkey])
                            mix_out(mt, mkey, base, qc)

                        fA_pe(0)
                        for s_ in range(NFT + 4):
                            if s_ + 1 < NFT:
                                fA_pe(s_ + 1)
                            if s_ < NFT:
                                fA(s_)
                            if 0 <= s_ - 2 < NFT:
                                fB(s_ - 2)
                            g2 = s_ - 4
                            if 0 <= g2 < NFT and ftiles[g2][1] == ftiles[g2][2] - 1:
                                fEpi(g2)
                P.wait_all("sp")
                P.emit()

            for r_ in range(4):
                P.dma("pool", lambda e, r_=r_: e.collective_compute("AllGather", ALU.bypass, replica_groups=GROUPS, ins=[mg_in[r_ * 128:(r_ + 1) * 128, :]], outs=[mg[r_ * 512:(r_ + 1) * 512, :]]),
                      reads=["mg_in"], writes=["mg"], inc=1, fresh=True)

            with contextlib.ExitStack() as ph:
                t = lambda name, shape, dt: sb("p3_%d_%s" % (l, name), shape, dt, ph)
                wo = t("wo", [128, 6, D], BF16)
                W1 = t("W1", [128, KD, DFF], BF16)
                W2 = t("W2", [128, KF, D], BF16)
                n = CH3
                hc = t("hc", [128, KD, n], F32)
                stg = [t("stg%d" % i, [128, 512], F32)[:, :] for i in range(3)] + ([hc[:, i, :] for i in range(KD)] if n == 512 else [])
                NSTG = len(stg)
                load_w(stg, wo, w_o[l], 6, D, "wo")
                load_w(stg, W1, w1[l], KD, DFF, "W1")
                load_w(stg, W2, w2[l], KF, D, "W2")
                mx = t("mx", [128, KD, n], BF16)
                h1 = t("h1", [128, max(KF, KD), n], BF16)
                rl = [t("rlA", [128, n], F32), t("rlB", [128, n], F32)]
                rstd, RK = rl[0], ("rl", 0)
                P.dma("sp", lambda e: e.dma_start(out=mgl, in_=Late(lambda: mg[bass.ds(dyn["off"], 512), :])), reads=["mg"], writes=["mgl"])
                for c in range(NC3):
                    cs = slice(c * n, (c + 1) * n)
                    P.dma("sp", lambda e: e.dma_start(out=mx[:, 0:4, :], in_=kview(mgl)[:, :, cs]), reads=["mgl"], writes=["mx"])
                    P.dma("sp", lambda e: e.dma_start(out=mx[:, 4:6, :], in_=memo.rearrange("(k p) n -> p k n", p=128)[:, :, cs]), reads=["memo"], writes=["mx"])
                    P.dma("sp", lambda e: e.dma_start(out=hc[:], in_=kview(h_src)[:, :, cs]), reads=["hT"], writes=["hc"] + ([("stg", i) for i in range(3, NSTG)] if c == 0 else []))
                    for m in range(KD):
                        pi = m % 2
                        for k in range(6):
                            P.op("pe", lambda e: e.matmul(PS[pi][:, 0:n], wo[:, k, m * 128:(m + 1) * 128], mx[:, k, :], start=(k == 0), stop=(k == 5)),
                                 reads=["mx", "wo"], writes=[psk(pi)])
                        P.op("dve", lambda e: e.tensor_tensor(out=hc[:, m, :], in0=hc[:, m, :], in1=PS[pi][:, 0:n], op=ALU.add), reads=[psk(pi), "hc"], writes=["hc"])
                    rms_rstd(hc, n, "hc", h1, rstd, 2, sqkey="h1", rkey=RK)
                    for k in range(KD):
                        P.op("dve", lambda e: e.scalar_tensor_tensor(out=mx[:, k, :], in0=hc[:, k, :], scalar=gcol(G2 + l, k), in1=rstd[:, 0:n], op0=ALU.mult, op1=ALU.mult),
                             reads=["hc", RK, "gn"], writes=["mx"])
                    for m in range(KF):
                        pi = 3 + m % 2
                        for k in range(KD):
                            P.op("pe", lambda e: e.matmul(PS[pi][:, 0:n], W1[:, k, m * 128:(m + 1) * 128], mx[:, k, :], start=(k == 0), stop=(k == KD - 1)),
                                 reads=["mx", "W1"], writes=[psk(pi)])
                        r = rl[m % 2]
                        P.op("act", lambda e: e.activation(out=r[:], in_=PS[pi][:, 0:n], func=AF.Relu), reads=[psk(pi)], writes=[("rl", m % 2)])
                        P.op("pool" if m % 2 else "dve", lambda e: e.tensor_tensor(out=h1[:, m, :], in0=r[:], in1=r[:], op=ALU.mult), reads=[("rl", m % 2)], writes=["h1"])
                    last = (l == L - 1)
                    for m in range(KD):
                        pi = 5 + m % 2
                        for k in range(KF):
                            P.op("pe", lambda e: e.matmul(PS[pi][:, 0:n], W2[:, k, m * 128:(m + 1) * 128], h1[:, k, :], start=(k == 0), stop=(k == KF - 1)),
                                 reads=["h1", "W2"], writes=[psk(pi)])
                        P.op("dve", lambda e: e.tensor_tensor(out=hc[:, m, :], in0=hc[:, m, :], in1=PS[pi][:, 0:n], op=ALU.add), reads=[psk(pi), "hc"], writes=["hc"])
                    if not last:
                        P.dma("sp", lambda e: e.dma_start(out=kview(hT)[:, :, cs], in_=hc[:]), reads=["hc"], writes=["hT"])
                    else:
                        rms_rstd(hc, n, "hc", h1, rstd, 2, sqkey="h1", rkey=RK)
                        for k in range(KD):
                            P.op("dve", lambda e: e.scalar_tensor_tensor(out=hc[:, k, :], in0=hc[:, k, :], scalar=gcol(GFIN, k), in1=rstd[:, 0:n], op0=ALU.mult, op1=ALU.mult),
                                 reads=["hc", RK, "gn"], writes=["hc"])
                        P.dma("sp", lambda e: e.dma_start(out=kview(yT)[:, :, cs], in_=hc[:]), reads=["hc"], writes=["yT"])
                P.wait_all("sp")
                P.pre["sp"] = sp_pre
                P.emit()
    return nc


def _consts():
    bf = ml_dtypes.bfloat16
    j = np.arange(128)[:, None]
    s = np.arange(128)[None, :]
    trineg = np.where(j >= s, -1.0, 0.0).astype(bf)
    negones = np.full((128, 128), -1.0).astype(bf)
    ones = np.ones((128, 128)).astype(bf)
    p = np.arange(128)[:, None]
    tq = np.arange(512)[None, :]
    msb = np.concatenate([(r * 128 + p < tq) for r in range(4)], axis=1).astype(np.float32).astype(bf)
    mfox = (np.concatenate([(r * 128 + p <= tq) for r in range(4)], axis=1).astype(np.float32) - 1.0) * 30000.0
    mfox = mfox.astype(bf)
    swap = np.zeros((128, 128), np.float32)
    swap[(np.arange(128) + 64) % 128, np.arange(128)] = 1.0
    return dict(c_trineg=trineg, c_negones=negones, c_ones=ones, c_msb=msb, c_mfox=mfox, c_swap=swap)


def make_in_maps(inp, S, DFF, NA, NB):
    L = NA + NB
    T = S // 4
    f32 = np.float32
    G = np.concatenate([inp["norm1_g"], inp["mem_norm_g"], inp["norm2_g"], inp["kv_norm_g"][None], inp["final_norm_g"][None]], 0).astype(f32)
    NG = G.shape[0]
    gains = np.ascontiguousarray(G.reshape(NG, 8, 128).transpose(2, 0, 1).reshape(128, NG * 8))
    consts = _consts()
    w_in_a, w_in_b, wkv = inp["w_in_a"], inp["w_in_b"], inp["w_kv_shared"]
    qm_cols = lambda w, off: w[:, :, off:off + 256]
    w_qm = np.ascontiguousarray(np.concatenate([qm_cols(w_in_a, 1536), qm_cols(w_in_b, 512)], 0)) if NB > 0 else np.ascontiguousarray(qm_cols(w_in_a, 1536))
    maps = []
    for c in range(8):
        b, r = divmod(c, 4)
        hs = slice(r * 128, (r + 1) * 128)
        wa = np.concatenate([w_in_a[:, :, r * 128:(r + 1) * 128], w_in_a[:, :, 512 + r * 128:512 + (r + 1) * 128], w_in_a[:, :, 1024 + r * 128:1024 + (r + 1) * 128]], axis=2)
        wb = w_in_b[:, :, hs] if NB > 0 else np.zeros((1, D, 128), f32)
        wk = np.concatenate([wkv[:, hs], wkv[:, 512 + r * 128:512 + (r + 1) * 128], wkv[:, 1024 + 2 * r:1024 + 2 * r + 2]], axis=1)
        m = dict(
            xT=np.ascontiguousarray(inp["x"][b, r * T:(r + 1) * T, :].T),
            memT=np.ascontiguousarray(inp["mem"][b].T),
            gains=gains, w_qm=w_qm, w_mkv=inp["w_mem_kv"], w_o=inp["w_o"], w1=inp["w_mlp1"], w2=inp["w_mlp2"],
            wa_own=np.ascontiguousarray(wa), wb_own=np.ascontiguousarray(wb), wkvs_own=np.ascontiguousarray(wk),
            bf_own=np.ascontiguousarray(inp["b_f"][2 * r:2 * r + 2].reshape(2, 1)),
            roff=np.array([[r * 512]], np.int32),
        )
        m.update(consts)
        maps.append(m)
    return maps


def assemble(results, S):
    T = S // 4
    out = np.empty((2, S, D), np.float32)
    for c in range(8):
        b, r = divmod(c, 4)
        out[b, r * T:(r + 1) * T, :] = results[c]["yT"].T
    return out


def kernel(**inputs):
    inp = {k: np.asarray(v) for k, v in inputs.items()}
    S, DFF, NA, NB = 16384, 4096, 2, 2
    nc = build_program(S, DFF, NA, NB)
    maps = make_in_maps(inp, S, DFF, NA, NB)
    res = run_bass_kernel_spmd(nc, maps, core_ids=list(range(8)))
    return assemble(res.results, S)
```

```python
import contextlib
import numpy as np
import ml_dtypes
import concourse.bass as bass
import concourse.mybir as mybir
from concourse.bass_utils import run_bass_kernel_spmd

F32 = mybir.dt.float32
BF16 = mybir.dt.bfloat16
I32 = mybir.dt.int32
AF = mybir.ActivationFunctionType
ALU = mybir.AluOpType

D = 1024
KD = 8
NMEM = 256
EPS = 1e-6
GROUPS = [[0, 1, 2, 3], [4, 5, 6, 7]]


class _Q:
    def __init__(self, name, sem):
        self.name = name
        self.sem = sem
        self.count = 0
        self.entries = []
        self.waited = {}


class Late:
    def __init__(self, f):
        self.f = f


class _Rec:
    def __getattr__(self, name):
        return lambda *a, **kw: (name, a, kw)


_REC = _Rec()


def _replay(eng, call):
    name, a, kw = call
    a = [x.f() if isinstance(x, Late) else x for x in a]
    kw = {k: (v.f() if isinstance(v, Late) else v) for k, v in kw.items()}
    try:
        return getattr(eng, name)(*a, **kw)
    except Exception:
        print("REPLAY FAIL", name, [getattr(x, "shape", x) for x in a], {k: getattr(v, "shape", v) for k, v in kw.items()})
        raise


class Prog:
    def __init__(self, nc, n_dma_sems=24):
        self.nc = nc
        self.q = {}
        self.keys = {}
        self.dma_sems = []
        self.n_dma_sems = n_dma_sems
        self.dma_rr = 0
        self.no_same_engine_wait = {"pe"}
        self.pre = {}
        self.cc_sems = []

    def setup(self, stack):
        nc = self.nc
        self.stack = stack
        for name in ("pe", "act", "dve", "pool", "sp"):
            sem = stack.enter_context(nc.semaphore("s_" + name))
            self.q[name] = _Q(name, sem)
        for i in range(self.n_dma_sems):
            sem = stack.enter_context(nc.semaphore("s_dma%d" % i))
            self.dma_sems.append([sem, 0])

    def _deps(self, reads, writes):
        deps = []
        for k in reads:
            st = self.keys.get(k)
            if st and st[0] is not None:
                deps.append(st[0])
        for k in writes:
            st = self.keys.get(k)
            if st:
                if st[0] is not None:
                    deps.append(st[0])
                deps.extend(st[1].values())
        return deps

    def _commit(self, reads, writes, tok):
        for k in reads:
            st = self.keys.setdefault(k, [None, {}])
            st[1][id(tok[0])] = tok
        for k in writes:
            self.keys[k] = [tok, {}]

    def _waits(self, q, deps):
        need = {}
        for sem, val in deps:
            if sem is q.sem and q.name in self.no_same_engine_wait:
                continue
            if q.waited.get(id(sem), 0) >= val:
                continue
            if need.get(id(sem), (None, 0))[1] < val:
                need[id(sem)] = (sem, val)
        out = []
        for sem, val in need.values():
            q.waited[id(sem)] = val
            out.append((sem, val))
        return out

    def op(self, qname, fn, reads=(), writes=()):
        q = self.q[qname]
        waits = self._waits(q, self._deps(reads, writes))
        q.count += 1
        tok = (q.sem, q.count)
        q.entries.append((waits, fn(_REC), (q.sem, 1)))
        self._commit(reads, writes, tok)
        return tok

    def dma(self, qname, fn, reads=(), writes=(), inc=16, fresh=False):
        q = self.q[qname]
        if fresh:
            slot = [self.stack.enter_context(self.nc.semaphore("s_cc%d" % len(self.cc_sems))), 0]
            self.cc_sems.append(slot)
        else:
            slot = self.dma_sems[self.dma_rr % self.n_dma_sems]
            self.dma_rr += 1
        deps = self._deps(reads, writes)
        if slot[1] > 0:
            deps.append((slot[0], slot[1]))
        waits = self._waits(q, deps)
        slot[1] += inc
        tok = (slot[0], slot[1])
        q.entries.append((waits, fn(_REC), (slot[0], inc)))
        self._commit(reads, writes, tok)
        return tok

    def wait_all(self, qname):
        q = self.q[qname]
        deps = []
        for st in self.keys.values():
            if st[0] is not None:
                deps.append(st[0])
            deps.extend(st[1].values())
        for slot in self.dma_sems + self.cc_sems:
            if slot[1] > 0:
                deps.append((slot[0], slot[1]))
        waits = self._waits(q, deps)
        q.entries.append((waits, None, None))

    def emit(self):
        nc = self.nc
        engs = {"pe": "tensor", "act": "scalar", "dve": "vector", "pool": "gpsimd", "sp": "sync"}
        with nc.Block() as block:
            for qname, attr in engs.items():
                q = self.q[qname]
                entries = q.entries
                q.entries = []

                def body(eng, entries=entries, qname=qname):
                    with (self.pre.pop(qname)(eng) if qname in self.pre else contextlib.nullcontext()):
                        for waits, fn, inc in entries:
                            for sem, val in waits:
                                eng.wait_ge(sem, val)
                            if fn is not None:
                                _replay(eng, fn).then_inc(inc[0], inc[1])

                getattr(block, attr)(body)


def build_program(S, DFF, NA, NB):
    L = NA + NB
    T = S // 4
    CH1 = min(512, T)
    NC1 = T // CH1
    CH3 = min(512, T)
    NC3 = T // CH3
    NQ = S // 512
    NKB = S // 128
    KF = DFF // 128
    NG = 3 * L + 2
    G1, GM, G2, GKV, GFIN = 0, L, 2 * L, 3 * L, 3 * L + 1

    nc = bass.Bass("TRN2", target_bir_lowering=False)
    din = lambda name, shape, dt=F32: nc.dram_tensor(name, shape, dt, kind="ExternalInput").ap()
    xT = din("xT", [D, T])
    memT = din("memT", [D, NMEM])
    gains = din("gains", [128, NG * KD])
    w_qm = din("w_qm", [L, D, 256])
    w_mkv = din("w_mkv", [L, D, 512])
    w_o = din("w_o", [L, 768, D])
    w1 = din("w1", [L, D, DFF])
    w2 = din("w2", [L, DFF, D])
    wa_own = din("wa_own", [max(NA, 1), D, 384])
    wb_own = din("wb_own", [max(NB, 1), D, 128])
    wkvs_own = din("wkvs_own", [D, 258])
    bf_own = din("bf_own", [2, 1])
    roff = din("roff", [1, 1], I32)
    c_trineg = din("c_trineg", [128, 128], BF16)
    c_negones = din("c_negones", [128, 128], BF16)
    c_ones = din("c_ones", [128, 128], BF16)
    c_msb = din("c_msb", [128, 4 * 512], BF16)
    c_mfox = din("c_mfox", [128, 4 * 512], BF16)
    c_swap = din("c_swap", [128, 128], F32)
    yT = nc.dram_tensor("yT", [D, T], F32, kind="ExternalOutput").ap()

    hT = nc.dram_tensor("hT", [D, T], F32).ap()
    xg_in = nc.dram_tensor("xg_in", [D, T], BF16).ap()
    xg = nc.dram_tensor("xg", [4 * D, T], BF16).ap()
    memo = nc.dram_tensor("memo", [256, T], BF16).ap()
    mg_in = nc.dram_tensor("mg_in", [4 * 128, T], BF16).ap()
    mg = nc.dram_tensor("mg", [4 * 512, T], BF16).ap()
    mgl = nc.dram_tensor("mgl", [4 * 128, T], BF16).ap()
    ksave = nc.dram_tensor("ksave", [2 * 128, S], BF16).ap()
    vsave = nc.dram_tensor("vsave", [2 * 128, NKB * 128], BF16).ap()

    kview = lambda ap2d: ap2d.rearrange("(k p) n -> p k n", p=128)

    with contextlib.ExitStack() as st:
        P = Prog(nc)
        P.setup(st)
        sb = lambda name, shape, dt, stack=st: stack.enter_context(nc.sbuf_tensor(name, shape, dt))
        PSA = st.enter_context(nc.psum_tensor("psall", [128, 8 * 512], F32))
        PS = [PSA[:, i * 512:(i + 1) * 512] for i in range(8)]
        psk = lambda i: ("ps", i)

        dyn = {}

        @contextlib.contextmanager
        def sp_pre(eng):
            dyn["n"] = dyn.get("n", 0) + 1
            with eng.register("roff_reg%d" % dyn["n"]) as reg:
                eng.reg_load(reg, roff[0:1, 0:1])
                dyn["off"] = eng.snap(reg, min_val=0, max_val=1536)
                yield

        ones = sb("ones", [128, 128], BF16)
        swp = sb("swp", [128, 128], F32)
        gn = sb("gn", [128, NG * KD], F32)
        for dst, src, nm in ((ones, c_ones, "ones"), (swp, c_swap, "swp"), (gn, gains, "gn")):
            P.dma("sp", lambda e, dst=dst, src=src: e.dma_start(out=dst[:], in_=src), writes=[nm])
        gcol = lambda g, k: gn[:, g * KD + k:g * KD + k + 1]

        def rms_rstd(X, n, key, tmp_sq, rstd, ps_i, sqkey="sq", rkey="rstd"):
            P.op("act", lambda e: e.activation(out=tmp_sq[:, 0:KD, 0:n], in_=X[:, :, 0:n], func=AF.Square), reads=[key, "ones"], writes=[sqkey])
            for k in range(KD):
                P.op("pe", lambda e, k=k: e.matmul(PS[ps_i][:, 0:n], ones[:], tmp_sq[:, k, 0:n], start=(k == 0), stop=(k == KD - 1)),
                     reads=[sqkey, "ones"], writes=[psk(ps_i)])
            P.op("act", lambda e: e.activation(out=rstd[:, 0:n], in_=PS[ps_i][:, 0:n], func=AF.Sqrt, scale=1.0 / D, bias=EPS), reads=[psk(ps_i)], writes=[rkey])
            P.op("dve", lambda e: e.reciprocal(out=rstd[:, 0:n], in_=rstd[:, 0:n]), reads=[rkey], writes=[rkey])

        def softmax_av(z_mm, nblk, mask_fn, vaug_fn, vkey, base, out_tile, out_key, wk, n=512):
            o_ps = 6
            for bi in range(nblk):
                zi = bi % 2
                z_mm(bi, zi)
                pt = wk["p"][bi % 2]
                m = mask_fn(bi)
                if m is not None:
                    zt = wk["zt"]
                    P.op("dve", lambda e, zi=zi, m=m: e.tensor_tensor(out=zt[:, 0:n], in0=PS[zi][:, 0:n], in1=m, op=ALU.add), reads=[psk(zi), "mfox"], writes=["zt"])
                    P.op("act", lambda e, pt=pt: e.activation(out=pt[:, 0:n], in_=zt[:, 0:n], func=AF.Exp), reads=["zt"], writes=[("p", bi % 2)])
                else:
                    P.op("act", lambda e, zi=zi, pt=pt: e.activation(out=pt[:, 0:n], in_=PS[zi][:, 0:n], func=AF.Exp), reads=[psk(zi)], writes=[("p", bi % 2)])
                P.op("pe", lambda e, bi=bi, pt=pt: e.matmul(PS[o_ps][:, 0:n], vaug_fn(bi), pt[:, 0:n], start=(bi == 0), stop=(bi == nblk - 1)),
                     reads=[("p", bi % 2), vkey], writes=[psk(o_ps)])
            osb = wk["osb"]
            P.op("act", lambda e: e.activation(out=osb[:, 0:n], in_=PS[o_ps][:, 0:n], func=AF.Copy), reads=[psk(o_ps)], writes=["osb"])
            P.op("pe", lambda e: e.matmul(PS[7][:, 0:n], swp[:], osb[:, 0:n], start=True, stop=True), reads=["osb", "swp"], writes=[psk(7)])
            rd = wk["rden"]
            P.op("dve", lambda e: e.reciprocal(out=rd[base:base + 64, 0:n], in_=PS[7][base:base + 64, 0:n]), reads=[psk(7)], writes=["rden"])
            P.op("dve", lambda e: e.tensor_tensor(out=out_tile[base:base + 64, 0:n], in0=osb[base:base + 64, 0:n], in1=rd[base:base + 64, 0:n], op=ALU.mult),
                 reads=["osb", "rden"], writes=[out_key])

        stg_n = [0]

        def load_w(stg, dst3, src2d, K, N, key):
            SW = stg[0].shape[1]
            kper = max(1, SW // N)
            ncol = min(N, SW)
            for k0 in range(0, K, kper):
                kk = min(kper, K - k0)
                for n0 in range(0, N, ncol):
                    i = stg_n[0] % len(stg)
                    stg_n[0] += 1
                    sview = stg[i][:, 0:kk * ncol].rearrange("p (k n) -> p k n", n=ncol)
                    P.dma("sp", lambda e: e.dma_start(out=sview, in_=src2d[k0 * 128:(k0 + kk) * 128, n0:n0 + ncol].rearrange("(k p) n -> p k n", p=128)), writes=[("stg", i)])
                    eng = ("dve", "act")[i % 2]
                    if eng == "act":
                        P.op("act", lambda e: e.activation(out=dst3[:, k0:k0 + kk, n0:n0 + ncol], in_=sview, func=AF.Copy), reads=[("stg", i)], writes=[key])
                    else:
                        P.op(eng, lambda e: e.tensor_copy(out=dst3[:, k0:k0 + kk, n0:n0 + ncol], in_=sview), reads=[("stg", i)], writes=[key])

        for l in range(L):
            isA = l < NA
            h_src = xT if l == 0 else hT
            with contextlib.ExitStack() as ph:
                t = lambda name, shape, dt: sb("p1_%d_%s" % (l, name), shape, dt, ph)
                wqm = t("wqm", [128, KD, 256], BF16)
                wmkv = t("wmkv", [128, KD, 512], BF16)
                stg = [t("stg%d" % i, [128, 1024], F32) for i in range(3)]
                load_w(stg, wqm, w_qm[l], KD, 256, "wqm")
                load_w(stg, wmkv, w_mkv[l], KD, 512, "wmkv")
                hc = t("hc", [128, KD, 512], F32)
                sq = t("sq", [128, KD, 512], BF16)
                rstd = t("rstd", [128, 512], F32)
                hn = t("hn", [128, KD, 512], BF16)
                mkT = t("mkT", [128, 2, NMEM], BF16)
                mvaug = t("mvaug", [128, 4, 2, 128], BF16)
                qm = t("qm", [128, 2, 512], BF16)
                mo = t("mo", [128, 2, 512], BF16)
                wk = {"p": [t("pA", [128, 512], BF16), t("pB", [128, 512], BF16)], "osb": t("osb", [128, 512], F32), "rden": t("rden", [128, 512], F32)}
                P.dma("sp", lambda e: e.dma_start(out=hc[:, :, 0:NMEM], in_=kview(memT)), writes=["hc"])
                rms_rstd(hc, NMEM, "hc", sq, rstd, 2)
                for k in range(KD):
                    P.op("dve", lambda e, k=k: e.scalar_tensor_tensor(out=hn[:, k, 0:NMEM], in0=hc[:, k, 0:NMEM], scalar=gcol(GM + l, k), in1=rstd[:, 0:NMEM], op0=ALU.mult, op1=ALU.mult),
                         reads=["hc", "rstd", "gn"], writes=["hn"])
                for m in range(2):
                    for k in range(KD):
                        P.op("pe", lambda e, m=m, k=k: e.matmul(PS[m][:, 0:NMEM], wmkv[:, k, m * 128:(m + 1) * 128], hn[:, k, 0:NMEM], start=(k == 0), stop=(k == KD - 1)),
                             reads=["hn", "wmkv"], writes=[psk(m)])
                    P.op("act", lambda e, m=m: e.activation(out=mkT[:, m, :], in_=PS[m][:, 0:NMEM], func=AF.Copy), reads=[psk(m)], writes=["mkT"])
                P.op("pool", lambda e: e.memset(mvaug[:], 1.0), writes=["mvaug"])
                for mb in range(2):
                    for k in range(KD):
                        P.op("pe", lambda e, mb=mb, k=k: e.matmul(PS[2 + mb][:, 0:256], hn[:, k, mb * 128:(mb + 1) * 128], wmkv[:, k, 256:512], start=(k == 0), stop=(k == KD - 1)),
                             reads=["hn", "wmkv"], writes=[psk(2 + mb)])
                    for j in range(4):
                        c0 = (j % 2) * 64
                        P.op("act", lambda e, mb=mb, j=j, c0=c0: e.activation(out=mvaug[:, j, mb, c0:c0 + 64], in_=PS[2 + mb][:, j * 64:(j + 1) * 64], func=AF.Copy),
                             reads=[psk(2 + mb)], writes=["mvaug"])
                xha = t("xha", [128, KD, T], BF16)
                for c in range(NC1):
                    cs = slice(c * CH1, (c + 1) * CH1)
                    n = CH1
                    P.dma("sp", lambda e, cs=cs: e.dma_start(out=hc[:, :, 0:n], in_=kview(h_src)[:, :, cs]), reads=["hT"], writes=["hc"])
                    rms_rstd(hc, n, "hc", sq, rstd, 2)
                    for k in range(KD):
                        P.op("dve", lambda e, k=k: e.tensor_tensor(out=xha[:, k, cs], in0=hc[:, k, 0:n], in1=rstd[:, 0:n], op=ALU.mult), reads=["hc", "rstd"], writes=[("xh", c)])
                    P.dma("sp", lambda e, cs=cs: e.dma_start(out=kview(xg_in)[:, :, cs], in_=xha[:, :, cs]), reads=[("xh", c)], writes=["xg_in"])
                for k in range(KD):
                    P.dma("pool", lambda e, k=k: e.collective_compute("AllGather", ALU.bypass, replica_groups=GROUPS, ins=[xg_in[k * 128:(k + 1) * 128, :]], outs=[xg[k * 512:(k + 1) * 512, :]]),
                          reads=["xg_in"], writes=["xg"], inc=1, fresh=True)
                for c in range(NC1):
                    cs = slice(c * CH1, (c + 1) * CH1)
                    n = CH1
                    for k in range(KD):
                        P.op("act", lambda e, k=k: e.activation(out=hn[:, k, 0:n], in_=xha[:, k, cs], func=AF.Copy, scale=gcol(G1 + l, k)),
                             reads=[("xh", c), "gn"], writes=["hn"])
                    for m in range(2):
                        for k in range(KD):
                            P.op("pe", lambda e, m=m, k=k: e.matmul(PS[m][:, 0:n], wqm[:, k, m * 128:(m + 1) * 128], hn[:, k, 0:n], start=(k == 0), stop=(k == KD - 1)),
                                 reads=["hn", "wqm"], writes=[psk(m)])
                        P.op("act", lambda e, m=m: e.activation(out=qm[:, m, 0:n], in_=PS[m][:, 0:n], func=AF.Copy, scale=0.125), reads=[psk(m)], writes=["qm"])
                    for j in range(4):
                        base = (j % 2) * 64
                        jj = j // 2

                        def z_mm(bi, zi, base=base, jj=jj):
                            P.op("pe", lambda e: e.matmul(PS[zi][:, 0:n], mkT[base:base + 64, jj, bi * 128:(bi + 1) * 128], qm[base:base + 64, jj, 0:n], start=True, stop=True),
                                 reads=["mkT", "qm"], writes=[psk(zi)])

                        softmax_av(z_mm, 2, lambda bi: None, lambda bi, j=j: mvaug[:, j, bi, :], "mvaug", base, mo[:, jj, :], "mo", wk, n=n)
                    P.dma("sp", lambda e, cs=cs: e.dma_start(out=memo.rearrange("(k p) n -> p k n", p=128)[:, :, cs], in_=mo[:, :, 0:n]), reads=["mo"], writes=["memo"])
                P.wait_all("sp")
                P.emit()

            with contextlib.ExitStack() as ph:
                t = lambda name, shape, dt: sb("p2_%d_%s" % (l, name), shape, dt, ph)
                xc = [t("xcA", [128, KD, 512], BF16), t("xcB", [128, KD, 512], BF16)]
                wst = t("wst", [128, KD, 384], F32)
                mixc = [t("mix%d" % i, [128, 512], BF16) for i in range(4)]

                def load_own_w(dst, src_ap, ncol, gidx, key, c0=0):
                    P.dma("sp", lambda e: e.dma_start(out=wst[:, :, 0:ncol], in_=kview(src_ap)), writes=["wst"])
                    for k in range(KD):
                        P.op("dve", lambda e, k=k: e.tensor_scalar(out=dst[:, k, c0:c0 + ncol], in0=wst[:, k, 0:ncol], scalar1=gcol(gidx, k), scalar2=None, op0=ALU.mult),
                             reads=["wst", "gn"], writes=[key])

                def load_xc(tc, slot=None):
                    slot = tc % 2 if slot is None else slot % 2
                    rk, cc = divmod(tc * 512, T)
                    buf = xc[slot]
                    P.dma("sp", lambda e: e.dma_start(out=buf[:], in_=xg.rearrange("(k r p) t -> r p k t", k=KD, r=4)[rk][:, :, cc:cc + 512]), reads=["xg"], writes=[("xc", slot)])
                    return buf, ("xc", slot)

                def mix_out(mix_tile, mkey, base, qc):
                    rdst, cc = divmod(qc * 512, T)
                    P.dma("sp", lambda e: e.dma_start(out=mg_in[rdst * 128 + base:rdst * 128 + base + 64, cc:cc + 512], in_=mix_tile[base:base + 64, :]),
                          reads=[mkey], writes=["mg_in"])

                def load_const(name, shape, dt, src):
                    tl = t(name, shape, dt)
                    P.dma("sp", lambda e: e.dma_start(out=tl[:], in_=src), writes=[name])
                    return tl

                if isA:
                    trineg = load_const("trineg", [128, 128], BF16, c_trineg)
                    negones = load_const("negones", [128, 128], BF16, c_negones)
                    msb = load_const("msb", [128, 4 * 512], BF16, c_msb)
                    wown = t("wown", [128, KD, 384], BF16)
                    load_own_w(wown, wa_own[l], 384, G1 + l, "wown")
                    qT2 = t("qT2", [128, S], BF16)
                    kT2 = t("kT2", [128, S], BF16)
                    v2 = t("v2", [128, NKB, 128], BF16)
                    E2 = [t("e%d" % i, [128, 1024], F32) for i in range(3)]
                    SP2 = [t("sp%d" % i, [128, 1024], BF16) for i in range(3)]
                    X2 = [t("x%d" % i, [128, 1024], F32) for i in range(2)]
                    W2 = [t("w%d" % i, [128, 1024], BF16) for i in range(2)]
                    LB2 = [t("lb%d" % i, [128, 1024], BF16) for i in range(2)]
                    nxt = load_xc(0)
                    for tc in range(NQ):
                        buf, bk = nxt
                        if tc + 1 < NQ:
                            nxt = load_xc(tc + 1)
                        ts_ = slice(tc * 512, (tc + 1) * 512)
                        for which, dst, dkey, sc in ((0, qT2, "qT2", 0.125), (1, kT2, "kT2", 1.0)):
                            pi = which
                            for k in range(KD):
                                P.op("pe", lambda e, k=k, which=which, pi=pi: e.matmul(PS[pi][:, :], wown[:, k, which * 128:(which + 1) * 128], buf[:, k, :], start=(k == 0), stop=(k == KD - 1)),
                                     reads=[bk, "wown"], writes=[psk(pi)])
                            P.op("act", lambda e, dst=dst, pi=pi, sc=sc: e.activation(out=dst[:, ts_], in_=PS[pi][:, :], func=AF.Copy, scale=sc), reads=[psk(pi)], writes=[dkey])
                        for tb in range(4):
                            pi = 2 + tb % 2
                            for k in range(KD):
                                P.op("pe", lambda e, k=k, tb=tb, pi=pi: e.matmul(PS[pi][:, 0:128], buf[:, k, tb * 128:(tb + 1) * 128], wown[:, k, 256:384], start=(k == 0), stop=(k == KD - 1)),
                                     reads=[bk, "wown"], writes=[psk(pi)])
                            P.op("dve", lambda e, tb=tb, pi=pi: e.tensor_copy(out=v2[:, tc * 4 + tb, :], in_=PS[pi][:, 0:128]), reads=[psk(pi)], writes=["v2"])
                    ptiles = [(qc, i, 4 * qc + 4) for qc in range(NQ) for i in range(4 * qc + 4)]
                    NPT = len(ptiles)
                    hsl = lambda hh: slice(hh * 512, (hh + 1) * 512)

                    def pA_pe(s_):
                        qc, i, nb = ptiles[s_]
                        kb, zb = nb - 1 - i, (0 if s_ % 2 == 0 else 6)
                        qs = slice(qc * 512, (qc + 1) * 512)
                        for hh in range(2):
                            base = hh * 64
                            P.op("pe", lambda e: e.matmul(PS[zb + hh][:, :], kT2[base:base + 64, kb * 128:(kb + 1) * 128], qT2[base:base + 64, qs], start=True, stop=True),
                                 reads=["kT2", "qT2"], writes=[psk(zb + hh)])

                    def pA(s_):
                        qc, i, nb = ptiles[s_]
                        kb, zb, eb = nb - 1 - i, (0 if s_ % 2 == 0 else 6), s_ % 3
                        P.op("act", lambda e: e.activation(out=E2[eb][:], in_=PSA[:, zb * 512:(zb + 2) * 512], func=AF.Exp), reads=[psk(zb), psk(zb + 1)], writes=[("E2", eb)])
                        P.op("act", lambda e: e.activation(out=SP2[eb][:], in_=E2[eb][:], func=AF.Ln, bias=1.0), reads=[("E2", eb)], writes=[("SP2", eb)])
                        if kb >= 4 * qc:
                            rel = kb - 4 * qc
                            mk = msb[:, rel * 512:(rel + 1) * 512]
                            for hh in range(2):
                                P.op("pool", lambda e: e.tensor_tensor(out=SP2[eb][:, hsl(hh)], in0=SP2[eb][:, hsl(hh)], in1=mk, op=ALU.mult), reads=[("SP2", eb), "msb"], writes=[("SP2", eb)])
                                P.op("pool", lambda e: e.tensor_tensor(out=E2[eb][:, hsl(hh)], in0=E2[eb][:, hsl(hh)], in1=mk, op=ALU.mult), reads=[("E2", eb), "msb"], writes=[("E2", eb)])

                    def pB(s_):
                        qc, i, nb = ptiles[s_]
                        eb = s_ % 3
                        for hh in range(2):
                            P.op("pe", lambda e: e.matmul(PS[2 + hh][:, :], trineg[:], SP2[eb][:, hsl(hh)], start=True, stop=(i == 0)), reads=[("SP2", eb), "trineg"], writes=[psk(2 + hh)])
                            if i > 0:
                                P.op("pe", lambda e: e.matmul(PS[2 + hh][:, :], negones[:], LB2[(i - 1) % 2][:, hsl(hh)], start=False, stop=True),
                                     reads=[("LB2", (i - 1) % 2), "negones"], writes=[psk(2 + hh)])
                        if i < nb - 1:
                            if i == 0:
                                P.op("pool", lambda e: e.tensor_copy(out=LB2[0][:], in_=SP2[eb][:]), reads=[("SP2", eb)], writes=[("LB2", 0)])
                            else:
                                P.op("pool", lambda e: e.tensor_tensor(out=LB2[i % 2][:], in0=LB2[(i - 1) % 2][:], in1=SP2[eb][:], op=ALU.add),
                                     reads=[("SP2", eb), ("LB2", (i - 1) % 2)], writes=[("LB2", i % 2)])
                        P.op("act", lambda e: e.activation(out=X2[s_ % 2][:], in_=PSA[:, 2 * 512:4 * 512], func=AF.Exp), reads=[psk(2), psk(3)], writes=[("X2", s_ % 2)])

                    def pC(s_):
                        qc, i, nb = ptiles[s_]
                        kb, eb = nb - 1 - i, s_ % 3
                        o_ps = 4 + qc % 2
                        P.op("dve", lambda e: e.tensor_tensor(out=W2[s_ % 2][:], in0=E2[eb][:], in1=X2[s_ % 2][:], op=ALU.mult), reads=[("E2", eb), ("X2", s_ % 2)], writes=[("W2", s_ % 2)])
                        for hh in range(2):
                            base = hh * 64
                            P.op("pe", lambda e: e.matmul(PS[o_ps][base:base + 64, :], v2[:, kb, base:base + 64], W2[s_ % 2][:, hsl(hh)], start=(i == 0), stop=(i == nb - 1)),
                                 reads=[("W2", s_ % 2), "v2"], writes=[(psk(o_ps), hh)])
                        if i == nb - 1:
                            mt, mkey = mixc[qc % 4], ("mix", qc % 4)
                            P.op("dve", lambda e: e.tensor_copy(out=mt[:, :], in_=PS[o_ps][:, :]), reads=[(psk(o_ps), 0), (psk(o_ps), 1)], writes=[mkey])
                            for hh in range(2):
                                mix_out(mt, mkey, hh * 64, qc)

                    pA_pe(0)
                    for s_ in range(NPT + 2):
                        if s_ + 1 < NPT:
                            pA_pe(s_ + 1)
                        if s_ < NPT:
                            pA(s_)
                        if 0 <= s_ - 1 < NPT:
                            pB(s_ - 1)
                        if 0 <= s_ - 2 < NPT:
                            pC(s_ - 2)
                else:
                    lb_ = l - NA
                    first_b = (lb_ == 0)
                    mfox = load_const("mfox", [128, 4 * 512], BF16, c_mfox)
                    wq = t("wq", [128, KD, 128], BF16)
                    load_own_w(wq, wb_own[lb_], 128, G1 + l, "wq")
                    if first_b:
                        wkv = t("wkv", [128, KD, 258], BF16)
                        load_own_w(wkv, wkvs_own, 258, GKV, "wkv")
                        bfo = t("bfo", [2, 1], F32)
                        P.dma("sp", lambda e: e.dma_start(out=bfo[:], in_=bf_own), writes=["bfo"])
                        P.op("dve", lambda e: e.tensor_scalar(out=bfo[:], in0=bfo[:], scalar1=-1.0, scalar2=None, op0=ALU.mult), reads=["bfo"], writes=["bfo"])
                        one2 = t("one2", [2, 512], F32)
                        P.op("pool", lambda e: e.memset(one2[:], 1.0), writes=["one2"])
                        fe = t("fe", [2, 512], F32)
                        cc_ = [t("ccA", [2, 512], F32), t("ccB", [2, 512], F32)]
                        r1 = t("r1", [2, 512], F32)
                        spl = t("spl", [2, 6, 512], BF16)
                    qaug = t("qaug", [128, S], BF16)
                    kaug = t("kaug", [128, S], BF16)
                    vaug = t("vaug", [128, NKB, 128], BF16)
                    wk = {"p": [t("pA", [128, 512], BF16), t("pB", [128, 512], BF16)], "osb": t("osb", [128, 512], F32), "rden": t("rden", [128, 512], F32),
                          "zt": t("zt", [128, 512], F32), "zt2": [t("ztA", [128, 512], F32), t("ztB", [128, 512], F32)],
                          "p3": [t("p3_%d" % i, [128, 512], BF16) for i in range(4)]}
                    for hh in range(2):
                        base = hh * 64
                        P.op("pool", lambda e: e.memset(qaug[64:128, :], 1.0), reads=[], writes=["qaug"])
                        if first_b:
                            P.op("pool", lambda e: e.memset(kaug[64:128, :], 1.0), writes=["kaug"])
                            P.op("pool", lambda e: e.memset(vaug[:], 1.0), writes=["vaug"])
                        else:
                            P.dma("sp", lambda e, hh=hh: e.dma_start(out=kaug[:], in_=ksave[hh * 128:(hh + 1) * 128, :]), reads=["ksave"], writes=["kaug"])
                            P.dma("sp", lambda e, hh=hh: e.dma_start(out=vaug[:], in_=vsave[hh * 128:(hh + 1) * 128, :].rearrange("p (b c) -> p b c", c=128)), reads=["vsave"], writes=["vaug"])
                        nxt = load_xc(0, hh * NQ)
                        for tc in range(NQ):
                            buf, bk = nxt
                            if tc + 1 < NQ:
                                nxt = load_xc(tc + 1, tc + 1 + hh * NQ)
                            ts_ = slice(tc * 512, (tc + 1) * 512)
                            for k in range(KD):
                                P.op("pe", lambda e, k=k: e.matmul(PS[0][0:64, :], wq[:, k, base:base + 64], buf[:, k, :], start=(k == 0), stop=(k == KD - 1)),
                                     reads=[bk, "wq"], writes=[psk(0)])
                            P.op("act", lambda e: e.activation(out=qaug[0:64, ts_], in_=PS[0][0:64, :], func=AF.Copy, scale=0.125), reads=[psk(0)], writes=["qaug"])
                            if first_b:
                                for k in range(KD):
                                    P.op("pe", lambda e, k=k: e.matmul(PS[1][0:64, :], wkv[:, k, base:base + 64], buf[:, k, :], start=(k == 0), stop=(k == KD - 1)),
                                         reads=[bk, "wkv"], writes=[psk(1)])
                                P.op("act", lambda e: e.activation(out=kaug[0:64, ts_], in_=PS[1][0:64, :], func=AF.Copy), reads=[psk(1)], writes=["kaug"])
                                for tb in range(4):
                                    pi = 2 + tb % 2
                                    for k in range(KD):
                                        P.op("pe", lambda e, k=k, tb=tb, pi=pi: e.matmul(PS[pi][:, 0:64], buf[:, k, tb * 128:(tb + 1) * 128], wkv[:, k, 128 + base:128 + base + 64], start=(k == 0), stop=(k == KD - 1)),
                                             reads=[bk, "wkv"], writes=[psk(pi)])
                                    P.op("dve", lambda e, tb=tb, pi=pi: e.tensor_copy(out=vaug[:, tc * 4 + tb, base:base + 64], in_=PS[pi][:, 0:64]), reads=[psk(pi)], writes=["vaug"])
                                for k in range(KD):
                                    P.op("pe", lambda e, k=k: e.matmul(PS[4][0:2, :], wkv[:, k, 256:258], buf[:, k, :], start=(k == 0), stop=(k == KD - 1)),
                                         reads=[bk, "wkv"], writes=[psk(4)])
                                P.op("act", lambda e: e.activation(out=fe[:], in_=PS[4][0:2, :], func=AF.Exp, scale=-1.0, bias=bfo[:, 0:1]), reads=[psk(4), "bfo"], writes=["fe"])
                                P.op("act", lambda e: e.activation(out=fe[:], in_=fe[:], func=AF.Ln, bias=1.0), reads=["fe"], writes=["fe"])
                                P.op("dve", lambda e: e.tensor_scalar(out=fe[:], in0=fe[:], scalar1=-1.0, scalar2=None, op0=ALU.mult), reads=["fe"], writes=["fe"])
                                cur, prev = cc_[tc % 2], cc_[(tc + 1) % 2]
                                init = 0.0 if tc == 0 else prev[:, 511:512]
                                P.op("dve", lambda e, cur=cur, init=init: e.tensor_tensor_scan(out=cur[:], data0=one2[:], data1=fe[:], initial=init, op0=ALU.mult, op1=ALU.add),
                                     reads=["fe", "one2", ("cc", (tc + 1) % 2)], writes=[("cc", tc % 2)])
                                P.op("dve", lambda e, cur=cur: e.tensor_copy(out=spl[:, 0, :], in_=cur[:]), reads=[("cc", tc % 2)], writes=["spl"])
                                P.op("dve", lambda e, cur=cur: e.tensor_tensor(out=r1[:], in0=cur[:], in1=spl[:, 0, :], op=ALU.subtract), reads=[("cc", tc % 2), "spl"], writes=["r1"])
                                P.op("dve", lambda e: e.tensor_copy(out=spl[:, 1, :], in_=r1[:]), reads=["r1"], writes=["spl"])
                                P.op("dve", lambda e: e.tensor_tensor(out=r1[:], in0=r1[:], in1=spl[:, 1, :], op=ALU.subtract), reads=["r1", "spl"], writes=["r1"])
                                P.op("dve", lambda e: e.tensor_copy(out=spl[:, 2, :], in_=r1[:]), reads=["r1"], writes=["spl"])
                                P.op("dve", lambda e: e.tensor_scalar(out=spl[:, 3:6, :], in0=spl[:, 0:3, :], scalar1=-1.0, scalar2=None, op0=ALU.mult), reads=["spl"], writes=["spl"])
                                for j3 in range(3):
                                    P.dma("sp", lambda e, hh=hh, j3=j3: e.dma_start(out=qaug[64 + j3:65 + j3, ts_], in_=spl[hh:hh + 1, j3, :]), reads=["spl"], writes=["qaug"])
                                    P.dma("sp", lambda e, hh=hh, j3=j3: e.dma_start(out=kaug[67 + j3:68 + j3, ts_], in_=spl[hh:hh + 1, 3 + j3, :]), reads=["spl"], writes=["kaug"])
                                    P.dma("sp", lambda e, hh=hh, j3=j3: e.dma_start(out=kaug[70 + j3:71 + j3, ts_], in_=spl[hh:hh + 1, j3, :]), reads=["spl"], writes=["kaug"])
                        if first_b and NB > 1:
                            P.dma("sp", lambda e, hh=hh: e.dma_start(out=ksave[hh * 128:(hh + 1) * 128, :], in_=kaug[:]), reads=["kaug"], writes=["ksave"])
                            P.dma("sp", lambda e, hh=hh: e.dma_start(out=vsave[hh * 128:(hh + 1) * 128, :].rearrange("p (b c) -> p b c", c=128), in_=vaug[:]), reads=["vaug"], writes=["vsave"])
                        if not first_b:
                            P.dma("sp", lambda e: e.dma_start(out=qaug[64:67, :], in_=kaug[70:73, :]), reads=["kaug"], writes=["qaug"])
                        ftiles = [(qc, bi, 4 * qc + 4) for qc in range(NQ) for bi in range(4 * qc + 4)]
                        NFT = len(ftiles)
                        PB = wk["p3"]

                        def fA_pe(g):
                            qc, bi, nb = ftiles[g]
                            zi = g % 3
                            qs = slice(qc * 512, (qc + 1) * 512)
                            P.op("pe", lambda e: e.matmul(PS[zi][:, :], kaug[0:70, bi * 128:(bi + 1) * 128], qaug[0:70, qs], start=True, stop=True),
                                 reads=["kaug", "qaug"], writes=[psk(zi)])

                        def fA(g, base=base):
                            qc, bi, nb = ftiles[g]
                            zi, pb = g % 3, g % 4
                            if bi >= 4 * qc:
                                rel = bi - 4 * qc
                                m = mfox[:, rel * 512:(rel + 1) * 512]
                                zt = wk["zt2"][g % 2]
                                P.op("dve", lambda e: e.tensor_tensor(out=zt[:], in0=PS[zi][:, :], in1=m, op=ALU.add), reads=[psk(zi), "mfox"], writes=[("zt", g % 2)])
                                P.op("act", lambda e: e.activation(out=PB[pb][:], in_=zt[:], func=AF.Exp), reads=[("zt", g % 2)], writes=[("p3", pb)])
                            else:
                                P.op("act", lambda e: e.activation(out=PB[pb][:], in_=PS[zi][:, :], func=AF.Exp), reads=[psk(zi)], writes=[("p3", pb)])

                        def fB(g, base=base):
                            qc, bi, nb = ftiles[g]
                            o_ps = 4 + qc % 2
                            P.op("pe", lambda e: e.matmul(PS[o_ps][:, :], vaug[:, bi, :], PB[g % 4][:], start=(bi == 0), stop=(bi == nb - 1)),
                                 reads=[("p3", g % 4), "vaug"], writes=[psk(o_ps)])

                        def fEpi(g, base=base):
                            qc, bi, nb = ftiles[g]
                            o_ps = 4 + qc % 2
                            osb, rd = wk["osb"], wk["rden"]
                            mt, mkey = mixc[qc % 2], ("mix", qc % 2)
                            P.op("act", lambda e: e.activation(out=osb[:], in_=PS[o_ps][:, :], func=AF.Copy), reads=[psk(o_ps)], writes=["osb"])
                            P.op("pe", lambda e: e.matmul(PS[7][:, :], swp[:], osb[:], start=True, stop=True), reads=["osb", "swp"], writes=[psk(7)])
                            P.op("dve", lambda e: e.reciprocal(out=rd[base:base + 64, :], in_=PS[7][base:base + 64, :]), reads=[psk(7)], writes=["rden"])
                            P.op("dve", lambda e: e.tensor_tensor(out=mt[base:base + 64, :], in0=osb[base:base + 64, :], in1=rd[base:base + 64, :], op=ALU.mult),
                                 reads=["osb", "rden"], writes=[mkey])
                            mix_out(mt, mkey, base, qc)

                        fA_pe(0)
                        for s_ in range(NFT + 4):
                            if s_ + 1 < NFT:
                                fA_pe(s_ + 1)
                            if s_ < NFT:
                                fA(s_)
                            if 0 <= s_ - 2 < NFT:
                                fB(s_ - 2)
                            g2 = s_ - 4
                            if 0 <= g2 < NFT and ftiles[g2][1] == ftiles[g2][2] - 1:
                                fEpi(g2)
                P.wait_all("sp")
                P.emit()

            for r_ in range(4):
                P.dma("pool", lambda e, r_=r_: e.collective_compute("AllGather", ALU.bypass, replica_groups=GROUPS, ins=[mg_in[r_ * 128:(r_ + 1) * 128, :]], outs=[mg[r_ * 512:(r_ + 1) * 512, :]]),
                      reads=["mg_in"], writes=["mg"], inc=1, fresh=True)

            with contextlib.ExitStack() as ph:
                t = lambda name, shape, dt: sb("p3_%d_%s" % (l, name), shape, dt, ph)
                wo = t("wo", [128, 6, D], BF16)
                W1 = t("W1", [128, KD, DFF], BF16)
                W2 = t("W2", [128, KF, D], BF16)
                n = CH3
                hc = t("hc", [128, KD, n], F32)
                stg = [t("stg%d" % i, [128, 512], F32)[:, :] for i in range(3)] + ([hc[:, i, :] for i in range(KD)] if n == 512 else [])
                NSTG = len(stg)
                load_w(stg, wo, w_o[l], 6, D, "wo")
                load_w(stg, W1, w1[l], KD, DFF, "W1")
                load_w(stg, W2, w2[l], KF, D, "W2")
                mx = t("mx", [128, KD, n], BF16)
                h1 = t("h1", [128, max(KF, KD), n], BF16)
                rl = [t("rlA", [128, n], F32), t("rlB", [128, n], F32)]
                rstd, RK = rl[0], ("rl", 0)
                P.dma("sp", lambda e: e.dma_start(out=mgl, in_=Late(lambda: mg[bass.ds(dyn["off"], 512), :])), reads=["mg"], writes=["mgl"])
                for c in range(NC3):
                    cs = slice(c * n, (c + 1) * n)
                    P.dma("sp", lambda e: e.dma_start(out=mx[:, 0:4, :], in_=kview(mgl)[:, :, cs]), reads=["mgl"], writes=["mx"])
                    P.dma("sp", lambda e: e.dma_start(out=mx[:, 4:6, :], in_=memo.rearrange("(k p) n -> p k n", p=128)[:, :, cs]), reads=["memo"], writes=["mx"])
                    P.dma("sp", lambda e: e.dma_start(out=hc[:], in_=kview(h_src)[:, :, cs]), reads=["hT"], writes=["hc"] + ([("stg", i) for i in range(3, NSTG)] if c == 0 else []))
                    for m in range(KD):
                        pi = m % 2
                        for k in range(6):
                            P.op("pe", lambda e: e.matmul(PS[pi][:, 0:n], wo[:, k, m * 128:(m + 1) * 128], mx[:, k, :], start=(k == 0), stop=(k == 5)),
                                 reads=["mx", "wo"], writes=[psk(pi)])
                        P.op("dve", lambda e: e.tensor_tensor(out=hc[:, m, :], in0=hc[:, m, :], in1=PS[pi][:, 0:n], op=ALU.add), reads=[psk(pi), "hc"], writes=["hc"])
                    rms_rstd(hc, n, "hc", h1, rstd, 2, sqkey="h1", rkey=RK)
                    for k in range(KD):
                        P.op("dve", lambda e: e.scalar_tensor_tensor(out=mx[:, k, :], in0=hc[:, k, :], scalar=gcol(G2 + l, k), in1=rstd[:, 0:n], op0=ALU.mult, op1=ALU.mult),
                             reads=["hc", RK, "gn"], writes=["mx"])
                    for m in range(KF):
                        pi = 3 + m % 2
                        for k in range(KD):
                            P.op("pe", lambda e: e.matmul(PS[pi][:, 0:n], W1[:, k, m * 128:(m + 1) * 128], mx[:, k, :], start=(k == 0), stop=(k == KD - 1)),
                                 reads=["mx", "W1"], writes=[psk(pi)])
                        r = rl[m % 2]
                        P.op("act", lambda e: e.activation(out=r[:], in_=PS[pi][:, 0:n], func=AF.Relu), reads=[psk(pi)], writes=[("rl", m % 2)])
                        P.op("pool" if m % 2 else "dve", lambda e: e.tensor_tensor(out=h1[:, m, :], in0=r[:], in1=r[:], op=ALU.mult), reads=[("rl", m % 2)], writes=["h1"])
                    last = (l == L - 1)
                    for m in range(KD):
                        pi = 5 + m % 2
                        for k in range(KF):
                            P.op("pe", lambda e: e.matmul(PS[pi][:, 0:n], W2[:, k, m * 128:(m + 1) * 128], h1[:, k, :], start=(k == 0), stop=(k == KF - 1)),
                                 reads=["h1", "W2"], writes=[psk(pi)])
                        P.op("dve", lambda e: e.tensor_tensor(out=hc[:, m, :], in0=hc[:, m, :], in1=PS[pi][:, 0:n], op=ALU.add), reads=[psk(pi), "hc"], writes=["hc"])
                    if not last:
                        P.dma("sp", lambda e: e.dma_start(out=kview(hT)[:, :, cs], in_=hc[:]), reads=["hc"], writes=["hT"])
                    else:
                        rms_rstd(hc, n, "hc", h1, rstd, 2, sqkey="h1", rkey=RK)
                        for k in range(KD):
                            P.op("dve", lambda e: e.scalar_tensor_tensor(out=hc[:, k, :], in0=hc[:, k, :], scalar=gcol(GFIN, k), in1=rstd[:, 0:n], op0=ALU.mult, op1=ALU.mult),
                                 reads=["hc", RK, "gn"], writes=["hc"])
                        P.dma("sp", lambda e: e.dma_start(out=kview(yT)[:, :, cs], in_=hc[:]), reads=["hc"], writes=["yT"])
                P.wait_all("sp")
                P.pre["sp"] = sp_pre
                P.emit()
    return nc


def _consts():
    bf = ml_dtypes.bfloat16
    j = np.arange(128)[:, None]
    s = np.arange(128)[None, :]
    trineg = np.where(j >= s, -1.0, 0.0).astype(bf)
    negones = np.full((128, 128), -1.0).astype(bf)
    ones = np.ones((128, 128)).astype(bf)
    p = np.arange(128)[:, None]
    tq = np.arange(512)[None, :]
    msb = np.concatenate([(r * 128 + p < tq) for r in range(4)], axis=1).astype(np.float32).astype(bf)
    mfox = (np.concatenate([(r * 128 + p <= tq) for r in range(4)], axis=1).astype(np.float32) - 1.0) * 30000.0
    mfox = mfox.astype(bf)
    swap = np.zeros((128, 128), np.float32)
    swap[(np.arange(128) + 64) % 128, np.arange(128)] = 1.0
    return dict(c_trineg=trineg, c_negones=negones, c_ones=ones, c_msb=msb, c_mfox=mfox, c_swap=swap)


def make_in_maps(inp, S, DFF, NA, NB):
    L = NA + NB
    T = S // 4
    f32 = np.float32
    G = np.concatenate([inp["norm1_g"], inp["mem_norm_g"], inp["norm2_g"], inp["kv_norm_g"][None], inp["final_norm_g"][None]], 0).astype(f32)
    NG = G.shape[0]
    gains = np.ascontiguousarray(G.reshape(NG, 8, 128).transpose(2, 0, 1).reshape(128, NG * 8))
    consts = _consts()
    w_in_a, w_in_b, wkv = inp["w_in_a"], inp["w_in_b"], inp["w_kv_shared"]
    qm_cols = lambda w, off: w[:, :, off:off + 256]
    w_qm = np.ascontiguousarray(np.concatenate([qm_cols(w_in_a, 1536), qm_cols(w_in_b, 512)], 0)) if NB > 0 else np.ascontiguousarray(qm_cols(w_in_a, 1536))
    maps = []
    for c in range(8):
        b, r = divmod(c, 4)
        hs = slice(r * 128, (r + 1) * 128)
        wa = np.concatenate([w_in_a[:, :, r * 128:(r + 1) * 128], w_in_a[:, :, 512 + r * 128:512 + (r + 1) * 128], w_in_a[:, :, 1024 + r * 128:1024 + (r + 1) * 128]], axis=2)
        wb = w_in_b[:, :, hs] if NB > 0 else np.zeros((1, D, 128), f32)
        wk = np.concatenate([wkv[:, hs], wkv[:, 512 + r * 128:512 + (r + 1) * 128], wkv[:, 1024 + 2 * r:1024 + 2 * r + 2]], axis=1)
        m = dict(
            xT=np.ascontiguousarray(inp["x"][b, r * T:(r + 1) * T, :].T),
            memT=np.ascontiguousarray(inp["mem"][b].T),
            gains=gains, w_qm=w_qm, w_mkv=inp["w_mem_kv"], w_o=inp["w_o"], w1=inp["w_mlp1"], w2=inp["w_mlp2"],
            wa_own=np.ascontiguousarray(wa), wb_own=np.ascontiguousarray(wb), wkvs_own=np.ascontiguousarray(wk),
            bf_own=np.ascontiguousarray(inp["b_f"][2 * r:2 * r + 2].reshape(2, 1)),
            roff=np.array([[r * 512]], np.int32),
        )
        m.update(consts)
        maps.append(m)
    return maps


def assemble(results, S):
    T = S // 4
    out = np.empty((2, S, D), np.float32)
    for c in range(8):
        b, r = divmod(c, 4)
        out[b, r * T:(r + 1) * T, :] = results[c]["yT"].T
    return out


def kernel(**inputs):
    inp = {k: np.asarray(v) for k, v in inputs.items()}
    S, DFF, NA, NB = 16384, 4096, 2, 2
    nc = build_program(S, DFF, NA, NB)
    maps = make_in_maps(inp, S, DFF, NA, NB)
    res = run_bass_kernel_spmd(nc, maps, core_ids=list(range(8)))
    return assemble(res.results, S)
```
